# Optimizing a Trainium2 kernel written in Bass

```python
import functools
import jax, jax.numpy as jnp
from jax import lax
import numpy as np

D_MODEL = 2048
BATCH = 32
SEQ = 256
DEPTH = 2
DEC_BATCH = 2
DEC_SEQ = 4096
PAST_LEN = 256

GRID_W = 64
N_ATTN_LAYERS = (DEPTH + 1) // 2
N_CMLP_LAYERS = DEPTH // 2
D_CONV = D_MODEL // 2
CONV_WIDTH = 31
CONV_PAD = CONV_WIDTH // 2
MLA_HEADS = D_MODEL // 256
Q_RANK = D_MODEL // 4
KV_RANK = D_MODEL // 8
NOPE_DIM = 128
ROPE_DIM = 64
V_DIM = 128
QK_DIM = NOPE_DIM + ROPE_DIM
ROPE_THETA = 10000.0
Q_BLOCK = 128
D_CMLP = D_MODEL
CHUNK = 128
CMLP_GROUPS = 8
D_FF = 256 * (-(-8 * D_MODEL // (3 * 256)))
EVEN_IN = 2 * D_CONV + Q_RANK + KV_RANK + ROPE_DIM
EVEN_OUT = D_CONV + MLA_HEADS * V_DIM
ATTN_SCALE = QK_DIM ** -0.5
EPS = 1e-6

kernel_name = 'hybrid_flow_conv_mla_gmlp_step'


def rmsnorm(x, g):
    xf = x.astype(jnp.float32)
    y = xf * lax.rsqrt(jnp.mean(xf * xf, axis=-1, keepdims=True) + EPS)
    return (y * g.astype(jnp.float32)).astype(x.dtype)


def layernorm(x, g, b):
    xf = x.astype(jnp.float32)
    mu = jnp.mean(xf, axis=-1, keepdims=True)
    var = jnp.mean(jnp.square(xf - mu), axis=-1, keepdims=True)
    y = (xf - mu) * lax.rsqrt(var + EPS)
    return (y * g.astype(jnp.float32) + b.astype(jnp.float32)).astype(x.dtype)


def adaln(cond, w, b):
    m = jax.nn.silu(cond) @ w + b
    return jnp.split(m[:, None, :], 6, axis=-1)


def swiglu(h, wg, wu, wd):
    return (jax.nn.silu(h @ wg) * (h @ wu)) @ wd


def axial_rope_tables(n_tokens, dtype):
    n_rows = n_tokens // GRID_W
    rows = jnp.repeat(jnp.arange(n_rows), GRID_W)
    cols = jnp.tile(jnp.arange(GRID_W), n_rows)
    pos = jnp.stack([rows, cols], axis=-1).astype(jnp.float32)
    n_freq = ROPE_DIM // 4
    inv_freq = jnp.power(ROPE_THETA, -jnp.arange(n_freq, dtype=jnp.float32) * 2.0 / (ROPE_DIM // 2))
    ang = pos[:, :, None] * inv_freq
    return jnp.cos(ang)[:, None].astype(dtype), jnp.sin(ang)[:, None].astype(dtype)


def apply_axial_rope(x, cos, sin):
    xr = x.reshape(x.shape[:-1] + (2, 2, ROPE_DIM // 4))
    x1, x2 = xr[..., 0, :], xr[..., 1, :]
    out = jnp.stack([x1 * cos - x2 * sin, x1 * sin + x2 * cos], axis=-2)
    return out.reshape(x.shape)


def block_attention(q, k, v):
    b, t, h, d = q.shape
    qb = q.reshape(b, t // Q_BLOCK, Q_BLOCK, h, d).transpose(1, 0, 2, 3, 4)

    def one_block(qi):
        s = jnp.einsum('bqhd,bkhd->bhqk', qi, k).astype(jnp.float32) * ATTN_SCALE
        p = jax.nn.softmax(s, axis=-1).astype(v.dtype)
        return jnp.einsum('bhqk,bkhd->bqhd', p, v)

    o = lax.map(one_block, qb)
    return o.transpose(1, 0, 2, 3, 4).reshape(b, t, h, v.shape[-1])


def conformer_conv(a_in, conv_w, conv_b, ln_g, ln_b):
    a = a_in[..., :D_CONV] * jax.nn.sigmoid(a_in[..., D_CONV:])
    a = lax.conv_general_dilated(a, conv_w, window_strides=(1,), padding=[(CONV_PAD, CONV_PAD)],
                                 dimension_numbers=('NWC', 'WIO', 'NWC'),
                                 feature_group_count=D_CONV) + conv_b
    return jax.nn.silu(layernorm(a, ln_g, ln_b))


def mla(q_c, kv_c, k_pe, g_qa, w_qb, g_kva, w_kvb, rope, ctx):
    b, t, _ = q_c.shape
    q = (rmsnorm(q_c, g_qa) @ w_qb).reshape(b, t, MLA_HEADS, QK_DIM)
    q_nope, q_pe = q[..., :NOPE_DIM], q[..., NOPE_DIM:]
    ckv = rmsnorm(kv_c, g_kva)
    k_pe = k_pe[:, :, None, :]
    if rope is not None:
        cos, sin = rope
        q_pe = apply_axial_rope(q_pe, cos, sin)
        k_pe = apply_axial_rope(k_pe, cos, sin)
    ckv_all, kpe_all = ckv, k_pe
    if ctx is not None:
        ctx_ckv, ctx_kpe = ctx
        ckv_all = jnp.concatenate([ctx_ckv, ckv], axis=1)
        kpe_all = jnp.concatenate([ctx_kpe[:, :, None, :], k_pe], axis=1)
    tk = ckv_all.shape[1]
    kv = (ckv_all @ w_kvb).reshape(b, tk, MLA_HEADS, NOPE_DIM + V_DIM)
    k = jnp.concatenate([kv[..., :NOPE_DIM],
                         jnp.broadcast_to(kpe_all, (b, tk, MLA_HEADS, ROPE_DIM))], axis=-1)
    v = kv[..., NOPE_DIM:]
    o = block_attention(jnp.concatenate([q_nope, q_pe], axis=-1), k, v)
    return o.reshape(b, t, MLA_HEADS * V_DIM), ckv, k_pe[:, :, 0, :]


def even_mixer(h, p, rope, ctx):
    w_in, conv_w, conv_b, conv_ln_g, conv_ln_b, g_qa, w_qb, g_kva, w_kvb, w_o = p
    z = h @ w_in
    o1 = 2 * D_CONV
    o2 = o1 + Q_RANK
    o3 = o2 + KV_RANK
    a = conformer_conv(z[..., :o1], conv_w, conv_b, conv_ln_g, conv_ln_b)
    att, ckv, kpe = mla(z[..., o1:o2], z[..., o2:o3], z[..., o3:], g_qa, w_qb, g_kva, w_kvb, rope, ctx)
    return jnp.concatenate([a, att], axis=-1) @ w_o, (ckv, kpe)


def cmlp_mixer(h, p):
    w_in, ln_g, ln_b, w_s, b_s, w_o = p
    b, t, _ = h.shape
    z = jax.nn.gelu(h @ w_in)
    u, v = z[..., :D_CMLP], z[..., D_CMLP:]
    v = layernorm(v, ln_g, ln_b).reshape(b, t // CHUNK, CHUNK, CMLP_GROUPS, D_CMLP // CMLP_GROUPS)
    v = jnp.einsum('gpq,bnqgc->bnpgc', w_s, v) + b_s.T[:, :, None]
    return (u * v.reshape(b, t, D_CMLP)) @ w_o, None


def residual_block(x, mods, shared, mixer):
    sh_m, sc_m, gt_m, sh_f, sc_f, gt_f = mods
    g_mix, g_ffn, wg, wu, wd = shared
    out, aux = mixer(rmsnorm(x, g_mix) * (1 + sc_m) + sh_m)
    x = x + gt_m * out
    x = x + gt_f * swiglu(rmsnorm(x, g_ffn) * (1 + sc_f) + sh_f, wg, wu, wd)
    return x, aux


def setup_inputs(seed: int = 0) -> dict:
    key = jax.random.key(seed)
    ks = iter(jax.random.split(key, 40))

    def nrm(shape, scale):
        return jax.random.normal(next(ks), shape, jnp.float32) * scale

    def gain(shape):
        return 1.0 + nrm(shape, 0.02)

    NE, NO = N_ATTN_LAYERS, N_CMLP_LAYERS
    return {
        'x_prompt': nrm((BATCH, SEQ, D_MODEL), 1.0),
        'x_sample': nrm((DEC_BATCH, DEC_SEQ, D_MODEL), 1.0),
        'cache_ckv': nrm((DEC_BATCH, NE, PAST_LEN, KV_RANK), 1.0),
        'cache_kpe': nrm((DEC_BATCH, NE, PAST_LEN, ROPE_DIM), 1.0),
        'c': nrm((DEC_BATCH, D_MODEL), 1.0),
        'c_ctx': nrm((D_MODEL,), 1.0),
        'mod_w': nrm((DEPTH, D_MODEL, 6 * D_MODEL), 0.5 * D_MODEL ** -0.5),
        'mod_b': nrm((DEPTH, 6 * D_MODEL), 0.02),
        'norm_mix_g': gain((DEPTH, D_MODEL)),
        'norm_ffn_g': gain((DEPTH, D_MODEL)),
        'ffn_w_gate': nrm((DEPTH, D_MODEL, D_FF), D_MODEL ** -0.5),
        'ffn_w_up': nrm((DEPTH, D_MODEL, D_FF), D_MODEL ** -0.5),
        'ffn_w_down': nrm((DEPTH, D_FF, D_MODEL), D_FF ** -0.5),
        'ev_w_in': nrm((NE, D_MODEL, EVEN_IN), D_MODEL ** -0.5),
        'ev_conv_w': nrm((NE, CONV_WIDTH, 1, D_CONV), CONV_WIDTH ** -0.5),
        'ev_conv_b': nrm((NE, D_CONV), 0.02),
        'ev_conv_ln_g': gain((NE, D_CONV)),
        'ev_conv_ln_b': nrm((NE, D_CONV), 0.02),
        'ev_q_norm_g': gain((NE, Q_RANK)),
        'ev_w_qb': nrm((NE, Q_RANK, MLA_HEADS * QK_DIM), Q_RANK ** -0.5),
        'ev_kv_norm_g': gain((NE, KV_RANK)),
        'ev_w_kvb': nrm((NE, KV_RANK, MLA_HEADS * (NOPE_DIM + V_DIM)), KV_RANK ** -0.5),
        'ev_w_o': nrm((NE, EVEN_OUT, D_MODEL), EVEN_OUT ** -0.5),
        'od_w_in': nrm((NO, D_MODEL, 2 * D_CMLP), D_MODEL ** -0.5),
        'od_ln_g': gain((NO, D_CMLP)),
        'od_ln_b': nrm((NO, D_CMLP), 0.02),
        'od_w_s': nrm((NO, CMLP_GROUPS, CHUNK, CHUNK), CHUNK ** -0.5),
        'od_b_s': gain((NO, CMLP_GROUPS, CHUNK)),
        'od_w_o': nrm((NO, D_CMLP, D_MODEL), D_CMLP ** -0.5),
        'final_norm_g': gain((D_MODEL,)),
    }


def reference(x_prompt, x_sample, cache_ckv, cache_kpe, c, c_ctx,
              mod_w, mod_b, norm_mix_g, norm_ffn_g, ffn_w_gate, ffn_w_up, ffn_w_down,
              ev_w_in, ev_conv_w, ev_conv_b, ev_conv_ln_g, ev_conv_ln_b,
              ev_q_norm_g, ev_w_qb, ev_kv_norm_g, ev_w_kvb, ev_w_o,
              od_w_in, od_ln_g, od_ln_b, od_w_s, od_b_s, od_w_o, final_norm_g):
    rope = axial_rope_tables(x_sample.shape[1], x_sample.dtype)
    xp, xs = x_prompt, x_sample
    new_ckv, new_kpe = [], []
    for l in range(DEPTH):
        mods_p = adaln(c_ctx[None, :], mod_w[l], mod_b[l])
        mods_s = adaln(c, mod_w[l], mod_b[l])
        shared = (norm_mix_g[l], norm_ffn_g[l], ffn_w_gate[l], ffn_w_up[l], ffn_w_down[l])
        if l % 2 == 0:
            e = l // 2
            ev = (ev_w_in[e], ev_conv_w[e], ev_conv_b[e], ev_conv_ln_g[e], ev_conv_ln_b[e],
                  ev_q_norm_g[e], ev_w_qb[e], ev_kv_norm_g[e], ev_w_kvb[e], ev_w_o[e])
            xp, ctx_kv = residual_block(xp, mods_p, shared,
                                        functools.partial(even_mixer, p=ev, rope=None, ctx=None))
            new_ckv.append(ctx_kv[0])
            new_kpe.append(ctx_kv[1])
            xs, _ = residual_block(xs, mods_s, shared,
                                   functools.partial(even_mixer, p=ev, rope=rope,
                                                     ctx=(cache_ckv[:, e], cache_kpe[:, e])))
        else:
            o = l // 2
            od = (od_w_in[o], od_ln_g[o], od_ln_b[o], od_w_s[o], od_b_s[o], od_w_o[o])
            xp, _ = residual_block(xp, mods_p, shared, functools.partial(cmlp_mixer, p=od))
            xs, _ = residual_block(xs, mods_s, shared, functools.partial(cmlp_mixer, p=od))
    y_prompt = rmsnorm(xp, final_norm_g)
    y_sample = rmsnorm(xs, final_norm_g)
    state_ckv = jnp.stack(new_ckv, axis=1)
    state_kpe = jnp.stack(new_kpe, axis=1)
    return (y_prompt, y_sample, state_ckv, state_kpe)
```

```python
import numpy as np
import concourse.bass as bass
import concourse.mybir as mybir
from concourse.bass_utils import run_bass_kernel_spmd

F32 = mybir.dt.float32
BF16 = mybir.dt.bfloat16
AF = mybir.ActivationFunctionType
ALU = mybir.AluOpType

D = 2048
NC16 = 16
T = 1024
DFF = 5632
EPS = 1e-6
SEQ_P = 256
NKS = 4352
SCALE = 192 ** -0.5
WSLOT = 8704
ARENA = 79104

PCOL = {}
_off = 0
for _n, _w in (("nmg0", 16), ("nmg1", 16), ("nfg0", 16), ("nfg1", 16), ("fng", 16),
               ("modb0", 96), ("modb1", 96), ("convb", 8), ("clng", 8), ("clnb", 8),
               ("qng", 4), ("kvng", 2), ("olng", 16), ("olnb", 16), ("convw", 248)):
    PCOL[_n] = _off
    _off += _w
NPAR = _off


class Tile:
    __slots__ = ("name", "w", "r", "excl")

    def __init__(self, name, w=None):
        self.name = name
        self.excl = False
        self.w = list(w) if w else []
        self.r = []


class Rec:
    def __init__(self):
        self.calls = []

    def __getattr__(self, name):
        def f(*a, **k):
            self.calls.append((name, a, k))
            return self
        return f


class Eng:
    def __init__(self, name):
        self.name = name
        self.items = []
        self.count = 0
        self.waited = {}


class Prog:
    ENG = ("pe", "act", "dve", "pool", "sp")

    def __init__(self, nc):
        self.nc = nc
        self.eng = {n: Eng(n) for n in self.ENG}
        self.sems = {}
        self.dma_val = {}
        self.out_tokens = []

    def sem(self, key):
        if key not in self.sems:
            self.sems[key] = self.nc.alloc_semaphore(name="s_" + key)
        return self.sems[key]

    def _deps(self, e, reads, writes):
        deps = {}
        own = "e_" + e.name
        for t in reads:
            for (k, v) in t.w:
                if deps.get(k, 0) < v:
                    deps[k] = v
            if t.excl:
                for (k, v) in t.r:
                    if k != own and deps.get(k, 0) < v:
                        deps[k] = v
        for t in writes:
            for (k, v) in t.w:
                if deps.get(k, 0) < v:
                    deps[k] = v
            for (k, v) in t.r:
                if deps.get(k, 0) < v:
                    deps[k] = v
        waits = []
        for k, v in deps.items():
            if k == "e_pe" and e.name == "pe":
                continue
            if e.waited.get(k, 0) >= v:
                continue
            e.waited[k] = v
            waits.append((k, v))
        return waits

    def _commit(self, tok, reads, writes):
        for t in reads:
            t.r = [x for x in t.r if x[0] != tok[0]] + [tok]
        for t in writes:
            t.w = [tok]
            t.r = []

    def op(self, en, fn, reads=(), writes=()):
        e = self.eng[en]
        waits = self._deps(e, reads, writes)
        e.count += 1
        tok = ("e_" + en, e.count)
        rec = Rec()
        fn(rec)
        assert rec.calls
        e.items.append(("op", waits, rec.calls))
        self._commit(tok, reads, writes)
        return tok

    def dma(self, qn, out_ap, in_ap, skey, reads=(), writes=(), is_output=False):
        e = self.eng[qn]
        waits = self._deps(e, reads, writes)
        k = "d_" + skey
        self.dma_val[k] = self.dma_val.get(k, 0) + 16
        tok = (k, self.dma_val[k])
        e.items.append(("dma", waits, (out_ap, in_ap, k)))
        self._commit(tok, reads, writes)
        if is_output:
            self.out_tokens.append(tok)
        return tok

    def dma_multi(self, qn, pairs, skey, reads=(), writes=()):
        e = self.eng[qn]
        waits = self._deps(e, reads, writes)
        k = "d_" + skey
        for i, (out_ap, in_ap) in enumerate(pairs):
            self.dma_val[k] = self.dma_val.get(k, 0) + 16
            e.items.append(("dma", waits if i == 0 else [], (out_ap, in_ap, k)))
        tok = (k, self.dma_val[k])
        self._commit(tok, reads, writes)
        return tok

    def fence_tokens(self):
        toks = [("e_" + n, self.eng[n].count) for n in ("pe", "act", "dve", "pool") if self.eng[n].count]
        for k, v in self.dma_val.items():
            if not k.startswith("d_w"):
                toks.append((k, v))
        return toks

    def emit(self):
        nc = self.nc
        fin = {}
        for (k, v) in self.out_tokens:
            fin[k] = max(fin.get(k, 0), v)
        sp = self.eng["sp"]
        sp.items.append(("wait", [(k, v) for k, v in fin.items()], None))
        for en in self.ENG:
            self.sem("e_" + en)
        for k in self.dma_val:
            self.sem(k)

        def run(en, h):
            e = self.eng[en]
            own = self.sems["e_" + en]
            for kind, waits, payload in e.items:
                for (k, v) in waits:
                    h.wait_ge(self.sems[k], v)
                if kind == "op":
                    for (mname, a, kw) in payload:
                        ins = getattr(h, mname)(*a, **kw)
                    ins.then_inc(own, 1)
                elif kind == "dma":
                    out_ap, in_ap, k = payload
                    h.dma_start(out=out_ap, in_=in_ap).then_inc(self.sems[k], 16)

        with nc.Block() as block:
            @block.tensor
            def _(h):
                run("pe", h)

            @block.scalar
            def _(h):
                run("act", h)

            @block.vector
            def _(h):
                run("dve", h)

            @block.gpsimd
            def _(h):
                run("pool", h)

            @block.sync
            def _(h):
                run("sp", h)


class Builder:
    def __init__(self):
        nc = self.nc = bass.Bass("TRN2", target_bir_lowering=False)
        self.P = Prog(nc)
        self.dram = {}
        self.tiles = {}
        self.arena_tiles = set()
        self.fence_toks = []
        self.xT = nc.alloc_sbuf_tensor("xT", [128, 16, T], F32)
        self.wslot = [nc.alloc_sbuf_tensor(f"wslot{i}", [128, WSLOT], BF16) for i in range(3)]
        self.ring = [nc.alloc_sbuf_tensor(f"ring{i}", [128, 512], F32) for i in range(3)]
        self.rstd = [nc.alloc_sbuf_tensor(f"rstd{i}", [128, 512], F32) for i in range(2)]
        self.par = nc.alloc_sbuf_tensor("par", [128, NPAR], F32)
        self.mods = [nc.alloc_sbuf_tensor(f"mods{l}", [128, 2, 96], F32) for l in range(2)]
        self.acoef = nc.alloc_sbuf_tensor("acoef", [128, 8, 16], F32)
        self.identf = nc.alloc_sbuf_tensor("identf", [128, 128], F32)
        self.identb = nc.alloc_sbuf_tensor("identb", [128, 128], BF16)
        self.ones = nc.alloc_sbuf_tensor("ones", [128, 128], BF16)
        self.condT = nc.alloc_sbuf_tensor("condT", [128, 16, 2], F32)
        self.scT = nc.alloc_sbuf_tensor("scT", [128, 16, 2], BF16)
        self.hmask = nc.alloc_sbuf_tensor("hmask", [128, 32], F32)
        self.arena = nc.alloc_sbuf_tensor("arena", [128, ARENA // 2], BF16)
        self.psum = nc.alloc_psum_tensor("psum", [128, 8, 512], F32)
        self.wn = 0
        self.rn = 0
        self.bn = 0
        self.nbanks = 4

    def din(self, name, shape):
        t = self.nc.dram_tensor("i_" + name, list(shape), F32, kind="ExternalInput")
        self.dram[name] = t.ap()
        return self.dram[name]

    def dout(self, name, shape):
        t = self.nc.dram_tensor("o_" + name, list(shape), F32, kind="ExternalOutput")
        self.dram[name] = t.ap()
        return self.dram[name]

    def Tl(self, *key, arena=False):
        t = self.tiles.get(key)
        if t is None:
            t = Tile(key, self.fence_toks if arena else None)
            t.excl = key[0] == "ps"
            self.tiles[key] = t
            if arena:
                self.arena_tiles.add(key)
        return t

    def A(self, *key):
        return self.Tl(*key, arena=True)

    def fence(self):
        self.fence_toks = self.P.fence_tokens()
        for k in self.arena_tiles:
            del self.tiles[k]
        self.arena_tiles = set()

    def av(self, off, dt, n):
        assert off % 32 == 0 and off + n * (4 if dt == F32 else 2) <= ARENA, (off, n)
        a = self.arena[:, off // 2: off // 2 + n * (2 if dt == F32 else 1)]
        return a.bitcast(F32) if dt == F32 else a

    def pc(self, name, c=0, n=1):
        o = PCOL[name] + c
        return self.par[:, o:o + n]

    def bank(self):
        b = self.bn
        self.bn = (self.bn + 1) % self.nbanks
        return b, self.Tl("ps", b)

    def ringbuf(self):
        r = self.rn
        self.rn = (self.rn + 1) % 3
        return self.ring[r], self.Tl("ring", r)

    def wload(self, parts, tot):
        s = self.wn
        self.wn = (self.wn + 1) % 3
        nk = parts[0][1]
        view = self.wslot[s][:, 0:nk * tot].rearrange("p (k f) -> p k f", k=nk)
        tw = self.Tl("w", s)
        self.P.dma_multi("pool", [(view[:, :, co:co + ncol], src) for (co, nk_, ncol, src) in parts], f"w{s}", writes=[tw])
        return view, tw

    def mm(self, out_ap, pairs, reads, wt, first=True, last=True):
        def fn(h):
            n = len(pairs)
            for i, (l, r) in enumerate(pairs):
                ins = h.matmul(out_ap, l, r, start=(first and i == 0), stop=(last and i == n - 1))
            return ins
        self.P.op("pe", fn, reads=reads, writes=[wt])

    def mm1(self, out_ap, l, r, start, stop, reads, wt):
        self.P.op("pe", lambda h: h.matmul(out_ap, l, r, start=start, stop=stop), reads=list(reads) + [self.Tl("ones")], writes=[wt])

    def wsrc(self, name, r0, nk, c0, ncol):
        w = self.dram[name]
        return w[r0:r0 + nk * 128, c0:c0 + ncol].rearrange("(k p) f -> p k f", p=128)

    def make_rstd(self, dst_ap, dst_t, bank_ap, bank_t, inv_n):
        P = self.P
        P.op("act", lambda h: h.activation(dst_ap, bank_ap, AF.Sqrt, scale=inv_n, bias=EPS), reads=[bank_t], writes=[dst_t])
        P.op("dve", lambda h: h.reciprocal(dst_ap, dst_ap), reads=[dst_t], writes=[dst_t])

    def norm_mod(self, xsrc, xt, hdst, ht, ntiles, aidx, sh_ap, rd_extra=(), rs_idx=None):
        P = self.P
        SB = 5
        for ti, (t0, n) in enumerate(ntiles):
            bt = self.Tl("ps", SB)
            bap = self.psum[:, SB, 0:n]
            for c in range(16):
                rb, rt = self.ringbuf()
                sq = rb.bitcast(BF16)[:, 0:n]
                P.op("act", lambda h, sq=sq, c=c: h.activation(sq, xsrc(c, t0, n), AF.Square), reads=[xt(c, ti)], writes=[rt])
                self.mm1(bap, self.ones[:], sq, c == 0, c == 15, [rt], bt)
            ri = (ti % 2) if rs_idx is None else rs_idx
            rs = self.rstd[ri][:, 0:n]
            rst = self.Tl("rstd", ri)
            self.make_rstd(rs, rst, bap, bt, 1.0 / D)
            for c in range(16):
                rb, rt = self.ringbuf()
                tmp = rb[:, 0:n]
                P.op("dve", lambda h, tmp=tmp, c=c: h.tensor_tensor(tmp, xsrc(c, t0, n), rs, ALU.mult), reads=[xt(c, ti), rst], writes=[rt])
                P.op("act", lambda h, tmp=tmp, c=c: h.activation(hdst(c, t0, n), tmp, AF.Identity,
                                                                  scale=self.acoef[:, aidx, c:c + 1], bias=sh_ap(c)),
                     reads=[rt, self.Tl("acoef")] + list(rd_extra), writes=[ht(c, ti)])

    def load_T(self, src_rows, n, stg_ap, stg_t, dst, dst_t, skey, ncols=2048, dst_grp=None):
        P = self.P
        P.dma("sp", stg_ap[0:n, :], src_rows, skey, writes=[stg_t])
        nch = (ncols + 127) // 128
        for g in range(0, nch, 4):
            b, bt = self.bank()
            cs = list(range(g, min(g + 4, nch)))

            def tr(h, cs=cs, b=b):
                for j, c in enumerate(cs):
                    w = min(128, ncols - c * 128)
                    ins = h.transpose(self.psum[0:w, b, j * 128:j * 128 + n], stg_ap[0:n, c * 128:c * 128 + w], self.identf[0:n, 0:n])
                return ins
            P.op("pe", tr, reads=[stg_t, self.Tl("ident")], writes=[bt])
            if dst_grp is not None:
                ncs = len(cs)
                src = self.psum[:, b, 0:ncs * 128].rearrange("p (j t) -> p j t", j=ncs)[:, :, 0:n]
                wts = [dst_t(c) for c in cs]
                if (g // 4) % 2 == 0:
                    P.op("act", lambda h, src=src, g=g: h.activation(dst_grp(g), src, AF.Copy), reads=[bt], writes=wts)
                else:
                    P.op("dve", lambda h, src=src, g=g: h.tensor_copy(dst_grp(g), src), reads=[bt], writes=wts)
                continue
            for j, c in enumerate(cs):
                w = min(128, ncols - c * 128)
                eng = "act" if ((g // 4) % 2 == 0) else "dve"
                src = self.psum[0:w, b, j * 128:j * 128 + n]
                if eng == "act":
                    P.op("act", lambda h, c=c, src=src: h.activation(dst(c), src, AF.Copy), reads=[bt], writes=[dst_t(c)])
                else:
                    P.op("dve", lambda h, c=c, src=src: h.tensor_copy(dst(c), src), reads=[bt], writes=[dst_t(c)])

    def build(self):
        nc, P = self.nc, self.P
        din = self.din
        xp = din("xp", [T, D]); xs_own = din("xs_own", [T, D]); xs_all = din("xs_all", [4096, D])
        xs_halo = din("xs_halo", [32, D]); hmask_d = din("hmask", [128, 32])
        cckv = din("cckv", [256, 256]); ckpe = din("ckpe", [256, 64])
        condT_d = din("condT", [128, 16, 2]); par_d = din("params", [128, NPAR])
        ropeq = din("ropeq", [64, 2, T]); ropek = din("ropek", [64, 2, 4096])
        ident_d = din("ident", [128, 128])
        bs_d = din("b_s", [1, 1024]); wsT_d = din("w_sT", [128, 8, 128])
        for l in range(2):
            din(f"mod_w{l}", [D, 6 * D]); din(f"wg{l}", [D, DFF]); din(f"wu{l}", [D, DFF]); din(f"wd{l}", [DFF, D])
        din("ev_w_in", [D, 2880]); din("ev_w_in_sw", [D, 64]); din("ev_w_qb", [512, 1536]); din("ev_w_qb_sw", [512, 512])
        din("ev_w_kvb", [256, 2048]); din("ev_w_o", [D, D]); din("od_w_in", [D, 4096]); din("od_w_o", [D, D])
        yp = self.dout("yp", [T, D]); ys = self.dout("ys", [T, D])
        o_ckv = self.dout("o_ckv", [T, 256]); o_kpe = self.dout("o_kpe", [T, 64])

        t_par, t_id, t_cond = self.Tl("par"), self.Tl("ident"), self.Tl("cond")
        P.dma_multi("sp", [(self.par[:], par_d), (self.identf[:], ident_d), (self.condT[:], condT_d), (self.hmask[:], hmask_d)],
                    "setup", writes=[t_par, t_id, t_cond, self.Tl("hmask")])
        P.op("dve", lambda h: h.tensor_copy(self.identb[:], self.identf[:]), reads=[t_id], writes=[self.Tl("identb")])
        P.op("dve", lambda h: h.memset(self.ones[:], 1.0), writes=[self.Tl("ones")])
        P.op("act", lambda h: h.activation(self.scT[:], self.condT[:], AF.Silu), reads=[t_cond], writes=[self.Tl("scT")])
        self.tiles[("ones",)] = self.Tl("ones")

        self.mods_done = {}
        self.mod_pending = [(0, b_) for b_ in range(8, 24)] + [(1, b_) for b_ in range(24)]
        self.tick_n = 0
        self.tick_every = 1
        import os
        kskip = os.environ.get("KSKIP", "")
        import os
        kpass = os.environ.get("KPASS", "PS")
        kstage = int(os.environ.get("KSTAGE", "9"))
        for ps_ in ("P", "S"):
            if ps_ not in kpass:
                continue
            self.pass_ = ps_
            cond = 0 if ps_ == "P" else 1
            self.fence()
            if "l" not in kskip and not (ps_ == "S" and getattr(self, "preloaded", False)):
                self.load_x(xp if ps_ == "P" else xs_own)
            if "m" not in kskip:
                self.compute_mods(0, 0, 8)
            for l in range(2):
                if kstage >= 1 + 2 * l:
                    if l == 0:
                        self.even_mixer(cond)
                    else:
                        self.odd_mixer(cond)
                if kstage >= 2 + 2 * l:
                    self.compute_mods(l, 12, 24)
                    self.ffn(l, cond)
            if "f" not in kskip:
                nxt = xs_own if (ps_ == "P" and "S" in kpass and "l" not in kskip) else None
                self.final(yp if ps_ == "P" else ys, next_src=nxt)
                self.preloaded = nxt is not None
        P.emit()
        return nc

    def compute_mods(self, l, b0, b1):
        P = self.P
        for blk in range(b0, b1):
            if (l, blk) in self.mods_done:
                continue
            self.mods_done[(l, blk)] = True
            view, tw = self.wload([(0, 16, 512, self.wsrc(f"mod_w{l}", 0, 16, blk * 512, 512))], 512)
            b, bt = self.bank()

            def fn(h, view=view, b=b):
                for f in range(4):
                    for k in range(16):
                        ins = h.matmul(self.psum[:, b, f * 2:f * 2 + 2], view[:, k, f * 128:(f + 1) * 128], self.scT[:, k, :],
                                       start=(k == 0), stop=(k == 15))
                return ins
            P.op("pe", fn, reads=[tw, self.Tl("scT")], writes=[bt])
            tm = self.Tl("mods", l, blk)
            for j in range(2):
                src = self.psum[:, b, 0:8].rearrange("p (f j) -> p f j", j=2)[:, :, j]
                P.op("dve", lambda h, src=src, j=j, blk=blk: h.tensor_tensor(self.mods[l][:, j, blk * 4:blk * 4 + 4], src,
                                                                              self.pc(f"modb{l}", blk * 4, 4), ALU.add),
                     reads=[bt, self.Tl("par")], writes=[tm])
            which, r = divmod(blk, 4)
            if r == 3 and which in (1, 4):
                sub = 0 if which == 1 else 1
                gname = ("nmg" if sub == 0 else "nfg") + str(l)
                rd = [self.Tl("mods", l, which * 4 + i) for i in range(4)] + [self.Tl("par")]
                for j in range(2):
                    P.op("dve", lambda h, j=j, which=which, sub=sub, gname=gname:
                         h.scalar_tensor_tensor(self.acoef[:, l * 4 + sub * 2 + j, :], self.mods[l][:, j, which * 16:(which + 1) * 16], 1.0,
                                                self.pc(gname, 0, 16), ALU.add, ALU.mult),
                         reads=rd, writes=[self.Tl("acoef")])

    def tick(self):
        self.tick_n += 1
        if self.tick_n % self.tick_every:
            return
        while self.mod_pending:
            l, blk = self.mod_pending.pop(0)
            if (l, blk) not in self.mods_done:
                self.compute_mods(l, blk, blk + 1)
                return

    def modcol(self, l, cond, which, c):
        col = which * 16 + c
        return self.mods[l][:, cond, col:col + 1]

    def modtile(self, l, which):
        return [self.Tl("mods", l, which * 4 + i) for i in range(4)]

    def load_x(self, src, chunks=range(8)):
        for n in chunks:
            stg = self.av(32768 + (n % 2) * 8192, F32, 2048)
            st = self.A("stg", n % 2)
            self.load_T(src[n * 128:(n + 1) * 128, :], 128, stg, st,
                        lambda c, n=n: self.xT[:, c, n * 128:(n + 1) * 128],
                        lambda c, n=n: self.Tl("x", c, n // 4), f"stg{n % 2}",
                        dst_grp=lambda g, n=n: self.xT[:, g:g + 4, n * 128:(n + 1) * 128])

    def x_ap(self, c, t0, n):
        return self.xT[:, c, t0:t0 + n]

    def hT_view(self):
        return self.av(0, BF16, 16 * T).rearrange("p (c t) -> p c t", c=16)

    def resid(self, bank_ap, bt, c, half, gate_ap, gate_tiles):
        xs = self.xT[:, c, half * 512:(half + 1) * 512]
        xt = self.Tl("x", c, half)
        self.P.op("dve", lambda h: h.scalar_tensor_tensor(xs, bank_ap, gate_ap, xs, ALU.mult, ALU.add),
                  reads=[bt, xt] + gate_tiles, writes=[xt])

    def ffn(self, l, cond):
        P = self.P
        self.fence()
        hT = self.hT_view()
        self.norm_mod(self.x_ap, lambda c, ti: self.Tl("x", c, ti),
                      lambda c, t0, n: hT[:, c, t0:t0 + n], lambda c, ti: self.A("h", c, ti),
                      [(0, 512), (512, 512)], l * 4 + 2 + cond, lambda c: self.modcol(l, cond, 3, c), rd_extra=self.modtile(l, 3))
        self.tick_every = 2
        act = self.av(32768, BF16, 11 * T).rearrange("p (j t) -> p j t", j=11)
        gate_tiles = self.modtile(l, 5)
        for G in range(4):
            j0 = G * 11
            jj = 0
            while jj < 11:
                nj = min(2, 11 - jj)
                ncol = nj * 128
                c0 = (j0 + jj) * 128
                view, tw = self.wload([(0, 16, ncol, self.wsrc(f"wg{l}", 0, 16, c0, ncol)),
                                       (ncol, 16, ncol, self.wsrc(f"wu{l}", 0, 16, c0, ncol))], 2 * ncol)
                for half in range(2):
                    for q in range(nj):
                        hs = [hT[:, k, half * 512:(half + 1) * 512] for k in range(16)]
                        hts = [self.A("h", k, half) for k in range(16)]
                        bg, btg = self.bank()
                        self.mm(self.psum[:, bg, :], [(view[:, k, q * 128:(q + 1) * 128], hs[k]) for k in range(16)], [tw] + hts, btg)
                        bu, btu = self.bank()
                        self.mm(self.psum[:, bu, :], [(view[:, k, ncol + q * 128:ncol + (q + 1) * 128], hs[k]) for k in range(16)], [tw] + hts, btu)
                        rb, rt = self.ringbuf()
                        sg = rb[:, :]
                        P.op("act", lambda h, sg=sg, bg=bg: h.activation(sg, self.psum[:, bg, :], AF.Silu), reads=[btg], writes=[rt])
                        ta = self.A("act", jj + q, half)
                        P.op("dve", lambda h, sg=sg, bu=bu, jq=jj + q, half=half:
                             h.tensor_tensor(act[:, jq, half * 512:(half + 1) * 512], sg, self.psum[:, bu, :], ALU.mult),
                             reads=[rt, btu], writes=[ta])
                jj += nj
                self.tick()
            for cb in range(4):
                view, tw = self.wload([(0, 11, 512, self.wsrc(f"wd{l}", j0 * 128, 11, cb * 512, 512))], 512)
                for cc in range(4):
                    c = cb * 4 + cc
                    for half in range(2):
                        b, bt = self.bank()
                        self.mm(self.psum[:, b, :], [(view[:, j, cc * 128:(cc + 1) * 128], act[:, j, half * 512:(half + 1) * 512]) for j in range(11)],
                                [tw] + [self.A("act", j, half) for j in range(11)], bt)
                        self.resid(self.psum[:, b, :], bt, c, half, self.modcol(l, cond, 5, c), gate_tiles)
                self.tick()

    def final(self, out, next_src=None):
        P = self.P
        self.fence()
        SB = 5
        for half in range(2):
            bt = self.Tl("ps", SB)
            bap = self.psum[:, SB, :]
            for c in range(16):
                rb, rt = self.ringbuf()
                sq = rb.bitcast(BF16)[:, 0:512]
                P.op("act", lambda h, sq=sq, c=c: h.activation(sq, self.xT[:, c, half * 512:(half + 1) * 512], AF.Square),
                     reads=[self.Tl("x", c, half)], writes=[rt])
                self.mm1(bap, self.ones[:], sq, c == 0, c == 15, [rt], bt)
            rs = self.rstd[half][:, :]
            rst = self.Tl("rstd", half)
            self.make_rstd(rs, rst, bap, bt, 1.0 / D)
            for c in range(16):
                xs = self.xT[:, c, half * 512:(half + 1) * 512]
                P.op("dve", lambda h, xs=xs, c=c: h.scalar_tensor_tensor(xs, xs, self.pc("fng", c), rs, ALU.mult, ALU.mult),
                     reads=[rst, self.Tl("par"), self.Tl("x", c, half)], writes=[self.Tl("x", c, half)])
            for n4 in range(4):
                n = half * 4 + n4
                stg = self.av((n % 2) * 8192, F32, 2048)
                st = self.A("ostg", n % 2)
                for g in range(4):
                    b, bt2 = self.bank()

                    def tr(h, g=g, b=b, n=n):
                        for j in range(4):
                            c = g * 4 + j
                            ins = h.transpose(self.psum[:, b, j * 128:(j + 1) * 128], self.xT[:, c, n * 128:(n + 1) * 128], self.identf[:])
                        return ins
                    P.op("pe", tr, reads=[self.Tl("x", g * 4 + j, half) for j in range(4)] + [self.Tl("ident")], writes=[bt2])
                    dst = stg[:, g * 512:(g + 1) * 512]
                    if g % 2 == 0:
                        P.op("act", lambda h, dst=dst, b=b: h.activation(dst, self.psum[:, b, :], AF.Copy), reads=[bt2], writes=[st])
                    else:
                        P.op("dve", lambda h, dst=dst, b=b: h.tensor_copy(dst, self.psum[:, b, :]), reads=[bt2], writes=[st])
                P.dma("sp", out[n * 128:(n + 1) * 128, :], stg, f"ostg{n % 2}", reads=[st], is_output=True)
            if next_src is not None:
                self.load_x(next_src, range(half * 4, half * 4 + 4))

    def odd_mixer(self, cond):
        P = self.P
        l = 1
        self.compute_mods(1, 0, 12)
        self.fence()
        hT = self.hT_view()
        self.norm_mod(self.x_ap, lambda c, ti: self.Tl("x", c, ti),
                      lambda c, t0, n: hT[:, c, t0:t0 + n], lambda c, ti: self.A("h", c, ti),
                      [(0, 512), (512, 512)], l * 4 + 0 + cond, lambda c: self.modcol(l, cond, 0, c), rd_extra=self.modtile(l, 0))
        self.tick_every = 1
        uT = self.av(32768, BF16, 16 * 512).rearrange("p (c t) -> p c t", c=16)
        vg = self.av(49152, BF16, 4 * 2048).rearrange("p (n f) -> p n f", n=4)
        bsr = self.av(49152, F32, 1024)
        T2 = self.av(65536, F32, 16 * 128).rearrange("p (c t) -> p c t", c=16)
        wsT = self.av(73728, BF16, 1024).rearrange("p (g t) -> p g t", g=8)
        wsTf = self.av(32768, F32, 1024).rearrange("p (g t) -> p g t", g=8)
        st = self.av(75776, F32, 64)
        t_ws, t_wsf, t_bsr, t_T2 = self.A("wsT"), self.A("wsTf"), self.A("bsr"), self.A("T2")
        P.dma("sp", wsTf, self.dram["w_sT"], "osetup1", writes=[t_wsf])
        P.dma("sp", bsr, self.dram["b_s"].partition_broadcast(128), "osetup2", writes=[t_bsr])
        P.op("dve", lambda h: h.tensor_copy(wsT, wsTf), reads=[t_wsf], writes=[t_ws])
        for g in range(8):
            b, bt = self.bank()
            self.mm(self.psum[:, b, 0:128], [(self.ones[:], wsT[:, g, :])], [t_ws], bt)
            for cc in range(2):
                c = g * 2 + cc
                P.op("dve", lambda h, c=c, b=b, g=g: h.scalar_tensor_tensor(T2[:, c, :], self.psum[:, b, 0:128], self.pc("olnb", c),
                                                                         bsr[:, g * 128:(g + 1) * 128], ALU.mult, ALU.add),
                     reads=[bt, t_bsr, self.Tl("par")], writes=[t_T2])
        gate_tiles = self.modtile(l, 2)
        self.fence()
        t_ws, t_T2 = self.A("wsT"), self.A("T2")
        def vpart(half):
            hs = [hT[:, k, half * 512:(half + 1) * 512] for k in range(16)]
            hts = [self.A("h", k, half) for k in range(16)]
            t_st = self.A("ost")
            P.op("dve", lambda h: h.memset(st, 0.0), writes=[t_st])
            for cb in range(4):
                self.tick()
                view, tw = self.wload([(0, 16, 512, self.wsrc("od_w_in", 0, 16, 2048 + cb * 512, 512))], 512)
                for n in range(4):
                    tk = half * 512 + n * 128
                    b, bt = self.bank()
                    self.mm(self.psum[:, b, :], [(hT[:, k, tk:tk + 128], view[:, k, :]) for k in range(16)], [tw] + hts, bt)
                    tv = self.A("vg", n, cb)
                    dstv = vg[:, n, cb * 512:(cb + 1) * 512]
                    P.op("act", lambda h, dstv=dstv, b=b, n=n, cb=cb: h.activation(dstv, self.psum[:, b, :], AF.Gelu_apprx_tanh,
                                                                                accum_out=st[:, n * 4 + cb:n * 4 + cb + 1]),
                         reads=[bt], writes=[tv, t_st])
                    rb, rt = self.ringbuf()
                    P.op("act", lambda h, dstv=dstv, rb=rb, n=n, cb=cb: h.activation(rb.bitcast(BF16)[:, 0:512], dstv, AF.Square,
                                                                                  accum_out=st[:, 16 + n * 4 + cb:16 + n * 4 + cb + 1]),
                         reads=[tv], writes=[rt, t_st])
                    yield
            s1 = st[:, 0:16].rearrange("p (n c) -> p n c", c=4)
            s2 = st[:, 16:32].rearrange("p (n c) -> p n c", c=4)
            mean, ex2, var, rsd, nmr = (st[:, 32 + 4 * i:36 + 4 * i] for i in range(5))

            P.op("dve", lambda h: h.tensor_reduce(mean, s1, mybir.AxisListType.X, ALU.add), reads=[t_st], writes=[t_st])
            P.op("dve", lambda h: h.tensor_reduce(ex2, s2, mybir.AxisListType.X, ALU.add), reads=[t_st], writes=[t_st])
            P.op("dve", lambda h: h.tensor_scalar(mean, mean, 1.0 / 2048, None, ALU.mult), reads=[t_st], writes=[t_st])
            P.op("dve", lambda h: h.tensor_tensor(var, mean, mean, ALU.mult), reads=[t_st], writes=[t_st])
            P.op("dve", lambda h: h.scalar_tensor_tensor(var, ex2, 1.0 / 2048, var, ALU.mult, ALU.subtract), reads=[t_st], writes=[t_st])
            P.op("act", lambda h: h.activation(rsd, var, AF.Sqrt, scale=1.0, bias=EPS), reads=[t_st], writes=[t_st])
            P.op("dve", lambda h: h.reciprocal(rsd, rsd), reads=[t_st], writes=[t_st])
            P.op("dve", lambda h: h.scalar_tensor_tensor(nmr, mean, -1.0, rsd, ALU.mult, ALU.mult), reads=[t_st], writes=[t_st])
            for n in range(4):
                tvs = [self.A("vg", n, cb) for cb in range(4)]
                P.op("act", lambda h, n=n: h.activation(vg[:, n, :], vg[:, n, :], AF.Identity, scale=rsd[:, n:n + 1], bias=nmr[:, n:n + 1]),
                     reads=tvs + [t_st], writes=tvs)
        def upart(half):
            hs = [hT[:, k, half * 512:(half + 1) * 512] for k in range(16)]
            hts = [self.A("h", k, half) for k in range(16)]
            for cb in range(4):
                self.tick()
                view, tw = self.wload([(0, 16, 512, self.wsrc("od_w_in", 0, 16, cb * 512, 512))], 512)
                for cc in range(4):
                    c = cb * 4 + cc
                    b, bt = self.bank()
                    self.mm(self.psum[:, b, :], [(view[:, k, cc * 128:(cc + 1) * 128], hs[k]) for k in range(16)], [tw] + hts, bt)
                    P.op("act", lambda h, c=c, b=b: h.activation(uT[:, c, :], self.psum[:, b, :], AF.Gelu_apprx_tanh),
                         reads=[bt], writes=[self.A("u", c)])
        def sppart(half):
            for c4 in range(4):
                for n in range(4):
                    tvs = [self.A("vg", n, c4)]
                    b, bt = self.bank()

                    def sp(h, n=n, c4=c4, b=b):
                        for j in range(4):
                            c = c4 * 4 + j
                            ins = h.matmul(self.psum[:, b, j * 128:(j + 1) * 128], vg[:, n, c * 128:(c + 1) * 128], wsT[:, c // 2, :],
                                           start=True, stop=True)
                        return ins
                    P.op("pe", sp, reads=tvs + [t_ws], writes=[bt])
                    for j in range(4):
                        c = c4 * 4 + j
                        rb, rt = self.ringbuf()
                        tmp = rb[:, 0:128]
                        P.op("dve", lambda h, tmp=tmp, b=b, j=j, c=c: h.scalar_tensor_tensor(tmp, self.psum[:, b, j * 128:(j + 1) * 128], self.pc("olng", c),
                                                                                          T2[:, c, :], ALU.mult, ALU.add),
                             reads=[bt, t_T2, self.Tl("par")], writes=[rt])
                        us = uT[:, c, n * 128:(n + 1) * 128]
                        P.op("dve", lambda h, tmp=tmp, us=us: h.tensor_tensor(us, us, tmp, ALU.mult),
                             reads=[rt, self.A("u", c)], writes=[self.A("u", c)])
                    yield
        def wopart(half):
            for cb in range(4):
                self.tick()
                view, tw = self.wload([(0, 16, 512, self.wsrc("od_w_o", 0, 16, cb * 512, 512))], 512)
                for cc in range(4):
                    c = cb * 4 + cc
                    b, bt = self.bank()
                    self.mm(self.psum[:, b, :], [(view[:, k, cc * 128:(cc + 1) * 128], uT[:, k, :]) for k in range(16)],
                            [tw] + [self.A("u", k) for k in range(16)], bt)
                    self.resid(self.psum[:, b, :], bt, c, half, self.modcol(l, cond, 2, c), gate_tiles)
        def drain(*gens):
            gens = list(gens)
            while gens:
                for g_ in list(gens):
                    try:
                        next(g_)
                    except StopIteration:
                        gens.remove(g_)
        drain(vpart(0))
        upart(0)
        drain(sppart(0), vpart(1))
        wopart(0)
        upart(1)
        drain(sppart(1))
        wopart(1)

    def even_mixer(self, cond):
        P = self.P
        l = 0
        isS = self.pass_ == "S"
        self.fence()
        if isS:
            xh = self.av(71168, F32, 16 * 32).rearrange("p (c t) -> p c t", c=16)
            hh = self.av(73216, BF16, 16 * 32).rearrange("p (c t) -> p c t", c=16)
            stg = self.av(0, F32, 2048)
            self.load_T(self.dram["xs_halo"], 32, stg, self.A("hstg"), lambda c: xh[:, c, :], lambda c: self.A("xh", c), "hstg",
                        dst_grp=lambda g: xh[:, g:g + 4, :])
            self.norm_mod(lambda c, t0, n: xh[:, c, :], lambda c, ti: self.A("xh", c),
                          lambda c, t0, n: hh[:, c, :], lambda c, ti: self.A("hh", c), [(0, 32)], cond,
                          lambda c: self.modcol(l, cond, 0, c), rd_extra=self.modtile(l, 0))
            self.fence()
        hT = self.hT_view()
        self.norm_mod(self.x_ap, lambda c, ti: self.Tl("x", c, ti),
                      lambda c, t0, n: hT[:, c, t0:t0 + n], lambda c, ti: self.A("h", c, ti),
                      [(0, 512), (512, 512)], l * 4 + 0 + cond, lambda c: self.modcol(l, cond, 0, c), rd_extra=self.modtile(l, 0))
        self.tick_every = 1
        NK = NKS if isS else T
        if isS:
            ckv_all = self.av(32768, BF16, 2 * NKS).rearrange("p (c t) -> p c t", c=2)
            kpe_all = self.av(50176, BF16, NKS)
        else:
            ckv_all = self.av(32768, BF16, 2 * T).rearrange("p (c t) -> p c t", c=2)
            kpe_all = self.av(36864, BF16, T)
        qg = self.av(58880, BF16, 4 * T).rearrange("p (c t) -> p c t", c=4)
        rstd_q = self.av(67072, F32, T)
        if not isS:
            self.kside(lambda k, t0, n: hT[:, k, t0:t0 + n], lambda k, ti: self.A("h", k, ti), [(0, 512), (512, 512)],
                       ckv_all, kpe_all, 0, cond, state_out=True)
        if isS:
            PADW = 1054
            a_pad = self.av(38912, BF16, 8 * PADW).rearrange("p (c t) -> p c t", c=8)
        else:
            PADW = 4 * 286
            a_pad = self.av(38912, BF16, 8 * PADW).rearrange("p (c t) -> p c t", c=8)
            ap4 = self.av(38912, BF16, 8 * PADW).rearrange("p (c s t) -> p c s t", c=8, s=4)
            P.op("dve", lambda h: h.memset(a_pad[:, :, :], 0.0), writes=[self.A("apad", c) for c in range(8)])
        for cp in range(4):
            view, tw = self.wload([(0, 16, 256, self.wsrc("ev_w_in", 0, 16, cp * 256, 256)),
                                   (256, 16, 256, self.wsrc("ev_w_in", 0, 16, 1024 + cp * 256, 256))], 512)
            for q in range(2):
                c = cp * 2 + q
                tiles_ = [(0, 512, 0), (512, 512, 1)] + ([(0, 32, 2)] if isS else [])
                for (t0, n, kind) in tiles_:
                    if kind == 2:
                        hs = [hh[:, k, :] for k in range(16)]
                        hts = [self.A("hh", k) for k in range(16)]
                    else:
                        hs = [hT[:, k, t0:t0 + n] for k in range(16)]
                        hts = [self.A("h", k, kind) for k in range(16)]
                    bv, btv = self.bank()
                    self.mm(self.psum[:, bv, 0:n], [(view[:, k, q * 128:(q + 1) * 128], hs[k]) for k in range(16)], [tw] + hts, btv)
                    bg, btg = self.bank()
                    self.mm(self.psum[:, bg, 0:n], [(view[:, k, 256 + q * 128:256 + (q + 1) * 128], hs[k]) for k in range(16)], [tw] + hts, btg)
                    rb, rt = self.ringbuf()
                    sg = rb[:, 0:n]
                    P.op("act", lambda h, sg=sg, bg=bg, n=n: h.activation(sg, self.psum[:, bg, 0:n], AF.Sigmoid), reads=[btg], writes=[rt])
                    ta = self.A("apad", c)
                    if kind == 2:
                        P.op("dve", lambda h, sg=sg: h.tensor_tensor(sg, sg, self.hmask[:, :], ALU.mult), reads=[rt, self.Tl("hmask")], writes=[rt])
                        P.op("dve", lambda h, sg=sg, bv=bv, c=c: h.tensor_tensor(a_pad[:, c, 0:15], sg[:, 0:15], self.psum[:, bv, 0:15], ALU.mult),
                             reads=[rt, btv], writes=[ta])
                        P.op("dve", lambda h, sg=sg, bv=bv, c=c: h.tensor_tensor(a_pad[:, c, 1039:1054], sg[:, 15:30], self.psum[:, bv, 15:30], ALU.mult),
                             reads=[rt, btv], writes=[ta])
                    elif isS:
                        P.op("dve", lambda h, sg=sg, bv=bv, c=c, t0=t0: h.tensor_tensor(a_pad[:, c, 15 + t0:15 + t0 + 512], sg, self.psum[:, bv, :], ALU.mult),
                             reads=[rt, btv], writes=[ta])
                    else:
                        s0 = t0 // 256
                        P.op("dve", lambda h, sg=sg, bv=bv, c=c, s0=s0: h.tensor_tensor(ap4[:, c, s0:s0 + 2, 15:271],
                                                                                     sg.rearrange("p (s t) -> p s t", s=2),
                                                                                     self.psum[:, bv, :].rearrange("p (s t) -> p s t", s=2), ALU.mult),
                             reads=[rt, btv], writes=[ta])
            self.tick()
        view, tw = self.wload([(0, 16, 512, self.wsrc("ev_w_in", 0, 16, 2048, 512))], 512)
        SB = 5
        for half in range(2):
            hs = [hT[:, k, half * 512:(half + 1) * 512] for k in range(16)]
            hts = [self.A("h", k, half) for k in range(16)]
            sbt = self.Tl("ps", SB)
            for c in range(4):
                b, bt = self.bank()
                self.mm(self.psum[:, b, :], [(view[:, k, c * 128:(c + 1) * 128], hs[k]) for k in range(16)], [tw] + hts, bt)
                rb, rt = self.ringbuf()
                sq = rb.bitcast(BF16)[:, 0:512]
                P.op("act", lambda h, sq=sq, b=b: h.activation(sq, self.psum[:, b, :], AF.Square), reads=[bt], writes=[rt])
                self.mm1(self.psum[:, SB, :], self.ones[:], sq, c == 0, c == 3, [rt], sbt)
                P.op("dve", lambda h, b=b, c=c, half=half: h.tensor_scalar(qg[:, c, half * 512:(half + 1) * 512], self.psum[:, b, :],
                                                                          self.pc("qng", c), None, ALU.mult),
                     reads=[bt, self.Tl("par")], writes=[self.A("qg", c, half)])
            self.make_rstd(rstd_q[:, half * 512:(half + 1) * 512], self.A("rstdq", half), self.psum[:, SB, :], sbt, 1.0 / 512)
        self.fence()
        y = self.av(0, F32, 8 * T).rearrange("p (c t) -> p c t", c=8)
        diag = self.av(71168, BF16, 31 * 128).rearrange("p (k t) -> p k t", k=31)
        cw = self.pc("convw", 0, 248).rearrange("p (c k) -> p c k", c=8)
        S1, S2 = 4, 5
        for c in range(8):
            t_dgs = [self.A("diag", 0), self.A("diag", 1)]
            for k in range(31):
                P.op("dve", lambda h, c=c, k=k: h.tensor_scalar(diag[:, k, :], self.identb[:], cw[:, c, k:k + 1], None, ALU.mult),
                     reads=[self.Tl("identb"), self.Tl("par")], writes=[t_dgs[k // 16]])
            if isS:
                units = [(half * 512, 512, a_pad[:, c, half * 512:half * 512 + 542]) for half in range(2)]
            else:
                units = [(s * 256, 256, a_pad[:, c, s * 286:(s + 1) * 286]) for s in range(4)]
            if c % 2 == 1:
                self.tick()
            ubanks = [self.bank() for _ in units]
            for (t0, n, win), (b, bt) in zip(units, ubanks):
                self.mm(self.psum[:, b, 0:n], [(diag[:, k, :], win[:, k:k + n]) for k in range(16)], [t_dgs[0], self.A("apad", c)], bt, last=False)
            for (t0, n, win), (b, bt) in zip(units, ubanks):
                self.mm(self.psum[:, b, 0:n], [(diag[:, k, :], win[:, k:k + n]) for k in range(16, 31)], [t_dgs[1], self.A("apad", c)], bt, first=False)
                P.op("act", lambda h, b=b, c=c, t0=t0, n=n: h.activation(y[:, c, t0:t0 + n], self.psum[:, b, 0:n], AF.Identity,
                                                                      bias=self.pc("convb", c)),
                     reads=[bt, self.Tl("par")], writes=[self.A("y", c, t0 // 512)])
        self.fence()
        aT = self.av(38912, BF16, 8 * T).rearrange("p (c t) -> p c t", c=8)
        for half in range(2):
            t1, t2 = self.Tl("ps", S1), self.Tl("ps", S2)
            for c in range(8):
                ys_ = y[:, c, half * 512:(half + 1) * 512]
                rb, rt = self.ringbuf()
                yb = rb.bitcast(BF16)[:, 0:512]
                P.op("dve", lambda h, yb=yb, ys_=ys_: h.tensor_copy(yb, ys_), reads=[self.A("y", c, half)], writes=[rt])
                self.mm1(self.psum[:, S1, :], self.ones[:], yb, c == 0, c == 7, [rt], t1)
                rb2, rt2 = self.ringbuf()
                sq = rb2.bitcast(BF16)[:, 0:512]
                P.op("act", lambda h, sq=sq, ys_=ys_: h.activation(sq, ys_, AF.Square), reads=[self.A("y", c, half)], writes=[rt2])
                self.mm1(self.psum[:, S2, :], self.ones[:], sq, c == 0, c == 7, [rt2], t2)
            mean = self.rstd[0][:, :]
            rs = self.rstd[1][:, :]
            tm, tr_ = self.Tl("rstd", 0), self.Tl("rstd", 1)
            P.op("act", lambda h: h.activation(mean, self.psum[:, S1, :], AF.Copy, scale=1.0 / 1024), reads=[t1], writes=[tm])
            rb, rt = self.ringbuf()
            msq = rb[:, :]
            P.op("dve", lambda h, msq=msq: h.tensor_tensor(msq, mean, mean, ALU.mult), reads=[tm], writes=[rt])
            P.op("dve", lambda h, msq=msq: h.scalar_tensor_tensor(rs, self.psum[:, S2, :], 1.0 / 1024, msq, ALU.mult, ALU.subtract),
                 reads=[t2, rt], writes=[tr_])
            P.op("act", lambda h: h.activation(rs, rs, AF.Sqrt, scale=1.0, bias=EPS), reads=[tr_], writes=[tr_])
            P.op("dve", lambda h: h.reciprocal(rs, rs), reads=[tr_], writes=[tr_])
            for c in range(8):
                ys_ = y[:, c, half * 512:(half + 1) * 512]
                rb, rt = self.ringbuf()
                tmp = rb[:, :]
                P.op("dve", lambda h, tmp=tmp, ys_=ys_: h.tensor_tensor(tmp, ys_, mean, ALU.subtract), reads=[self.A("y", c, half), tm], writes=[rt])
                P.op("dve", lambda h, tmp=tmp: h.tensor_tensor(tmp, tmp, rs, ALU.mult), reads=[rt, tr_], writes=[rt])
                P.op("act", lambda h, tmp=tmp, c=c, half=half: h.activation(aT[:, c, half * 512:(half + 1) * 512], tmp, AF.Silu,
                                                                         scale=self.pc("clng", c), bias=self.pc("clnb", c)),
                     reads=[rt, self.Tl("par")], writes=[self.A("aT", c, half)])
        self.compute_mods(0, 8, 12)
        gate_tiles = self.modtile(l, 2)
        for half in range(2):
          for cb in range(4):
            view, tw = self.wload([(0, 8, 512, self.wsrc("ev_w_o", 0, 8, cb * 512, 512))], 512)
            for cc in range(4):
                c = cb * 4 + cc
                if True:
                    b, bt = self.bank()
                    self.mm(self.psum[:, b, :], [(view[:, k, cc * 128:(cc + 1) * 128], aT[:, k, half * 512:(half + 1) * 512]) for k in range(8)],
                            [tw] + [self.A("aT", k, half) for k in range(8)], bt)
                    self.resid(self.psum[:, b, :], bt, c, half, self.modcol(l, cond, 2, c), gate_tiles)
        if isS:
            self.fence()
            self.kside_sample(ckv_all, kpe_all, cond)
        self.fence()
        self.attention(ckv_all, kpe_all, qg, rstd_q, NK)
        attT = self.av(0, BF16, 8 * T).rearrange("p (c t) -> p c t", c=8)
        for cb in range(4):
            view, tw = self.wload([(0, 8, 512, self.wsrc("ev_w_o", 1024, 8, cb * 512, 512))], 512)
            for cc in range(4):
                c = cb * 4 + cc
                for half in range(2):
                    b, bt = self.bank()
                    self.mm(self.psum[:, b, :], [(view[:, k, cc * 128:(cc + 1) * 128], attT[:, k, half * 512:(half + 1) * 512]) for k in range(8)],
                            [tw] + [self.A("att", k, half) for k in range(8)], bt)
                    self.resid(self.psum[:, b, :], bt, c, half, self.modcol(l, cond, 2, c), gate_tiles)

    def kside(self, hsrc, htile, ntiles, ckv_all, kpe_all, koff, cond, state_out=False, rope_tok0=None, rs_idx=None):
        P = self.P
        SB = 4
        for ti, (t0, n) in enumerate(ntiles):
            view, tw = self.wload([(0, 16, 320, self.wsrc("ev_w_in", 0, 16, 2560, 320)),
                                   (320, 16, 64, self.wsrc("ev_w_in_sw", 0, 16, 0, 64))], 384)
            hs = [hsrc(k, t0, n) for k in range(16)]
            hts = [htile(k, ti) for k in range(16)]
            kb = []
            sbt = self.Tl("ps", SB)
            for c in range(2):
                b, bt = self.bank()
                self.mm(self.psum[:, b, 0:n], [(view[:, k, c * 128:(c + 1) * 128], hs[k]) for k in range(16)], [tw] + hts, bt)
                kb.append((b, bt))
                rb, rt = self.ringbuf()
                sq = rb.bitcast(BF16)[:, 0:n]
                P.op("act", lambda h, sq=sq, b=b, n=n: h.activation(sq, self.psum[:, b, 0:n], AF.Square), reads=[bt], writes=[rt])
                self.mm1(self.psum[:, SB, 0:n], self.ones[:], sq, c == 0, c == 1, [rt], sbt)
            if rs_idx is None:
                rs = self.rstd[ti % 2][:, 0:n]
                rst = self.Tl("rstd", ti % 2)
            else:
                rs = self.rstd[rs_idx][:, 256:256 + n]
                rst = self.Tl("rstdk", rs_idx)
            self.make_rstd(rs, rst, self.psum[:, SB, 0:n], sbt, 1.0 / 256)
            kt = self.A("kall", (koff + t0) // 512)
            if state_out:
                ckf = self.av(71168, F32, 2 * 512).rearrange("p (c t) -> p c t", c=2)
                kpf = self.av(75264, F32, 512)
            for c in range(2):
                b, bt = kb[c]
                if state_out:
                    P.op("dve", lambda h, b=b, c=c, n=n: h.scalar_tensor_tensor(ckf[:, c, 0:n], self.psum[:, b, 0:n], self.pc("kvng", c), rs, ALU.mult, ALU.mult),
                         reads=[bt, rst, self.Tl("par")], writes=[self.A("ckf", c)])
                    P.op("act", lambda h, c=c, n=n, t0=t0: h.activation(ckv_all[:, c, koff + t0:koff + t0 + n], ckf[:, c, 0:n], AF.Copy),
                         reads=[self.A("ckf", c)], writes=[kt])
                else:
                    P.op("dve", lambda h, b=b, c=c, n=n, t0=t0: h.scalar_tensor_tensor(ckv_all[:, c, koff + t0:koff + t0 + n], self.psum[:, b, 0:n],
                                                                                    self.pc("kvng", c), rs, ALU.mult, ALU.mult),
                         reads=[bt, rst, self.Tl("par")], writes=[kt])
            b, bt = self.bank()
            self.mm(self.psum[0:64, b, 0:n], [(view[:, k, 256:320], hs[k]) for k in range(16)], [tw] + hts, bt)
            if rope_tok0 is None:
                if state_out:
                    P.op("act", lambda h, b=b, n=n: h.activation(kpf[0:64, 0:n], self.psum[0:64, b, 0:n], AF.Copy), reads=[bt], writes=[self.A("kpf")])
                P.op("dve", lambda h, b=b, n=n, t0=t0: h.tensor_copy(kpe_all[0:64, koff + t0:koff + t0 + n], self.psum[0:64, b, 0:n]), reads=[bt], writes=[kt])
            else:
                b2, bt2 = self.bank()
                self.mm(self.psum[0:64, b2, 0:n], [(view[:, k, 320:384], hs[k]) for k in range(16)], [tw] + hts, bt2)
                rk = self.av(71168, F32, 2 * 512).rearrange("p (j t) -> p j t", j=2)
                trk = self.A("ropek")
                P.dma("sp", rk[0:64, :, 0:n], self.dram["ropek"][:, :, rope_tok0 + t0:rope_tok0 + t0 + n], "ropek", writes=[trk])
                rb, rt = self.ringbuf()
                rb2, rt2 = self.ringbuf()
                P.op("dve", lambda h, rb=rb, b=b, n=n: h.tensor_tensor(rb[0:64, 0:n], self.psum[0:64, b, 0:n], rk[0:64, 0, 0:n], ALU.mult), reads=[bt, trk], writes=[rt])
                P.op("dve", lambda h, rb2=rb2, b2=b2, n=n: h.tensor_tensor(rb2[0:64, 0:n], self.psum[0:64, b2, 0:n], rk[0:64, 1, 0:n], ALU.mult), reads=[bt2, trk], writes=[rt2])
                P.op("dve", lambda h, rb=rb, rb2=rb2, n=n, t0=t0: h.tensor_tensor(kpe_all[0:64, koff + t0:koff + t0 + n], rb[0:64, 0:n], rb2[0:64, 0:n], ALU.add),
                     reads=[rt, rt2], writes=[kt])
            if state_out:
                so = self.av(77312, F32, 320)
                for n4 in range(n // 128):
                    b, bt = self.bank()
                    tso = self.A("so")

                    def tr(h, b=b, n4=n4):
                        for c in range(2):
                            h.transpose(self.psum[:, b, c * 128:(c + 1) * 128], ckf[:, c, n4 * 128:(n4 + 1) * 128], self.identf[:])
                        return h.transpose(self.psum[:, b, 256:320], kpf[0:64, n4 * 128:(n4 + 1) * 128], self.identf[0:64, 0:64])
                    P.op("pe", tr, reads=[self.A("ckf", 0), self.A("ckf", 1), self.A("kpf"), self.Tl("ident")], writes=[bt])
                    P.op("dve", lambda h, b=b: h.tensor_copy(so[:, 0:320], self.psum[:, b, 0:320]), reads=[bt], writes=[tso])
                    r0 = t0 + n4 * 128
                    P.dma("sp", self.dram["o_ckv"][r0:r0 + 128, :], so[:, 0:256], "so", reads=[tso], is_output=True)
                    P.dma("sp", self.dram["o_kpe"][r0:r0 + 128, :], so[:, 256:320], "so", reads=[tso], is_output=True)

    def kside_sample(self, ckv_all, kpe_all, cond):
        P = self.P
        l = 0
        stg = self.av(0, F32, 2048)
        for n in range(2):
            tk = self.A("kall", 0)
            self.load_T(self.dram["cckv"][n * 128:(n + 1) * 128, :], 128, stg[:, 0:256], self.A("kstg", 0),
                        lambda c, n=n: ckv_all[:, c, n * 128:(n + 1) * 128], lambda c: tk, "kstg0", ncols=256)
            self.load_T(self.dram["ckpe"][n * 128:(n + 1) * 128, :], 128, stg[:, 256:320], self.A("kstg", 0),
                        lambda c, n=n: kpe_all[0:64, n * 128:(n + 1) * 128], lambda c: tk, "kstg0", ncols=64)
        stgs = [self.av(i * 8192, F32, 2048) for i in range(2)]
        xns = [self.av(16384 + i * 4096, BF16, 2048) for i in range(2)]
        hts_ = [self.av(24576 + i * 4096, BF16, 16 * 128).rearrange("p (c t) -> p c t", c=16) for i in range(2)]
        ssb = self.av(75264, F32, 8)
        psb = self.psum[:, :, :].bitcast(BF16)

        def s1(tt):
            pi = tt % 2
            st_, xn = stgs[pi], xns[pi]
            tst, txn, tss = self.A("kstg", pi), self.A("kxn", pi), self.A("kss", pi)
            P.dma("sp", st_, self.dram["xs_all"][tt * 128:(tt + 1) * 128, :], f"kstg{pi}", writes=[tst])
            P.op("dve", lambda h: h.memset(ssb[:, pi:pi + 1], 0.0), writes=[tss])
            P.op("act", lambda h: h.activation(xn, st_, AF.Square, accum_out=ssb[:, pi:pi + 1]), reads=[tst], writes=[txn, tss])
            P.op("act", lambda h: h.activation(ssb[:, 2 + pi:3 + pi], ssb[:, pi:pi + 1], AF.Sqrt, scale=1.0 / D, bias=EPS), reads=[tss], writes=[tss])
            P.op("dve", lambda h: h.reciprocal(ssb[:, 2 + pi:3 + pi], ssb[:, 2 + pi:3 + pi]), reads=[tss], writes=[tss])
            P.op("dve", lambda h: h.tensor_scalar(xn, st_, ssb[:, 2 + pi:3 + pi], None, ALU.mult), reads=[tst, tss], writes=[txn])

        def s2(tt):
            pi = tt % 2
            xn, ht_ = xns[pi], hts_[pi]
            txn = self.A("kxn", pi)
            rd = [self.Tl("acoef")] + self.modtile(l, 0)
            for g in range(4):
                b, bt = self.bank()

                def tr(h, g=g, b=b):
                    for j in range(4):
                        c = g * 4 + j
                        ins = h.transpose(psb[:, b, j * 128:(j + 1) * 128], xn[:, c * 128:(c + 1) * 128], self.identb[:])
                    return ins
                P.op("pe", tr, reads=[txn, self.Tl("identb")], writes=[bt])
                for j in range(4):
                    c = g * 4 + j
                    src = psb[:, b, j * 128:(j + 1) * 128]
                    if g % 2 == 0:
                        P.op("act", lambda h, c=c, src=src: h.activation(ht_[:, c, :], src, AF.Identity, scale=self.acoef[:, cond, c:c + 1],
                                                                     bias=self.modcol(l, cond, 0, c)),
                             reads=[bt] + rd, writes=[self.A("kh", pi, c)])
                    else:
                        P.op("dve", lambda h, c=c, src=src: h.tensor_scalar(ht_[:, c, :], src, self.acoef[:, cond, c:c + 1], self.modcol(l, cond, 0, c),
                                                                        ALU.mult, ALU.add),
                             reads=[bt] + rd, writes=[self.A("kh", pi, c)])

        def s3(tt):
            pi = tt % 2
            ht_ = hts_[pi]
            self.kside(lambda k, t0, n, ht_=ht_: ht_[:, k, :], lambda k, ti, pi=pi: self.A("kh", pi, k), [(0, 128)], ckv_all, kpe_all,
                       256 + tt * 128, cond, rope_tok0=tt * 128, rs_idx=pi)
        for step in range(32 + 2):
            if step < 32:
                s1(step)
            if 0 <= step - 1 < 32:
                s2(step - 1)
            if 0 <= step - 2 < 32:
                s3(step - 2)

    def attention(self, ckv_all, kpe_all, qg, rstd_q, NK):
        P = self.P
        isS = self.pass_ == "S"
        attT = self.av(0, BF16, 8 * T).rearrange("p (c t) -> p c t", c=8)
        PT = [self.av(16384 + i * 1024, BF16, 512) for i in range(3)]
        qn_h = self.av(19456, BF16, T)
        qpe_h = self.av(21504, BF16, T)
        wkvs = [self.av(o_, BF16, 512).rearrange("p (k f) -> p k f", k=2) for o_ in (23552, 75776)]
        wqs = [self.av(o_, BF16, 1024).rearrange("p (k f) -> p k f", k=4) for o_ in (24576, 76800)]

        def load_head_w(hd):
            si = hd % 2
            pairs = [(wkvs[si], self.wsrc("ev_w_kvb", 0, 2, hd * 256, 256)),
                     (wqs[si][:, :, 0:192], self.wsrc("ev_w_qb", 0, 4, hd * 192, 192))]
            if isS:
                pairs.append((wqs[si][:, :, 192:256], self.wsrc("ev_w_qb_sw", 0, 4, hd * 64, 64)))
            P.dma_multi("pool", pairs, f"whd{si}", writes=[self.A("whd", si)])
        nkc = NK // 128
        if isS:
            crs = [self.av(71168, F32, T), self.av(26624, F32, T)]
            tcr = self.A("crs")
            P.dma_multi("sp", [(crs[j][0:64, :], self.dram["ropeq"][:, j, :]) for j in range(2)], "ropeq", writes=[tcr])
            for j in range(2):
                for half in range(2):
                    sl = crs[j][0:64, half * 512:(half + 1) * 512]
                    P.op("dve", lambda h, sl=sl, half=half: h.tensor_tensor(sl, sl, rstd_q[0:64, half * 512:(half + 1) * 512], ALU.mult),
                         reads=[tcr, self.A("rstdq", half)], writes=[tcr])
        OB, SBK = 6, 7
        kts = [self.A("kall", i) for i in range((NK + 511) // 512)]
        load_head_w(0)
        for hd in range(8):
            wkv, wq = wkvs[hd % 2], wqs[hd % 2]
            twk = twq = self.A("whd", hd % 2)
            if hd + 1 < 8:
                load_head_w(hd + 1)
            s = self.wn
            self.wn = (self.wn + 1) % 3
            slot_t = self.Tl("w", s)
            n512 = (NK + 511) // 512

            def sub(nm, slot_t=slot_t):
                t = Tile(nm)
                t.w = list(slot_t.w)
                t.r = list(slot_t.r)
                return t
            tK = [sub(("K", i)) for i in range(n512)]
            tV = [sub(("V", i)) for i in range(n512)]
            KT = self.wslot[s][:, 0:NK]
            V = self.wslot[s][:, NK:2 * NK].rearrange("p (n d) -> p n d", d=128)
            for i, k0 in enumerate(range(0, NK, 512)):
                n = min(512, NK - k0)
                b, bt = self.bank()
                self.mm(self.psum[:, b, 0:n], [(wkv[:, k, 0:128], ckv_all[:, k, k0:k0 + n]) for k in range(2)], [twk] + kts, bt)
                if i % 2 == 0:
                    P.op("act", lambda h, b=b, k0=k0, n=n: h.activation(KT[:, k0:k0 + n], self.psum[:, b, 0:n], AF.Copy), reads=[bt], writes=[tK[i]])
                else:
                    P.op("dve", lambda h, b=b, k0=k0, n=n: h.tensor_copy(KT[:, k0:k0 + n], self.psum[:, b, 0:n]), reads=[bt], writes=[tK[i]])
                b, bt = self.bank()
                nch = n // 128

                def vf(h, b=b, k0=k0, nch=nch):
                    for j in range(nch):
                        for k in range(2):
                            ins = h.matmul(self.psum[:, b, j * 128:(j + 1) * 128], ckv_all[:, k, k0 + j * 128:k0 + (j + 1) * 128], wkv[:, k, 128:256],
                                           start=(k == 0), stop=(k == 1))
                    return ins
                P.op("pe", vf, reads=[twk] + kts, writes=[bt])
                vdst = V[:, k0 // 128:k0 // 128 + nch, :]
                vsrc = self.psum[:, b, 0:n].rearrange("p (n d) -> p n d", d=128)
                if i % 2 == 1:
                    P.op("act", lambda h, vdst=vdst, vsrc=vsrc: h.activation(vdst, vsrc, AF.Copy), reads=[bt], writes=[tV[i]])
                else:
                    P.op("dve", lambda h, vdst=vdst, vsrc=vsrc: h.tensor_copy(vdst, vsrc), reads=[bt], writes=[tV[i]])
            for half in range(2):
                qs = [qg[:, k, half * 512:(half + 1) * 512] for k in range(4)]
                qts = [self.A("qg", k, half) for k in range(4)]
                b, bt = self.bank()
                self.mm(self.psum[:, b, :], [(wq[:, k, 0:128], qs[k]) for k in range(4)], [twq] + qts, bt)
                P.op("dve", lambda h, b=b, half=half: h.tensor_tensor(qn_h[:, half * 512:(half + 1) * 512], self.psum[:, b, :],
                                                                     rstd_q[:, half * 512:(half + 1) * 512], ALU.mult),
                     reads=[bt, self.A("rstdq", half)], writes=[self.A("qn", half)])
                b, bt = self.bank()
                self.mm(self.psum[0:64, b, :], [(wq[:, k, 128:192], qs[k]) for k in range(4)], [twq] + qts, bt)
                if not isS:
                    P.op("dve", lambda h, b=b, half=half: h.tensor_tensor(qpe_h[0:64, half * 512:(half + 1) * 512], self.psum[0:64, b, :],
                                                                         rstd_q[0:64, half * 512:(half + 1) * 512], ALU.mult),
                         reads=[bt, self.A("rstdq", half)], writes=[self.A("qpe", half)])
                else:
                    b2, bt2 = self.bank()
                    self.mm(self.psum[0:64, b2, :], [(wq[:, k, 192:256], qs[k]) for k in range(4)], [twq] + qts, bt2)
                    rb, rt = self.ringbuf()
                    rb2, rt2 = self.ringbuf()
                    P.op("dve", lambda h, rb=rb, b=b, half=half: h.tensor_tensor(rb[0:64, :], self.psum[0:64, b, :], crs[0][0:64, half * 512:(half + 1) * 512], ALU.mult),
                         reads=[bt, tcr], writes=[rt])
                    P.op("dve", lambda h, rb2=rb2, b2=b2, half=half: h.tensor_tensor(rb2[0:64, :], self.psum[0:64, b2, :], crs[1][0:64, half * 512:(half + 1) * 512], ALU.mult),
                         reads=[bt2, tcr], writes=[rt2])
                    P.op("dve", lambda h, rb=rb, rb2=rb2, half=half: h.tensor_tensor(qpe_h[0:64, half * 512:(half + 1) * 512], rb[0:64, :], rb2[0:64, :], ALU.add),
                         reads=[rt, rt2], writes=[self.A("qpe", half)])
            if isS:
                units = [(half * 512, 512, list(range(nkc)), half) for half in range(2)]
            else:
                units = [(s_ * 256, 256, [2 * s_, 2 * s_ + 1], s_ // 2) for s_ in range(4)]
            for (q0, nq, kcs, half) in units:
                qrd = [self.A("qn", half), self.A("qpe", half)]
                tO, tS = self.Tl("ps", OB), self.Tl("ps", SBK)
                pend = []

                def emit_qk(j):
                    kc = kcs[j]
                    b, bt = self.bank()
                    self.mm(self.psum[:, b, 0:nq], [(KT[:, kc * 128:(kc + 1) * 128], qn_h[:, q0:q0 + nq]),
                                                    (kpe_all[0:64, kc * 128:(kc + 1) * 128], qpe_h[0:64, q0:q0 + nq])],
                            qrd + [kts[kc // 4], tK[kc // 4]], bt)
                    pend.append((b, bt))
                nj = len(kcs)
                for j in range(min(2, nj)):
                    emit_qk(j)
                for j in range(nj):
                    b, bt = pend[j]
                    pi = j % 3
                    tp = self.A("PT", pi)
                    P.op("act", lambda h, b=b, pi=pi: h.activation(PT[pi][:, 0:nq], self.psum[:, b, 0:nq], AF.Exp, scale=SCALE), reads=[bt], writes=[tp])
                    if j + 2 < nj:
                        emit_qk(j + 2)
                    kc = kcs[j]
                    self.mm1(self.psum[:, OB, 0:nq], V[:, kc, :], PT[pi][:, 0:nq], j == 0, j == nj - 1, [tp, tV[kc // 4]], tO)
                    self.mm1(self.psum[:, SBK, 0:nq], self.ones[:], PT[pi][:, 0:nq], j == 0, j == nj - 1, [tp], tS)
                rb, rt = self.ringbuf()
                P.op("dve", lambda h, rb=rb: h.reciprocal(rb[:, 0:nq], self.psum[:, SBK, 0:nq]), reads=[tS], writes=[rt])
                P.op("dve", lambda h, rb=rb, hd=hd, q0=q0: h.tensor_tensor(attT[:, hd, q0:q0 + nq], self.psum[:, OB, 0:nq], rb[:, 0:nq], ALU.mult),
                     reads=[tO, rt], writes=[self.A("att", hd, half)])
            toks = {}
            for t_ in tK + tV:
                for (k_, v_) in t_.w + t_.r:
                    toks[k_] = max(toks.get(k_, 0), v_)
            slot_t.w = []
            slot_t.r = list(toks.items())
            self.tick()


_CACHE = {}


def _fm(v):
    v = np.asarray(v, np.float32)
    return np.ascontiguousarray(v.reshape(-1, 128).T)


def _rope_tables(pos_tok):
    pos_tok = np.asarray(pos_tok)
    rows = (pos_tok // 64).astype(np.float32)
    cols = (pos_tok % 64).astype(np.float32)
    inv = np.power(np.float32(10000.0), -np.arange(16, dtype=np.float32) * np.float32(2.0) / np.float32(32)).astype(np.float32)
    out = np.zeros((64, 2, len(pos_tok)), np.float32)
    for d in range(64):
        i, j, f = d // 32, (d % 32) // 16, d % 16
        ang = ((rows if i == 0 else cols) * inv[f]).astype(np.float32)
        out[d, 0] = np.cos(ang)
        out[d, 1] = np.sin(ang) * (-1.0 if j == 0 else 1.0)
    return out


_PARTNER = np.array([(d // 32) * 32 + (1 - (d % 32) // 16) * 16 + d % 16 for d in range(64)])


def kernel(x_prompt, x_sample, cache_ckv, cache_kpe, c, c_ctx, mod_w, mod_b, norm_mix_g, norm_ffn_g,
           ffn_w_gate, ffn_w_up, ffn_w_down, ev_w_in, ev_conv_w, ev_conv_b, ev_conv_ln_g, ev_conv_ln_b,
           ev_q_norm_g, ev_w_qb, ev_kv_norm_g, ev_w_kvb, ev_w_o, od_w_in, od_ln_g, od_ln_b, od_w_s, od_b_s,
           od_w_o, final_norm_g):
    f = lambda a: np.ascontiguousarray(np.asarray(a, dtype=np.float32))
    x_prompt, x_sample = f(x_prompt), f(x_sample)
    if "nc" not in _CACHE:
        _CACHE["nc"] = Builder().build()
    nc = _CACHE["nc"]
    par = np.zeros((128, NPAR), np.float32)

    def put(name, arr):
        arr = np.asarray(arr, np.float32)
        par[:, PCOL[name]:PCOL[name] + arr.shape[1]] = arr
    for l in range(2):
        put(f"nmg{l}", _fm(norm_mix_g[l])); put(f"nfg{l}", _fm(norm_ffn_g[l])); put(f"modb{l}", _fm(mod_b[l]))
    put("fng", _fm(final_norm_g)); put("convb", _fm(ev_conv_b[0])); put("clng", _fm(ev_conv_ln_g[0])); put("clnb", _fm(ev_conv_ln_b[0]))
    put("qng", _fm(ev_q_norm_g[0])); put("kvng", _fm(ev_kv_norm_g[0])); put("olng", _fm(od_ln_g[0])); put("olnb", _fm(od_ln_b[0]))
    cw = np.asarray(ev_conv_w, np.float32)[0, :, 0, :]
    put("convw", cw.T.reshape(8, 128, 31).transpose(1, 0, 2).reshape(128, 248))
    w_in = f(ev_w_in[0]); w_qb = f(ev_w_qb[0])
    w_in_sw = np.ascontiguousarray(w_in[:, 2816 + _PARTNER])
    w_qb_sw = np.ascontiguousarray(np.concatenate([w_qb[:, h * 192 + 128 + _PARTNER] for h in range(8)], axis=1))
    shared = {
        "params": par, "ident": np.eye(128, dtype=np.float32),
        "b_s": f(od_b_s[0]).reshape(1, 1024), "w_sT": np.ascontiguousarray(f(od_w_s[0]).transpose(2, 0, 1)),
        "ev_w_in": w_in, "ev_w_in_sw": w_in_sw, "ev_w_qb": w_qb, "ev_w_qb_sw": w_qb_sw,
        "ev_w_kvb": f(ev_w_kvb[0]), "ev_w_o": f(ev_w_o[0]), "od_w_in": f(od_w_in[0]), "od_w_o": f(od_w_o[0]),
        "ropek": _rope_tables(np.arange(4096)),
    }
    for l in range(2):
        shared[f"mod_w{l}"] = f(mod_w[l]); shared[f"wg{l}"] = f(ffn_w_gate[l]); shared[f"wu{l}"] = f(ffn_w_up[l]); shared[f"wd{l}"] = f(ffn_w_down[l])
    in_maps = []
    for i in range(8):
        b, q = i // 4, i % 4
        halo = np.zeros((32, D), np.float32)
        hm = np.zeros((128, 32), np.float32)
        if q > 0:
            halo[0:15] = x_sample[b, q * 1024 - 15:q * 1024]; hm[:, 0:15] = 1.0
        if q < 3:
            halo[15:30] = x_sample[b, (q + 1) * 1024:(q + 1) * 1024 + 15]; hm[:, 15:30] = 1.0
        condT = np.stack([_fm(c_ctx), _fm(c[b])], axis=-1)
        m = dict(shared)
        m.update({
            "xp": x_prompt[4 * i:4 * i + 4].reshape(T, D), "xs_own": np.ascontiguousarray(x_sample[b, q * 1024:(q + 1) * 1024]),
            "xs_all": x_sample[b], "xs_halo": halo, "hmask": hm, "cckv": f(cache_ckv[b, 0]), "ckpe": f(cache_kpe[b, 0]),
            "condT": np.ascontiguousarray(condT), "ropeq": _rope_tables(np.arange(q * 1024, (q + 1) * 1024)),
        })
        in_maps.append({"i_" + k_: v_ for k_, v_ in m.items()})
    res = run_bass_kernel_spmd(nc, in_maps, core_ids=list(range(8)))
    R = res.results
    y_prompt = np.stack([R[i]["o_yp"].reshape(4, SEQ_P, D) for i in range(8)], 0).reshape(32, SEQ_P, D)
    y_sample = np.stack([R[i]["o_ys"] for i in range(8)], 0).reshape(2, 4096, D)
    s_ckv = np.stack([R[i]["o_o_ckv"].reshape(4, 1, SEQ_P, 256) for i in range(8)], 0).reshape(32, 1, SEQ_P, 256)
    s_kpe = np.stack([R[i]["o_o_kpe"].reshape(4, 1, SEQ_P, 64) for i in range(8)], 0).reshape(32, 1, SEQ_P, 64)
    return (np.ascontiguousarray(y_prompt, np.float32), np.ascontiguousarray(y_sample, np.float32),
            np.ascontiguousarray(s_ckv, np.float32), np.ascontiguousarray(s_kpe, np.float32))
```

```python
import numpy as np
import concourse.bass as bass
import concourse.mybir as mybir
from concourse.bass_utils import run_bass_kernel_spmd

F32 = mybir.dt.float32
BF16 = mybir.dt.bfloat16
AF = mybir.ActivationFunctionType
ALU = mybir.AluOpType

D = 2048
NC16 = 16
T = 1024
DFF = 5632
EPS = 1e-6
SEQ_P = 256
NKS = 4352
SCALE = 192 ** -0.5
WSLOT = 8704
ARENA = 79104

PCOL = {}
_off = 0
for _n, _w in (("nmg0", 16), ("nmg1", 16), ("nfg0", 16), ("nfg1", 16), ("fng", 16),
               ("modb0", 96), ("modb1", 96), ("convb", 8), ("clng", 8), ("clnb", 8),
               ("qng", 4), ("kvng", 2), ("olng", 16), ("olnb", 16), ("convw", 248)):
    PCOL[_n] = _off
    _off += _w
NPAR = _off


class Tile:
    __slots__ = ("name", "w", "r", "excl")

    def __init__(self, name, w=None):
        self.name = name
        self.excl = False
        self.w = list(w) if w else []
        self.r = []


class Rec:
    def __init__(self):
        self.calls = []

    def __getattr__(self, name):
        def f(*a, **k):
            self.calls.append((name, a, k))
            return self
        return f


class Eng:
    def __init__(self, name):
        self.name = name
        self.items = []
        self.count = 0
        self.waited = {}


class Prog:
    ENG = ("pe", "act", "dve", "pool", "sp")

    def __init__(self, nc):
        self.nc = nc
        self.eng = {n: Eng(n) for n in self.ENG}
        self.sems = {}
        self.dma_val = {}
        self.out_tokens = []

    def sem(self, key):
        if key not in self.sems:
            self.sems[key] = self.nc.alloc_semaphore(name="s_" + key)
        return self.sems[key]

    def _deps(self, e, reads, writes):
        deps = {}
        own = "e_" + e.name
        for t in reads:
            for (k, v) in t.w:
                if deps.get(k, 0) < v:
                    deps[k] = v
            if t.excl:
                for (k, v) in t.r:
                    if k != own and deps.get(k, 0) < v:
                        deps[k] = v
        for t in writes:
            for (k, v) in t.w:
                if deps.get(k, 0) < v:
                    deps[k] = v
            for (k, v) in t.r:
                if deps.get(k, 0) < v:
                    deps[k] = v
        waits = []
        for k, v in deps.items():
            if k == "e_pe" and e.name == "pe":
                continue
            if e.waited.get(k, 0) >= v:
                continue
            e.waited[k] = v
            waits.append((k, v))
        return waits

    def _commit(self, tok, reads, writes):
        for t in reads:
            t.r = [x for x in t.r if x[0] != tok[0]] + [tok]
        for t in writes:
            t.w = [tok]
            t.r = []

    def op(self, en, fn, reads=(), writes=()):
        e = self.eng[en]
        waits = self._deps(e, reads, writes)
        e.count += 1
        tok = ("e_" + en, e.count)
        rec = Rec()
        fn(rec)
        assert rec.calls
        e.items.append(("op", waits, rec.calls))
        self._commit(tok, reads, writes)
        return tok

    def dma(self, qn, out_ap, in_ap, skey, reads=(), writes=(), is_output=False):
        e = self.eng[qn]
        waits = self._deps(e, reads, writes)
        k = "d_" + skey
        self.dma_val[k] = self.dma_val.get(k, 0) + 16
        tok = (k, self.dma_val[k])
        e.items.append(("dma", waits, (out_ap, in_ap, k)))
        self._commit(tok, reads, writes)
        if is_output:
            self.out_tokens.append(tok)
        return tok

    def dma_multi(self, qn, pairs, skey, reads=(), writes=()):
        e = self.eng[qn]
        waits = self._deps(e, reads, writes)
        k = "d_" + skey
        for i, (out_ap, in_ap) in enumerate(pairs):
            self.dma_val[k] = self.dma_val.get(k, 0) + 16
            e.items.append(("dma", waits if i == 0 else [], (out_ap, in_ap, k)))
        tok = (k, self.dma_val[k])
        self._commit(tok, reads, writes)
        return tok

    def fence_tokens(self):
        toks = [("e_" + n, self.eng[n].count) for n in ("pe", "act", "dve", "pool") if self.eng[n].count]
        for k, v in self.dma_val.items():
            if not k.startswith("d_w"):
                toks.append((k, v))
        return toks

    def emit(self):
        nc = self.nc
        fin = {}
        for (k, v) in self.out_tokens:
            fin[k] = max(fin.get(k, 0), v)
        sp = self.eng["sp"]
        sp.items.append(("wait", [(k, v) for k, v in fin.items()], None))
        for en in self.ENG:
            self.sem("e_" + en)
        for k in self.dma_val:
            self.sem(k)

        def run(en, h):
            e = self.eng[en]
            own = self.sems["e_" + en]
            for kind, waits, payload in e.items:
                for (k, v) in waits:
                    h.wait_ge(self.sems[k], v)
                if kind == "op":
                    for (mname, a, kw) in payload:
                        ins = getattr(h, mname)(*a, **kw)
                    ins.then_inc(own, 1)
                elif kind == "dma":
                    out_ap, in_ap, k = payload
                    h.dma_start(out=out_ap, in_=in_ap).then_inc(self.sems[k], 16)

        with nc.Block() as block:
            @block.tensor
            def _(h):
                run("pe", h)

            @block.scalar
            def _(h):
                run("act", h)

            @block.vector
            def _(h):
                run("dve", h)

            @block.gpsimd
            def _(h):
                run("pool", h)

            @block.sync
            def _(h):
                run("sp", h)


class Builder:
    def __init__(self):
        nc = self.nc = bass.Bass("TRN2", target_bir_lowering=False)
        self.P = Prog(nc)
        self.dram = {}
        self.tiles = {}
        self.arena_tiles = set()
        self.fence_toks = []
        self.xT = nc.alloc_sbuf_tensor("xT", [128, 16, T], F32)
        self.wslot = [nc.alloc_sbuf_tensor(f"wslot{i}", [128, WSLOT], BF16) for i in range(3)]
        self.ring = [nc.alloc_sbuf_tensor(f"ring{i}", [128, 512], F32) for i in range(3)]
        self.rstd = [nc.alloc_sbuf_tensor(f"rstd{i}", [128, 512], F32) for i in range(2)]
        self.par = nc.alloc_sbuf_tensor("par", [128, NPAR], F32)
        self.mods = [nc.alloc_sbuf_tensor(f"mods{l}", [128, 2, 96], F32) for l in range(2)]
        self.acoef = nc.alloc_sbuf_tensor("acoef", [128, 8, 16], F32)
        self.identf = nc.alloc_sbuf_tensor("identf", [128, 128], F32)
        self.identb = nc.alloc_sbuf_tensor("identb", [128, 128], BF16)
        self.ones = nc.alloc_sbuf_tensor("ones", [128, 128], BF16)
        self.condT = nc.alloc_sbuf_tensor("condT", [128, 16, 2], F32)
        self.scT = nc.alloc_sbuf_tensor("scT", [128, 16, 2], BF16)
        self.hmask = nc.alloc_sbuf_tensor("hmask", [128, 32], F32)
        self.arena = nc.alloc_sbuf_tensor("arena", [128, ARENA // 2], BF16)
        self.psum = nc.alloc_psum_tensor("psum", [128, 8, 512], F32)
        self.wn = 0
        self.rn = 0
        self.bn = 0
        self.nbanks = 4

    def din(self, name, shape):
        t = self.nc.dram_tensor("i_" + name, list(shape), F32, kind="ExternalInput")
        self.dram[name] = t.ap()
        return self.dram[name]

    def dout(self, name, shape):
        t = self.nc.dram_tensor("o_" + name, list(shape), F32, kind="ExternalOutput")
        self.dram[name] = t.ap()
        return self.dram[name]

    def Tl(self, *key, arena=False):
        t = self.tiles.get(key)
        if t is None:
            t = Tile(key, self.fence_toks if arena else None)
            t.excl = key[0] == "ps"
            self.tiles[key] = t
            if arena:
                self.arena_tiles.add(key)
        return t

    def A(self, *key):
        return self.Tl(*key, arena=True)

    def fence(self):
        self.fence_toks = self.P.fence_tokens()
        for k in self.arena_tiles:
            del self.tiles[k]
        self.arena_tiles = set()

    def av(self, off, dt, n):
        assert off % 32 == 0 and off + n * (4 if dt == F32 else 2) <= ARENA, (off, n)
        a = self.arena[:, off // 2: off // 2 + n * (2 if dt == F32 else 1)]
        return a.bitcast(F32) if dt == F32 else a

    def pc(self, name, c=0, n=1):
        o = PCOL[name] + c
        return self.par[:, o:o + n]

    def bank(self):
        b = self.bn
        self.bn = (self.bn + 1) % self.nbanks
        return b, self.Tl("ps", b)

    def ringbuf(self):
        r = self.rn
        self.rn = (self.rn + 1) % 3
        return self.ring[r], self.Tl("ring", r)

    def wload(self, parts, tot):
        s = self.wn
        self.wn = (self.wn + 1) % 3
        nk = parts[0][1]
        view = self.wslot[s][:, 0:nk * tot].rearrange("p (k f) -> p k f", k=nk)
        tw = self.Tl("w", s)
        self.P.dma_multi("pool", [(view[:, :, co:co + ncol], src) for (co, nk_, ncol, src) in parts], f"w{s}", writes=[tw])
        return view, tw

    def mm(self, out_ap, pairs, reads, wt, first=True, last=True):
        def fn(h):
            n = len(pairs)
            for i, (l, r) in enumerate(pairs):
                ins = h.matmul(out_ap, l, r, start=(first and i == 0), stop=(last and i == n - 1))
            return ins
        self.P.op("pe", fn, reads=reads, writes=[wt])

    def mm1(self, out_ap, l, r, start, stop, reads, wt):
        self.P.op("pe", lambda h: h.matmul(out_ap, l, r, start=start, stop=stop), reads=list(reads) + [self.Tl("ones")], writes=[wt])

    def wsrc(self, name, r0, nk, c0, ncol):
        w = self.dram[name]
        return w[r0:r0 + nk * 128, c0:c0 + ncol].rearrange("(k p) f -> p k f", p=128)

    def make_rstd(self, dst_ap, dst_t, bank_ap, bank_t, inv_n):
        P = self.P
        P.op("act", lambda h: h.activation(dst_ap, bank_ap, AF.Sqrt, scale=inv_n, bias=EPS), reads=[bank_t], writes=[dst_t])
        P.op("dve", lambda h: h.reciprocal(dst_ap, dst_ap), reads=[dst_t], writes=[dst_t])

    def norm_mod(self, xsrc, xt, hdst, ht, ntiles, aidx, sh_ap, rd_extra=(), rs_idx=None):
        P = self.P
        SB = 5
        for ti, (t0, n) in enumerate(ntiles):
            bt = self.Tl("ps", SB)
            bap = self.psum[:, SB, 0:n]
            for c in range(16):
                rb, rt = self.ringbuf()
                sq = rb.bitcast(BF16)[:, 0:n]
                P.op("act", lambda h, sq=sq, c=c: h.activation(sq, xsrc(c, t0, n), AF.Square), reads=[xt(c, ti)], writes=[rt])
                self.mm1(bap, self.ones[:], sq, c == 0, c == 15, [rt], bt)
            ri = (ti % 2) if rs_idx is None else rs_idx
            rs = self.rstd[ri][:, 0:n]
            rst = self.Tl("rstd", ri)
            self.make_rstd(rs, rst, bap, bt, 1.0 / D)
            for c in range(16):
                rb, rt = self.ringbuf()
                tmp = rb[:, 0:n]
                P.op("dve", lambda h, tmp=tmp, c=c: h.tensor_tensor(tmp, xsrc(c, t0, n), rs, ALU.mult), reads=[xt(c, ti), rst], writes=[rt])
                P.op("act", lambda h, tmp=tmp, c=c: h.activation(hdst(c, t0, n), tmp, AF.Identity,
                                                                  scale=self.acoef[:, aidx, c:c + 1], bias=sh_ap(c)),
                     reads=[rt, self.Tl("acoef")] + list(rd_extra), writes=[ht(c, ti)])

    def load_T(self, src_rows, n, stg_ap, stg_t, dst, dst_t, skey, ncols=2048, dst_grp=None):
        P = self.P
        P.dma("sp", stg_ap[0:n, :], src_rows, skey, writes=[stg_t])
        nch = (ncols + 127) // 128
        for g in range(0, nch, 4):
            b, bt = self.bank()
            cs = list(range(g, min(g + 4, nch)))

            def tr(h, cs=cs, b=b):
                for j, c in enumerate(cs):
                    w = min(128, ncols - c * 128)
                    ins = h.transpose(self.psum[0:w, b, j * 128:j * 128 + n], stg_ap[0:n, c * 128:c * 128 + w], self.identf[0:n, 0:n])
                return ins
            P.op("pe", tr, reads=[stg_t, self.Tl("ident")], writes=[bt])
            if dst_grp is not None:
                ncs = len(cs)
                src = self.psum[:, b, 0:ncs * 128].rearrange("p (j t) -> p j t", j=ncs)[:, :, 0:n]
                wts = [dst_t(c) for c in cs]
                if (g // 4) % 2 == 0:
                    P.op("act", lambda h, src=src, g=g: h.activation(dst_grp(g), src, AF.Copy), reads=[bt], writes=wts)
                else:
                    P.op("dve", lambda h, src=src, g=g: h.tensor_copy(dst_grp(g), src), reads=[bt], writes=wts)
                continue
            for j, c in enumerate(cs):
                w = min(128, ncols - c * 128)
                eng = "act" if ((g // 4) % 2 == 0) else "dve"
                src = self.psum[0:w, b, j * 128:j * 128 + n]
                if eng == "act":
                    P.op("act", lambda h, c=c, src=src: h.activation(dst(c), src, AF.Copy), reads=[bt], writes=[dst_t(c)])
                else:
                    P.op("dve", lambda h, c=c, src=src: h.tensor_copy(dst(c), src), reads=[bt], writes=[dst_t(c)])

    def build(self):
        nc, P = self.nc, self.P
        din = self.din
        xp = din("xp", [T, D]); xs_own = din("xs_own", [T, D]); xs_all = din("xs_all", [4096, D])
        xs_halo = din("xs_halo", [32, D]); hmask_d = din("hmask", [128, 32])
        cckv = din("cckv", [256, 256]); ckpe = din("ckpe", [256, 64])
        condT_d = din("condT", [128, 16, 2]); par_d = din("params", [128, NPAR])
        ropeq = din("ropeq", [64, 2, T]); ropek = din("ropek", [64, 2, 4096])
        ident_d = din("ident", [128, 128])
        bs_d = din("b_s", [1, 1024]); wsT_d = din("w_sT", [128, 8, 128])
        for l in range(2):
            din(f"mod_w{l}", [D, 6 * D]); din(f"wg{l}", [D, DFF]); din(f"wu{l}", [D, DFF]); din(f"wd{l}", [DFF, D])
        din("ev_w_in", [D, 2880]); din("ev_w_in_sw", [D, 64]); din("ev_w_qb", [512, 1536]); din("ev_w_qb_sw", [512, 512])
        din("ev_w_kvb", [256, 2048]); din("ev_w_o", [D, D]); din("od_w_in", [D, 4096]); din("od_w_o", [D, D])
        yp = self.dout("yp", [T, D]); ys = self.dout("ys", [T, D])
        o_ckv = self.dout("o_ckv", [T, 256]); o_kpe = self.dout("o_kpe", [T, 64])

        t_par, t_id, t_cond = self.Tl("par"), self.Tl("ident"), self.Tl("cond")
        P.dma_multi("sp", [(self.par[:], par_d), (self.identf[:], ident_d), (self.condT[:], condT_d), (self.hmask[:], hmask_d)],
                    "setup", writes=[t_par, t_id, t_cond, self.Tl("hmask")])
        P.op("dve", lambda h: h.tensor_copy(self.identb[:], self.identf[:]), reads=[t_id], writes=[self.Tl("identb")])
        P.op("dve", lambda h: h.memset(self.ones[:], 1.0), writes=[self.Tl("ones")])
        P.op("act", lambda h: h.activation(self.scT[:], self.condT[:], AF.Silu), reads=[t_cond], writes=[self.Tl("scT")])
        self.tiles[("ones",)] = self.Tl("ones")

        self.mods_done = {}
        self.mod_pending = [(0, b_) for b_ in range(8, 24)] + [(1, b_) for b_ in range(24)]
        self.tick_n = 0
        self.tick_every = 1
        import os
        kskip = os.environ.get("KSKIP", "")
        import os
        kpass = os.environ.get("KPASS", "PS")
        kstage = int(os.environ.get("KSTAGE", "9"))
        for ps_ in ("P", "S"):
            if ps_ not in kpass:
                continue
            self.pass_ = ps_
            cond = 0 if ps_ == "P" else 1
            self.fence()
            if "l" not in kskip and not (ps_ == "S" and getattr(self, "preloaded", False)):
                self.load_x(xp if ps_ == "P" else xs_own)
            if "m" not in kskip:
                self.compute_mods(0, 0, 8)
            for l in range(2):
                if kstage >= 1 + 2 * l:
                    if l == 0:
                        self.even_mixer(cond)
                    else:
                        self.odd_mixer(cond)
                if kstage >= 2 + 2 * l:
                    self.compute_mods(l, 12, 24)
                    self.ffn(l, cond)
            if "f" not in kskip:
                nxt = xs_own if (ps_ == "P" and "S" in kpass and "l" not in kskip) else None
                self.final(yp if ps_ == "P" else ys, next_src=nxt)
                self.preloaded = nxt is not None
        P.emit()
        return nc

    def compute_mods(self, l, b0, b1):
        P = self.P
        for blk in range(b0, b1):
            if (l, blk) in self.mods_done:
                continue
            self.mods_done[(l, blk)] = True
            view, tw = self.wload([(0, 16, 512, self.wsrc(f"mod_w{l}", 0, 16, blk * 512, 512))], 512)
            b, bt = self.bank()

            def fn(h, view=view, b=b):
                for f in range(4):
                    for k in range(16):
                        ins = h.matmul(self.psum[:, b, f * 2:f * 2 + 2], view[:, k, f * 128:(f + 1) * 128], self.scT[:, k, :],
                                       start=(k == 0), stop=(k == 15))
                return ins
            P.op("pe", fn, reads=[tw, self.Tl("scT")], writes=[bt])
            tm = self.Tl("mods", l, blk)
            for j in range(2):
                src = self.psum[:, b, 0:8].rearrange("p (f j) -> p f j", j=2)[:, :, j]
                P.op("dve", lambda h, src=src, j=j, blk=blk: h.tensor_tensor(self.mods[l][:, j, blk * 4:blk * 4 + 4], src,
                                                                              self.pc(f"modb{l}", blk * 4, 4), ALU.add),
                     reads=[bt, self.Tl("par")], writes=[tm])
            which, r = divmod(blk, 4)
            if r == 3 and which in (1, 4):
                sub = 0 if which == 1 else 1
                gname = ("nmg" if sub == 0 else "nfg") + str(l)
                rd = [self.Tl("mods", l, which * 4 + i) for i in range(4)] + [self.Tl("par")]
                for j in range(2):
                    P.op("dve", lambda h, j=j, which=which, sub=sub, gname=gname:
                         h.scalar_tensor_tensor(self.acoef[:, l * 4 + sub * 2 + j, :], self.mods[l][:, j, which * 16:(which + 1) * 16], 1.0,
                                                self.pc(gname, 0, 16), ALU.add, ALU.mult),
                         reads=rd, writes=[self.Tl("acoef")])

    def tick(self):
        self.tick_n += 1
        if self.tick_n % self.tick_every:
            return
        while self.mod_pending:
            l, blk = self.mod_pending.pop(0)
            if (l, blk) not in self.mods_done:
                self.compute_mods(l, blk, blk + 1)
                return

    def modcol(self, l, cond, which, c):
        col = which * 16 + c
        return self.mods[l][:, cond, col:col + 1]

    def modtile(self, l, which):
        return [self.Tl("mods", l, which * 4 + i) for i in range(4)]

    def load_x(self, src, chunks=range(8)):
        for n in chunks:
            stg = self.av(32768 + (n % 2) * 8192, F32, 2048)
            st = self.A("stg", n % 2)
            self.load_T(src[n * 128:(n + 1) * 128, :], 128, stg, st,
                        lambda c, n=n: self.xT[:, c, n * 128:(n + 1) * 128],
                        lambda c, n=n: self.Tl("x", c, n // 4), f"stg{n % 2}",
                        dst_grp=lambda g, n=n: self.xT[:, g:g + 4, n * 128:(n + 1) * 128])

    def x_ap(self, c, t0, n):
        return self.xT[:, c, t0:t0 + n]

    def hT_view(self):
        return self.av(0, BF16, 16 * T).rearrange("p (c t) -> p c t", c=16)

    def resid(self, bank_ap, bt, c, half, gate_ap, gate_tiles):
        xs = self.xT[:, c, half * 512:(half + 1) * 512]
        xt = self.Tl("x", c, half)
        self.P.op("dve", lambda h: h.scalar_tensor_tensor(xs, bank_ap, gate_ap, xs, ALU.mult, ALU.add),
                  reads=[bt, xt] + gate_tiles, writes=[xt])

    def ffn(self, l, cond):
        P = self.P
        self.fence()
        hT = self.hT_view()
        self.norm_mod(self.x_ap, lambda c, ti: self.Tl("x", c, ti),
                      lambda c, t0, n: hT[:, c, t0:t0 + n], lambda c, ti: self.A("h", c, ti),
                      [(0, 512), (512, 512)], l * 4 + 2 + cond, lambda c: self.modcol(l, cond, 3, c), rd_extra=self.modtile(l, 3))
        self.tick_every = 2
        act = self.av(32768, BF16, 11 * T).rearrange("p (j t) -> p j t", j=11)
        gate_tiles = self.modtile(l, 5)
        for G in range(4):
            j0 = G * 11
            jj = 0
            while jj < 11:
                nj = min(2, 11 - jj)
                ncol = nj * 128
                c0 = (j0 + jj) * 128
                view, tw = self.wload([(0, 16, ncol, self.wsrc(f"wg{l}", 0, 16, c0, ncol)),
                                       (ncol, 16, ncol, self.wsrc(f"wu{l}", 0, 16, c0, ncol))], 2 * ncol)
                for half in range(2):
                    for q in range(nj):
                        hs = [hT[:, k, half * 512:(half + 1) * 512] for k in range(16)]
                        hts = [self.A("h", k, half) for k in range(16)]
                        bg, btg = self.bank()
                        self.mm(self.psum[:, bg, :], [(view[:, k, q * 128:(q + 1) * 128], hs[k]) for k in range(16)], [tw] + hts, btg)
                        bu, btu = self.bank()
                        self.mm(self.psum[:, bu, :], [(view[:, k, ncol + q * 128:ncol + (q + 1) * 128], hs[k]) for k in range(16)], [tw] + hts, btu)
                        rb, rt = self.ringbuf()
                        sg = rb[:, :]
                        P.op("act", lambda h, sg=sg, bg=bg: h.activation(sg, self.psum[:, bg, :], AF.Silu), reads=[btg], writes=[rt])
                        ta = self.A("act", jj + q, half)
                        P.op("dve", lambda h, sg=sg, bu=bu, jq=jj + q, half=half:
                             h.tensor_tensor(act[:, jq, half * 512:(half + 1) * 512], sg, self.psum[:, bu, :], ALU.mult),
                             reads=[rt, btu], writes=[ta])
                jj += nj
                self.tick()
            for cb in range(4):
                view, tw = self.wload([(0, 11, 512, self.wsrc(f"wd{l}", j0 * 128, 11, cb * 512, 512))], 512)
                for cc in range(4):
                    c = cb * 4 + cc
                    for half in range(2):
                        b, bt = self.bank()
                        self.mm(self.psum[:, b, :], [(view[:, j, cc * 128:(cc + 1) * 128], act[:, j, half * 512:(half + 1) * 512]) for j in range(11)],
                                [tw] + [self.A("act", j, half) for j in range(11)], bt)
                        self.resid(self.psum[:, b, :], bt, c, half, self.modcol(l, cond, 5, c), gate_tiles)
                self.tick()

    def final(self, out, next_src=None):
        P = self.P
        self.fence()
        SB = 5
        for half in range(2):
            bt = self.Tl("ps", SB)
            bap = self.psum[:, SB, :]
            for c in range(16):
                rb, rt = self.ringbuf()
                sq = rb.bitcast(BF16)[:, 0:512]
                P.op("act", lambda h, sq=sq, c=c: h.activation(sq, self.xT[:, c, half * 512:(half + 1) * 512], AF.Square),
                     reads=[self.Tl("x", c, half)], writes=[rt])
                self.mm1(bap, self.ones[:], sq, c == 0, c == 15, [rt], bt)
            rs = self.rstd[half][:, :]
            rst = self.Tl("rstd", half)
            self.make_rstd(rs, rst, bap, bt, 1.0 / D)
            for c in range(16):
                xs = self.xT[:, c, half * 512:(half + 1) * 512]
                P.op("dve", lambda h, xs=xs, c=c: h.scalar_tensor_tensor(xs, xs, self.pc("fng", c), rs, ALU.mult, ALU.mult),
                     reads=[rst, self.Tl("par"), self.Tl("x", c, half)], writes=[self.Tl("x", c, half)])
            for n4 in range(4):
                n = half * 4 + n4
                stg = self.av((n % 2) * 8192, F32, 2048)
                st = self.A("ostg", n % 2)
                for g in range(4):
                    b, bt2 = self.bank()

                    def tr(h, g=g, b=b, n=n):
                        for j in range(4):
                            c = g * 4 + j
                            ins = h.transpose(self.psum[:, b, j * 128:(j + 1) * 128], self.xT[:, c, n * 128:(n + 1) * 128], self.identf[:])
                        return ins
                    P.op("pe", tr, reads=[self.Tl("x", g * 4 + j, half) for j in range(4)] + [self.Tl("ident")], writes=[bt2])
                    dst = stg[:, g * 512:(g + 1) * 512]
                    if g % 2 == 0:
                        P.op("act", lambda h, dst=dst, b=b: h.activation(dst, self.psum[:, b, :], AF.Copy), reads=[bt2], writes=[st])
                    else:
                        P.op("dve", lambda h, dst=dst, b=b: h.tensor_copy(dst, self.psum[:, b, :]), reads=[bt2], writes=[st])
                P.dma("sp", out[n * 128:(n + 1) * 128, :], stg, f"ostg{n % 2}", reads=[st], is_output=True)
            if next_src is not None:
                self.load_x(next_src, range(half * 4, half * 4 + 4))

    def odd_mixer(self, cond):
        P = self.P
        l = 1
        self.compute_mods(1, 0, 12)
        self.fence()
        hT = self.hT_view()
        self.norm_mod(self.x_ap, lambda c, ti: self.Tl("x", c, ti),
                      lambda c, t0, n: hT[:, c, t0:t0 + n], lambda c, ti: self.A("h", c, ti),
                      [(0, 512), (512, 512)], l * 4 + 0 + cond, lambda c: self.modcol(l, cond, 0, c), rd_extra=self.modtile(l, 0))
        self.tick_every = 1
        uT = self.av(32768, BF16, 16 * 512).rearrange("p (c t) -> p c t", c=16)
        vg = self.av(49152, BF16, 4 * 2048).rearrange("p (n f) -> p n f", n=4)
        bsr = self.av(49152, F32, 1024)
        T2 = self.av(65536, F32, 16 * 128).rearrange("p (c t) -> p c t", c=16)
        wsT = self.av(73728, BF16, 1024).rearrange("p (g t) -> p g t", g=8)
        wsTf = self.av(32768, F32, 1024).rearrange("p (g t) -> p g t", g=8)
        st = self.av(75776, F32, 64)
        t_ws, t_wsf, t_bsr, t_T2 = self.A("wsT"), self.A("wsTf"), self.A("bsr"), self.A("T2")
        P.dma("sp", wsTf, self.dram["w_sT"], "osetup1", writes=[t_wsf])
        P.dma("sp", bsr, self.dram["b_s"].partition_broadcast(128), "osetup2", writes=[t_bsr])
        P.op("dve", lambda h: h.tensor_copy(wsT, wsTf), reads=[t_wsf], writes=[t_ws])
        for g in range(8):
            b, bt = self.bank()
            self.mm(self.psum[:, b, 0:128], [(self.ones[:], wsT[:, g, :])], [t_ws], bt)
            for cc in range(2):
                c = g * 2 + cc
                P.op("dve", lambda h, c=c, b=b, g=g: h.scalar_tensor_tensor(T2[:, c, :], self.psum[:, b, 0:128], self.pc("olnb", c),
                                                                         bsr[:, g * 128:(g + 1) * 128], ALU.mult, ALU.add),
                     reads=[bt, t_bsr, self.Tl("par")], writes=[t_T2])
        gate_tiles = self.modtile(l, 2)
        self.fence()
        t_ws, t_T2 = self.A("wsT"), self.A("T2")
        def vpart(half):
            hs = [hT[:, k, half * 512:(half + 1) * 512] for k in range(16)]
            hts = [self.A("h", k, half) for k in range(16)]
            t_st = self.A("ost")
            P.op("dve", lambda h: h.memset(st, 0.0), writes=[t_st])
            for cb in range(4):
                self.tick()
                view, tw = self.wload([(0, 16, 512, self.wsrc("od_w_in", 0, 16, 2048 + cb * 512, 512))], 512)
                for n in range(4):
                    tk = half * 512 + n * 128
                    b, bt = self.bank()
                    self.mm(self.psum[:, b, :], [(hT[:, k, tk:tk + 128], view[:, k, :]) for k in range(16)], [tw] + hts, bt)
                    tv = self.A("vg", n, cb)
                    dstv = vg[:, n, cb * 512:(cb + 1) * 512]
                    P.op("act", lambda h, dstv=dstv, b=b, n=n, cb=cb: h.activation(dstv, self.psum[:, b, :], AF.Gelu_apprx_tanh,
                                                                                accum_out=st[:, n * 4 + cb:n * 4 + cb + 1]),
                         reads=[bt], writes=[tv, t_st])
                    rb, rt = self.ringbuf()
                    P.op("act", lambda h, dstv=dstv, rb=rb, n=n, cb=cb: h.activation(rb.bitcast(BF16)[:, 0:512], dstv, AF.Square,
                                                                                  accum_out=st[:, 16 + n * 4 + cb:16 + n * 4 + cb + 1]),
                         reads=[tv], writes=[rt, t_st])
                    yield
            s1 = st[:, 0:16].rearrange("p (n c) -> p n c", c=4)
            s2 = st[:, 16:32].rearrange("p (n c) -> p n c", c=4)
            mean, ex2, var, rsd, nmr = (st[:, 32 + 4 * i:36 + 4 * i] for i in range(5))

            P.op("dve", lambda h: h.tensor_reduce(mean, s1, mybir.AxisListType.X, ALU.add), reads=[t_st], writes=[t_st])
            P.op("dve", lambda h: h.tensor_reduce(ex2, s2, mybir.AxisListType.X, ALU.add), reads=[t_st], writes=[t_st])
            P.op("dve", lambda h: h.tensor_scalar(mean, mean, 1.0 / 2048, None, ALU.mult), reads=[t_st], writes=[t_st])
            P.op("dve", lambda h: h.tensor_tensor(var, mean, mean, ALU.mult), reads=[t_st], writes=[t_st])
            P.op("dve", lambda h: h.scalar_tensor_tensor(var, ex2, 1.0 / 2048, var, ALU.mult, ALU.subtract), reads=[t_st], writes=[t_st])
            P.op("act", lambda h: h.activation(rsd, var, AF.Sqrt, scale=1.0, bias=EPS), reads=[t_st], writes=[t_st])
            P.op("dve", lambda h: h.reciprocal(rsd, rsd), reads=[t_st], writes=[t_st])
            P.op("dve", lambda h: h.scalar_tensor_tensor(nmr, mean, -1.0, rsd, ALU.mult, ALU.mult), reads=[t_st], writes=[t_st])
            for n in range(4):
                tvs = [self.A("vg", n, cb) for cb in range(4)]
                P.op("act", lambda h, n=n: h.activation(vg[:, n, :], vg[:, n, :], AF.Identity, scale=rsd[:, n:n + 1], bias=nmr[:, n:n + 1]),
                     reads=tvs + [t_st], writes=tvs)
        def upart(half):
            hs = [hT[:, k, half * 512:(half + 1) * 512] for k in range(16)]
            hts = [self.A("h", k, half) for k in range(16)]
            for cb in range(4):
                self.tick()
                view, tw = self.wload([(0, 16, 512, self.wsrc("od_w_in", 0, 16, cb * 512, 512))], 512)
                for cc in range(4):
                    c = cb * 4 + cc
                    b, bt = self.bank()
                    self.mm(self.psum[:, b, :], [(view[:, k, cc * 128:(cc + 1) * 128], hs[k]) for k in range(16)], [tw] + hts, bt)
                    P.op("act", lambda h, c=c, b=b: h.activation(uT[:, c, :], self.psum[:, b, :], AF.Gelu_apprx_tanh),
                         reads=[bt], writes=[self.A("u", c)])
        def sppart(half):
            for c4 in range(4):
                for n in range(4):
                    tvs = [self.A("vg", n, c4)]
                    b, bt = self.bank()

                    def sp(h, n=n, c4=c4, b=b):
                        for j in range(4):
                            c = c4 * 4 + j
                            ins = h.matmul(self.psum[:, b, j * 128:(j + 1) * 128], vg[:, n, c * 128:(c + 1) * 128], wsT[:, c // 2, :],
                                           start=True, stop=True)
                        return ins
                    P.op("pe", sp, reads=tvs + [t_ws], writes=[bt])
                    for j in range(4):
                        c = c4 * 4 + j
                        rb, rt = self.ringbuf()
                        tmp = rb[:, 0:128]
                        P.op("dve", lambda h, tmp=tmp, b=b, j=j, c=c: h.scalar_tensor_tensor(tmp, self.psum[:, b, j * 128:(j + 1) * 128], self.pc("olng", c),
                                                                                          T2[:, c, :], ALU.mult, ALU.add),
                             reads=[bt, t_T2, self.Tl("par")], writes=[rt])
                        us = uT[:, c, n * 128:(n + 1) * 128]
                        P.op("dve", lambda h, tmp=tmp, us=us: h.tensor_tensor(us, us, tmp, ALU.mult),
                             reads=[rt, self.A("u", c)], writes=[self.A("u", c)])
                    yield
        def wopart(half):
            for cb in range(4):
                self.tick()
                view, tw = self.wload([(0, 16, 512, self.wsrc("od_w_o", 0, 16, cb * 512, 512))], 512)
                for cc in range(4):
                    c = cb * 4 + cc
                    b, bt = self.bank()
                    self.mm(self.psum[:, b, :], [(view[:, k, cc * 128:(cc + 1) * 128], uT[:, k, :]) for k in range(16)],
                            [tw] + [self.A("u", k) for k in range(16)], bt)
                    self.resid(self.psum[:, b, :], bt, c, half, self.modcol(l, cond, 2, c), gate_tiles)
        def drain(*gens):
            gens = list(gens)
            while gens:
                for g_ in list(gens):
                    try:
                        next(g_)
                    except StopIteration:
                        gens.remove(g_)
        drain(vpart(0))
        upart(0)
        drain(sppart(0), vpart(1))
        wopart(0)
        upart(1)
        drain(sppart(1))
        wopart(1)

    def even_mixer(self, cond):
        P = self.P
        l = 0
        isS = self.pass_ == "S"
        self.fence()
        if isS:
            xh = self.av(71168, F32, 16 * 32).rearrange("p (c t) -> p c t", c=16)
            hh = self.av(73216, BF16, 16 * 32).rearrange("p (c t) -> p c t", c=16)
            stg = self.av(0, F32, 2048)
            self.load_T(self.dram["xs_halo"], 32, stg, self.A("hstg"), lambda c: xh[:, c, :], lambda c: self.A("xh", c), "hstg",
                        dst_grp=lambda g: xh[:, g:g + 4, :])
            self.norm_mod(lambda c, t0, n: xh[:, c, :], lambda c, ti: self.A("xh", c),
                          lambda c, t0, n: hh[:, c, :], lambda c, ti: self.A("hh", c), [(0, 32)], cond,
                          lambda c: self.modcol(l, cond, 0, c), rd_extra=self.modtile(l, 0))
            self.fence()
        hT = self.hT_view()
        self.norm_mod(self.x_ap, lambda c, ti: self.Tl("x", c, ti),
                      lambda c, t0, n: hT[:, c, t0:t0 + n], lambda c, ti: self.A("h", c, ti),
                      [(0, 512), (512, 512)], l * 4 + 0 + cond, lambda c: self.modcol(l, cond, 0, c), rd_extra=self.modtile(l, 0))
        self.tick_every = 1
        NK = NKS if isS else T
        if isS:
            ckv_all = self.av(32768, BF16, 2 * NKS).rearrange("p (c t) -> p c t", c=2)
            kpe_all = self.av(50176, BF16, NKS)
        else:
            ckv_all = self.av(32768, BF16, 2 * T).rearrange("p (c t) -> p c t", c=2)
            kpe_all = self.av(36864, BF16, T)
        qg = self.av(58880, BF16, 4 * T).rearrange("p (c t) -> p c t", c=4)
        rstd_q = self.av(67072, F32, T)
        if not isS:
            self.kside(lambda k, t0, n: hT[:, k, t0:t0 + n], lambda k, ti: self.A("h", k, ti), [(0, 512), (512, 512)],
                       ckv_all, kpe_all, 0, cond, state_out=True)
        if isS:
            PADW = 1054
            a_pad = self.av(38912, BF16, 8 * PADW).rearrange("p (c t) -> p c t", c=8)
        else:
            PADW = 4 * 286
            a_pad = self.av(38912, BF16, 8 * PADW).rearrange("p (c t) -> p c t", c=8)
            ap4 = self.av(38912, BF16, 8 * PADW).rearrange("p (c s t) -> p c s t", c=8, s=4)
            P.op("dve", lambda h: h.memset(a_pad[:, :, :], 0.0), writes=[self.A("apad", c) for c in range(8)])
        for cp in range(4):
            view, tw = self.wload([(0, 16, 256, self.wsrc("ev_w_in", 0, 16, cp * 256, 256)),
                                   (256, 16, 256, self.wsrc("ev_w_in", 0, 16, 1024 + cp * 256, 256))], 512)
            for q in range(2):
                c = cp * 2 + q
                tiles_ = [(0, 512, 0), (512, 512, 1)] + ([(0, 32, 2)] if isS else [])
                for (t0, n, kind) in tiles_:
                    if kind == 2:
                        hs = [hh[:, k, :] for k in range(16)]
                        hts = [self.A("hh", k) for k in range(16)]
                    else:
                        hs = [hT[:, k, t0:t0 + n] for k in range(16)]
                        hts = [self.A("h", k, kind) for k in range(16)]
                    bv, btv = self.bank()
                    self.mm(self.psum[:, bv, 0:n], [(view[:, k, q * 128:(q + 1) * 128], hs[k]) for k in range(16)], [tw] + hts, btv)
                    bg, btg = self.bank()
                    self.mm(self.psum[:, bg, 0:n], [(view[:, k, 256 + q * 128:256 + (q + 1) * 128], hs[k]) for k in range(16)], [tw] + hts, btg)
                    rb, rt = self.ringbuf()
                    sg = rb[:, 0:n]
                    P.op("act", lambda h, sg=sg, bg=bg, n=n: h.activation(sg, self.psum[:, bg, 0:n], AF.Sigmoid), reads=[btg], writes=[rt])
                    ta = self.A("apad", c)
                    if kind == 2:
                        P.op("dve", lambda h, sg=sg: h.tensor_tensor(sg, sg, self.hmask[:, :], ALU.mult), reads=[rt, self.Tl("hmask")], writes=[rt])
                        P.op("dve", lambda h, sg=sg, bv=bv, c=c: h.tensor_tensor(a_pad[:, c, 0:15], sg[:, 0:15], self.psum[:, bv, 0:15], ALU.mult),
                             reads=[rt, btv], writes=[ta])
                        P.op("dve", lambda h, sg=sg, bv=bv, c=c: h.tensor_tensor(a_pad[:, c, 1039:1054], sg[:, 15:30], self.psum[:, bv, 15:30], ALU.mult),
                             reads=[rt, btv], writes=[ta])
                    elif isS:
                        P.op("dve", lambda h, sg=sg, bv=bv, c=c, t0=t0: h.tensor_tensor(a_pad[:, c, 15 + t0:15 + t0 + 512], sg, self.psum[:, bv, :], ALU.mult),
                             reads=[rt, btv], writes=[ta])
                    else:
                        s0 = t0 // 256
                        P.op("dve", lambda h, sg=sg, bv=bv, c=c, s0=s0: h.tensor_tensor(ap4[:, c, s0:s0 + 2, 15:271],
                                                                                     sg.rearrange("p (s t) -> p s t", s=2),
                                                                                     self.psum[:, bv, :].rearrange("p (s t) -> p s t", s=2), ALU.mult),
                             reads=[rt, btv], writes=[ta])
            self.tick()
        view, tw = self.wload([(0, 16, 512, self.wsrc("ev_w_in", 0, 16, 2048, 512))], 512)
        SB = 5
        for half in range(2):
            hs = [hT[:, k, half * 512:(half + 1) * 512] for k in range(16)]
            hts = [self.A("h", k, half) for k in range(16)]
            sbt = self.Tl("ps", SB)
            for c in range(4):
                b, bt = self.bank()
                self.mm(self.psum[:, b, :], [(view[:, k, c * 128:(c + 1) * 128], hs[k]) for k in range(16)], [tw] + hts, bt)
                rb, rt = self.ringbuf()
                sq = rb.bitcast(BF16)[:, 0:512]
                P.op("act", lambda h, sq=sq, b=b: h.activation(sq, self.psum[:, b, :], AF.Square), reads=[bt], writes=[rt])
                self.mm1(self.psum[:, SB, :], self.ones[:], sq, c == 0, c == 3, [rt], sbt)
                P.op("dve", lambda h, b=b, c=c, half=half: h.tensor_scalar(qg[:, c, half * 512:(half + 1) * 512], self.psum[:, b, :],
                                                                          self.pc("qng", c), None, ALU.mult),
                     reads=[bt, self.Tl("par")], writes=[self.A("qg", c, half)])
            self.make_rstd(rstd_q[:, half * 512:(half + 1) * 512], self.A("rstdq", half), self.psum[:, SB, :], sbt, 1.0 / 512)
        self.fence()
        y = self.av(0, F32, 8 * T).rearrange("p (c t) -> p c t", c=8)
        diag = self.av(71168, BF16, 31 * 128).rearrange("p (k t) -> p k t", k=31)
        cw = self.pc("convw", 0, 248).rearrange("p (c k) -> p c k", c=8)
        S1, S2 = 4, 5
        for c in range(8):
            t_dgs = [self.A("diag", 0), self.A("diag", 1)]
            for k in range(31):
                P.op("dve", lambda h, c=c, k=k: h.tensor_scalar(diag[:, k, :], self.identb[:], cw[:, c, k:k + 1], None, ALU.mult),
                     reads=[self.Tl("identb"), self.Tl("par")], writes=[t_dgs[k // 16]])
            if isS:
                units = [(half * 512, 512, a_pad[:, c, half * 512:half * 512 + 542]) for half in range(2)]
            else:
                units = [(s * 256, 256, a_pad[:, c, s * 286:(s + 1) * 286]) for s in range(4)]
            if c % 2 == 1:
                self.tick()
            ubanks = [self.bank() for _ in units]
            for (t0, n, win), (b, bt) in zip(units, ubanks):
                self.mm(self.psum[:, b, 0:n], [(diag[:, k, :], win[:, k:k + n]) for k in range(16)], [t_dgs[0], self.A("apad", c)], bt, last=False)
            for (t0, n, win), (b, bt) in zip(units, ubanks):
                self.mm(self.psum[:, b, 0:n], [(diag[:, k, :], win[:, k:k + n]) for k in range(16, 31)], [t_dgs[1], self.A("apad", c)], bt, first=False)
                P.op("act", lambda h, b=b, c=c, t0=t0, n=n: h.activation(y[:, c, t0:t0 + n], self.psum[:, b, 0:n], AF.Identity,
                                                                      bias=self.pc("convb", c)),
                     reads=[bt, self.Tl("par")], writes=[self.A("y", c, t0 // 512)])
        self.fence()
        aT = self.av(38912, BF16, 8 * T).rearrange("p (c t) -> p c t", c=8)
        for half in range(2):
            t1, t2 = self.Tl("ps", S1), self.Tl("ps", S2)
            for c in range(8):
                ys_ = y[:, c, half * 512:(half + 1) * 512]
                rb, rt = self.ringbuf()
                yb = rb.bitcast(BF16)[:, 0:512]
                P.op("dve", lambda h, yb=yb, ys_=ys_: h.tensor_copy(yb, ys_), reads=[self.A("y", c, half)], writes=[rt])
                self.mm1(self.psum[:, S1, :], self.ones[:], yb, c == 0, c == 7, [rt], t1)
                rb2, rt2 = self.ringbuf()
                sq = rb2.bitcast(BF16)[:, 0:512]
                P.op("act", lambda h, sq=sq, ys_=ys_: h.activation(sq, ys_, AF.Square), reads=[self.A("y", c, half)], writes=[rt2])
                self.mm1(self.psum[:, S2, :], self.ones[:], sq, c == 0, c == 7, [rt2], t2)
            mean = self.rstd[0][:, :]
            rs = self.rstd[1][:, :]
            tm, tr_ = self.Tl("rstd", 0), self.Tl("rstd", 1)
            P.op("act", lambda h: h.activation(mean, self.psum[:, S1, :], AF.Copy, scale=1.0 / 1024), reads=[t1], writes=[tm])
            rb, rt = self.ringbuf()
            msq = rb[:, :]
            P.op("dve", lambda h, msq=msq: h.tensor_tensor(msq, mean, mean, ALU.mult), reads=[tm], writes=[rt])
            P.op("dve", lambda h, msq=msq: h.scalar_tensor_tensor(rs, self.psum[:, S2, :], 1.0 / 1024, msq, ALU.mult, ALU.subtract),
                 reads=[t2, rt], writes=[tr_])
            P.op("act", lambda h: h.activation(rs, rs, AF.Sqrt, scale=1.0, bias=EPS), reads=[tr_], writes=[tr_])
            P.op("dve", lambda h: h.reciprocal(rs, rs), reads=[tr_], writes=[tr_])
            for c in range(8):
                ys_ = y[:, c, half * 512:(half + 1) * 512]
                rb, rt = self.ringbuf()
                tmp = rb[:, :]
                P.op("dve", lambda h, tmp=tmp, ys_=ys_: h.tensor_tensor(tmp, ys_, mean, ALU.subtract), reads=[self.A("y", c, half), tm], writes=[rt])
                P.op("dve", lambda h, tmp=tmp: h.tensor_tensor(tmp, tmp, rs, ALU.mult), reads=[rt, tr_], writes=[rt])
                P.op("act", lambda h, tmp=tmp, c=c, half=half: h.activation(aT[:, c, half * 512:(half + 1) * 512], tmp, AF.Silu,
                                                                         scale=self.pc("clng", c), bias=self.pc("clnb", c)),
                     reads=[rt, self.Tl("par")], writes=[self.A("aT", c, half)])
        self.compute_mods(0, 8, 12)
        gate_tiles = self.modtile(l, 2)
        for half in range(2):
          for cb in range(4):
            view, tw = self.wload([(0, 8, 512, self.wsrc("ev_w_o", 0, 8, cb * 512, 512))], 512)
            for cc in range(4):
                c = cb * 4 + cc
                if True:
                    b, bt = self.bank()
                    self.mm(self.psum[:, b, :], [(view[:, k, cc * 128:(cc + 1) * 128], aT[:, k, half * 512:(half + 1) * 512]) for k in range(8)],
                            [tw] + [self.A("aT", k, half) for k in range(8)], bt)
                    self.resid(self.psum[:, b, :], bt, c, half, self.modcol(l, cond, 2, c), gate_tiles)
        if isS:
            self.fence()
            self.kside_sample(ckv_all, kpe_all, cond)
        self.fence()
        self.attention(ckv_all, kpe_all, qg, rstd_q, NK)
        attT = self.av(0, BF16, 8 * T).rearrange("p (c t) -> p c t", c=8)
        for cb in range(4):
            view, tw = self.wload([(0, 8, 512, self.wsrc("ev_w_o", 1024, 8, cb * 512, 512))], 512)
            for cc in range(4):
                c = cb * 4 + cc
                for half in range(2):
                    b, bt = self.bank()
                    self.mm(self.psum[:, b, :], [(view[:, k, cc * 128:(cc + 1) * 128], attT[:, k, half * 512:(half + 1) * 512]) for k in range(8)],
                            [tw] + [self.A("att", k, half) for k in range(8)], bt)
                    self.resid(self.psum[:, b, :], bt, c, half, self.modcol(l, cond, 2, c), gate_tiles)

    def kside(self, hsrc, htile, ntiles, ckv_all, kpe_all, koff, cond, state_out=False, rope_tok0=None, rs_idx=None, rk_pre=None):
        P = self.P
        SB = 4
        for ti, (t0, n) in enumerate(ntiles):
            view, tw = self.wload([(0, 16, 320, self.wsrc("ev_w_in", 0, 16, 2560, 320)),
                                   (320, 16, 64, self.wsrc("ev_w_in_sw", 0, 16, 0, 64))], 384)
            hs = [hsrc(k, t0, n) for k in range(16)]
            hts = [htile(k, ti) for k in range(16)]
            kb = []
            sbt = self.Tl("ps", SB)
            for c in range(2):
                b, bt = self.bank()
                self.mm(self.psum[:, b, 0:n], [(view[:, k, c * 128:(c + 1) * 128], hs[k]) for k in range(16)], [tw] + hts, bt)
                kb.append((b, bt))
                rb, rt = self.ringbuf()
                sq = rb.bitcast(BF16)[:, 0:n]
                P.op("act", lambda h, sq=sq, b=b, n=n: h.activation(sq, self.psum[:, b, 0:n], AF.Square), reads=[bt], writes=[rt])
                self.mm1(self.psum[:, SB, 0:n], self.ones[:], sq, c == 0, c == 1, [rt], sbt)
            if rs_idx is None:
                rs = self.rstd[ti % 2][:, 0:n]
                rst = self.Tl("rstd", ti % 2)
            else:
                rs = self.rstd[rs_idx][:, 256:256 + n]
                rst = self.Tl("rstdk", rs_idx)
            self.make_rstd(rs, rst, self.psum[:, SB, 0:n], sbt, 1.0 / 256)
            kt = self.A("kall", (koff + t0) // 512)
            if state_out:
                ckf = self.av(71168, F32, 2 * 512).rearrange("p (c t) -> p c t", c=2)
                kpf = self.av(75264, F32, 512)
            for c in range(2):
                b, bt = kb[c]
                if state_out:
                    P.op("dve", lambda h, b=b, c=c, n=n: h.scalar_tensor_tensor(ckf[:, c, 0:n], self.psum[:, b, 0:n], self.pc("kvng", c), rs, ALU.mult, ALU.mult),
                         reads=[bt, rst, self.Tl("par")], writes=[self.A("ckf", c)])
                    P.op("act", lambda h, c=c, n=n, t0=t0: h.activation(ckv_all[:, c, koff + t0:koff + t0 + n], ckf[:, c, 0:n], AF.Copy),
                         reads=[self.A("ckf", c)], writes=[kt])
                else:
                    P.op("dve", lambda h, b=b, c=c, n=n, t0=t0: h.scalar_tensor_tensor(ckv_all[:, c, koff + t0:koff + t0 + n], self.psum[:, b, 0:n],
                                                                                    self.pc("kvng", c), rs, ALU.mult, ALU.mult),
                         reads=[bt, rst, self.Tl("par")], writes=[kt])
            b, bt = self.bank()
            self.mm(self.psum[0:64, b, 0:n], [(view[:, k, 256:320], hs[k]) for k in range(16)], [tw] + hts, bt)
            if rope_tok0 is None:
                if state_out:
                    P.op("act", lambda h, b=b, n=n: h.activation(kpf[0:64, 0:n], self.psum[0:64, b, 0:n], AF.Copy), reads=[bt], writes=[self.A("kpf")])
                P.op("dve", lambda h, b=b, n=n, t0=t0: h.tensor_copy(kpe_all[0:64, koff + t0:koff + t0 + n], self.psum[0:64, b, 0:n]), reads=[bt], writes=[kt])
            else:
                b2, bt2 = self.bank()
                self.mm(self.psum[0:64, b2, 0:n], [(view[:, k, 320:384], hs[k]) for k in range(16)], [tw] + hts, bt2)
                if rk_pre is not None:
                    rk, trk = rk_pre
                else:
                    rk = self.av(71168, F32, 2 * 512).rearrange("p (j t) -> p j t", j=2)
                    trk = self.A("ropek")
                    P.dma("sp", rk[0:64, :, 0:n], self.dram["ropek"][:, :, rope_tok0 + t0:rope_tok0 + t0 + n], "ropek", writes=[trk])
                rb, rt = self.ringbuf()
                rb2, rt2 = self.ringbuf()
                P.op("dve", lambda h, rb=rb, b=b, n=n: h.tensor_tensor(rb[0:64, 0:n], self.psum[0:64, b, 0:n], rk[0:64, 0, 0:n], ALU.mult), reads=[bt, trk], writes=[rt])
                P.op("dve", lambda h, rb2=rb2, b2=b2, n=n: h.tensor_tensor(rb2[0:64, 0:n], self.psum[0:64, b2, 0:n], rk[0:64, 1, 0:n], ALU.mult), reads=[bt2, trk], writes=[rt2])
                P.op("dve", lambda h, rb=rb, rb2=rb2, n=n, t0=t0: h.tensor_tensor(kpe_all[0:64, koff + t0:koff + t0 + n], rb[0:64, 0:n], rb2[0:64, 0:n], ALU.add),
                     reads=[rt, rt2], writes=[kt])
            if state_out:
                so = self.av(77312, F32, 320)
                for n4 in range(n // 128):
                    b, bt = self.bank()
                    tso = self.A("so")

                    def tr(h, b=b, n4=n4):
                        for c in range(2):
                            h.transpose(self.psum[:, b, c * 128:(c + 1) * 128], ckf[:, c, n4 * 128:(n4 + 1) * 128], self.identf[:])
                        return h.transpose(self.psum[:, b, 256:320], kpf[0:64, n4 * 128:(n4 + 1) * 128], self.identf[0:64, 0:64])
                    P.op("pe", tr, reads=[self.A("ckf", 0), self.A("ckf", 1), self.A("kpf"), self.Tl("ident")], writes=[bt])
                    P.op("dve", lambda h, b=b: h.tensor_copy(so[:, 0:320], self.psum[:, b, 0:320]), reads=[bt], writes=[tso])
                    r0 = t0 + n4 * 128
                    P.dma("sp", self.dram["o_ckv"][r0:r0 + 128, :], so[:, 0:256], "so", reads=[tso], is_output=True)
                    P.dma("sp", self.dram["o_kpe"][r0:r0 + 128, :], so[:, 256:320], "so", reads=[tso], is_output=True)

    def kside_sample(self, ckv_all, kpe_all, cond):
        P = self.P
        l = 0
        stg = self.av(0, F32, 2048)
        for n in range(2):
            tk = self.A("kall", 0)
            self.load_T(self.dram["cckv"][n * 128:(n + 1) * 128, :], 128, stg[:, 0:256], self.A("kstg", 0),
                        lambda c, n=n: ckv_all[:, c, n * 128:(n + 1) * 128], lambda c: tk, "kstg0", ncols=256)
            self.load_T(self.dram["ckpe"][n * 128:(n + 1) * 128, :], 128, stg[:, 256:320], self.A("kstg", 0),
                        lambda c, n=n: kpe_all[0:64, n * 128:(n + 1) * 128], lambda c: tk, "kstg0", ncols=64)
        stgs = [self.av(i * 8192, F32, 2048) for i in range(2)]
        xns = [self.av(16384 + i * 4096, BF16, 2048) for i in range(2)]
        hts_ = [self.av(24576 + i * 4096, BF16, 16 * 128).rearrange("p (c t) -> p c t", c=16) for i in range(2)]
        ssb = self.av(75264, F32, 8)
        psb = self.psum[:, :, :].bitcast(BF16)

        def s1(tt):
            pi = tt % 2
            st_, xn = stgs[pi], xns[pi]
            tst, txn, tss = self.A("kstg", pi), self.A("kxn", pi), self.A("kss", pi)
            P.dma("sp", st_, self.dram["xs_all"][tt * 128:(tt + 1) * 128, :], f"kstg{pi}", writes=[tst])
            P.op("dve", lambda h: h.memset(ssb[:, pi:pi + 1], 0.0), writes=[tss])
            P.op("act", lambda h: h.activation(xn, st_, AF.Square, accum_out=ssb[:, pi:pi + 1]), reads=[tst], writes=[txn, tss])
            P.op("act", lambda h: h.activation(ssb[:, 2 + pi:3 + pi], ssb[:, pi:pi + 1], AF.Sqrt, scale=1.0 / D, bias=EPS), reads=[tss], writes=[tss])
            P.op("dve", lambda h: h.reciprocal(ssb[:, 2 + pi:3 + pi], ssb[:, 2 + pi:3 + pi]), reads=[tss], writes=[tss])
            P.op("dve", lambda h: h.tensor_scalar(xn, st_, ssb[:, 2 + pi:3 + pi], None, ALU.mult), reads=[tst, tss], writes=[txn])

        rks = [self.av(71168 + i * 1024, F32, 2 * 128).rearrange("p (j t) -> p j t", j=2) for i in range(2)]

        def s2(tt):
            pi = tt % 2
            xn, ht_ = xns[pi], hts_[pi]
            txn = self.A("kxn", pi)
            P.dma("sp", rks[pi][0:64, :, :], self.dram["ropek"][:, :, tt * 128:(tt + 1) * 128], f"ropek{pi}", writes=[self.A("ropek", pi)])
            rd = [self.Tl("acoef")] + self.modtile(l, 0)
            for g in range(4):
                b, bt = self.bank()

                def tr(h, g=g, b=b):
                    for j in range(4):
                        c = g * 4 + j
                        ins = h.transpose(psb[:, b, j * 128:(j + 1) * 128], xn[:, c * 128:(c + 1) * 128], self.identb[:])
                    return ins
                P.op("pe", tr, reads=[txn, self.Tl("identb")], writes=[bt])
                for j in range(4):
                    c = g * 4 + j
                    src = psb[:, b, j * 128:(j + 1) * 128]
                    if g % 2 == 0:
                        P.op("act", lambda h, c=c, src=src: h.activation(ht_[:, c, :], src, AF.Identity, scale=self.acoef[:, cond, c:c + 1],
                                                                     bias=self.modcol(l, cond, 0, c)),
                             reads=[bt] + rd, writes=[self.A("kh", pi, c)])
                    else:
                        P.op("dve", lambda h, c=c, src=src: h.tensor_scalar(ht_[:, c, :], src, self.acoef[:, cond, c:c + 1], self.modcol(l, cond, 0, c),
                                                                        ALU.mult, ALU.add),
                             reads=[bt] + rd, writes=[self.A("kh", pi, c)])

        def s3(tt):
            pi = tt % 2
            ht_ = hts_[pi]
            self.kside(lambda k, t0, n, ht_=ht_: ht_[:, k, :], lambda k, ti, pi=pi: self.A("kh", pi, k), [(0, 128)], ckv_all, kpe_all,
                       256 + tt * 128, cond, rope_tok0=tt * 128, rs_idx=pi, rk_pre=(rks[pi], self.A("ropek", pi)))
        for step in range(32 + 2):
            if step < 32:
                s1(step)
            if 0 <= step - 1 < 32:
                s2(step - 1)
            if 0 <= step - 2 < 32:
                s3(step - 2)

    def attention(self, ckv_all, kpe_all, qg, rstd_q, NK):
        P = self.P
        isS = self.pass_ == "S"
        attT = self.av(0, BF16, 8 * T).rearrange("p (c t) -> p c t", c=8)
        PT = [self.av(16384 + i * 1024, BF16, 512) for i in range(3)]
        qn_h = self.av(19456, BF16, T)
        qpe_h = self.av(21504, BF16, T)
        wkvs = [self.av(o_, BF16, 512).rearrange("p (k f) -> p k f", k=2) for o_ in (23552, 75776)]
        wqs = [self.av(o_, BF16, 1024).rearrange("p (k f) -> p k f", k=4) for o_ in (24576, 76800)]

        def load_head_w(hd):
            si = hd % 2
            pairs = [(wkvs[si], self.wsrc("ev_w_kvb", 0, 2, hd * 256, 256)),
                     (wqs[si][:, :, 0:192], self.wsrc("ev_w_qb", 0, 4, hd * 192, 192))]
            if isS:
                pairs.append((wqs[si][:, :, 192:256], self.wsrc("ev_w_qb_sw", 0, 4, hd * 64, 64)))
            P.dma_multi("pool", pairs, f"whd{si}", writes=[self.A("whd", si)])
        nkc = NK // 128
        if isS:
            crs = [self.av(71168, F32, T), self.av(26624, F32, T)]
            tcr = self.A("crs")
            P.dma_multi("sp", [(crs[j][0:64, :], self.dram["ropeq"][:, j, :]) for j in range(2)], "ropeq", writes=[tcr])
            for j in range(2):
                for half in range(2):
                    sl = crs[j][0:64, half * 512:(half + 1) * 512]
                    P.op("dve", lambda h, sl=sl, half=half: h.tensor_tensor(sl, sl, rstd_q[0:64, half * 512:(half + 1) * 512], ALU.mult),
                         reads=[tcr, self.A("rstdq", half)], writes=[tcr])
        OB, SBK = 6, 7
        kts = [self.A("kall", i) for i in range((NK + 511) // 512)]
        load_head_w(0)
        for hd in range(8):
            wkv, wq = wkvs[hd % 2], wqs[hd % 2]
            twk = twq = self.A("whd", hd % 2)
            if hd + 1 < 8:
                load_head_w(hd + 1)
            s = self.wn
            self.wn = (self.wn + 1) % 3
            slot_t = self.Tl("w", s)
            n512 = (NK + 511) // 512

            def sub(nm, slot_t=slot_t):
                t = Tile(nm)
                t.w = list(slot_t.w)
                t.r = list(slot_t.r)
                return t
            tK = [sub(("K", i)) for i in range(n512)]
            tV = [sub(("V", i)) for i in range(n512)]
            KT = self.wslot[s][:, 0:NK]
            V = self.wslot[s][:, NK:2 * NK].rearrange("p (n d) -> p n d", d=128)
            for i, k0 in enumerate(range(0, NK, 512)):
                n = min(512, NK - k0)
                b, bt = self.bank()
                self.mm(self.psum[:, b, 0:n], [(wkv[:, k, 0:128], ckv_all[:, k, k0:k0 + n]) for k in range(2)], [twk] + kts, bt)
                if i % 2 == 0:
                    P.op("act", lambda h, b=b, k0=k0, n=n: h.activation(KT[:, k0:k0 + n], self.psum[:, b, 0:n], AF.Copy), reads=[bt], writes=[tK[i]])
                else:
                    P.op("dve", lambda h, b=b, k0=k0, n=n: h.tensor_copy(KT[:, k0:k0 + n], self.psum[:, b, 0:n]), reads=[bt], writes=[tK[i]])
                b, bt = self.bank()
                nch = n // 128

                def vf(h, b=b, k0=k0, nch=nch):
                    for j in range(nch):
                        for k in range(2):
                            ins = h.matmul(self.psum[:, b, j * 128:(j + 1) * 128], ckv_all[:, k, k0 + j * 128:k0 + (j + 1) * 128], wkv[:, k, 128:256],
                                           start=(k == 0), stop=(k == 1))
                    return ins
                P.op("pe", vf, reads=[twk] + kts, writes=[bt])
                vdst = V[:, k0 // 128:k0 // 128 + nch, :]
                vsrc = self.psum[:, b, 0:n].rearrange("p (n d) -> p n d", d=128)
                if i % 2 == 1:
                    P.op("act", lambda h, vdst=vdst, vsrc=vsrc: h.activation(vdst, vsrc, AF.Copy), reads=[bt], writes=[tV[i]])
                else:
                    P.op("dve", lambda h, vdst=vdst, vsrc=vsrc: h.tensor_copy(vdst, vsrc), reads=[bt], writes=[tV[i]])
            for half in range(2):
                qs = [qg[:, k, half * 512:(half + 1) * 512] for k in range(4)]
                qts = [self.A("qg", k, half) for k in range(4)]
                b, bt = self.bank()
                self.mm(self.psum[:, b, :], [(wq[:, k, 0:128], qs[k]) for k in range(4)], [twq] + qts, bt)
                P.op("dve", lambda h, b=b, half=half: h.tensor_tensor(qn_h[:, half * 512:(half + 1) * 512], self.psum[:, b, :],
                                                                     rstd_q[:, half * 512:(half + 1) * 512], ALU.mult),
                     reads=[bt, self.A("rstdq", half)], writes=[self.A("qn", half)])
                b, bt = self.bank()
                self.mm(self.psum[0:64, b, :], [(wq[:, k, 128:192], qs[k]) for k in range(4)], [twq] + qts, bt)
                if not isS:
                    P.op("dve", lambda h, b=b, half=half: h.tensor_tensor(qpe_h[0:64, half * 512:(half + 1) * 512], self.psum[0:64, b, :],
                                                                         rstd_q[0:64, half * 512:(half + 1) * 512], ALU.mult),
                         reads=[bt, self.A("rstdq", half)], writes=[self.A("qpe", half)])
                else:
                    b2, bt2 = self.bank()
                    self.mm(self.psum[0:64, b2, :], [(wq[:, k, 192:256], qs[k]) for k in range(4)], [twq] + qts, bt2)
                    rb, rt = self.ringbuf()
                    rb2, rt2 = self.ringbuf()
                    P.op("dve", lambda h, rb=rb, b=b, half=half: h.tensor_tensor(rb[0:64, :], self.psum[0:64, b, :], crs[0][0:64, half * 512:(half + 1) * 512], ALU.mult),
                         reads=[bt, tcr], writes=[rt])
                    P.op("dve", lambda h, rb2=rb2, b2=b2, half=half: h.tensor_tensor(rb2[0:64, :], self.psum[0:64, b2, :], crs[1][0:64, half * 512:(half + 1) * 512], ALU.mult),
                         reads=[bt2, tcr], writes=[rt2])
                    P.op("dve", lambda h, rb=rb, rb2=rb2, half=half: h.tensor_tensor(qpe_h[0:64, half * 512:(half + 1) * 512], rb[0:64, :], rb2[0:64, :], ALU.add),
                         reads=[rt, rt2], writes=[self.A("qpe", half)])
            if isS:
                units = [(half * 512, 512, list(range(nkc)), half) for half in range(2)]
            else:
                units = [(s_ * 256, 256, [2 * s_, 2 * s_ + 1], s_ // 2) for s_ in range(4)]
            for (q0, nq, kcs, half) in units:
                qrd = [self.A("qn", half), self.A("qpe", half)]
                tO, tS = self.Tl("ps", OB), self.Tl("ps", SBK)
                pend = []

                def emit_qk(j):
                    kc = kcs[j]
                    b, bt = self.bank()
                    self.mm(self.psum[:, b, 0:nq], [(KT[:, kc * 128:(kc + 1) * 128], qn_h[:, q0:q0 + nq]),
                                                    (kpe_all[0:64, kc * 128:(kc + 1) * 128], qpe_h[0:64, q0:q0 + nq])],
                            qrd + [kts[kc // 4], tK[kc // 4]], bt)
                    pend.append((b, bt))
                nj = len(kcs)
                for j in range(min(2, nj)):
                    emit_qk(j)
                for j in range(nj):
                    b, bt = pend[j]
                    pi = j % 3
                    tp = self.A("PT", pi)
                    P.op("act", lambda h, b=b, pi=pi: h.activation(PT[pi][:, 0:nq], self.psum[:, b, 0:nq], AF.Exp, scale=SCALE), reads=[bt], writes=[tp])
                    if j + 2 < nj:
                        emit_qk(j + 2)
                    kc = kcs[j]
                    self.mm1(self.psum[:, OB, 0:nq], V[:, kc, :], PT[pi][:, 0:nq], j == 0, j == nj - 1, [tp, tV[kc // 4]], tO)
                    self.mm1(self.psum[:, SBK, 0:nq], self.ones[:], PT[pi][:, 0:nq], j == 0, j == nj - 1, [tp], tS)
                rb, rt = self.ringbuf()
                P.op("dve", lambda h, rb=rb: h.reciprocal(rb[:, 0:nq], self.psum[:, SBK, 0:nq]), reads=[tS], writes=[rt])
                P.op("dve", lambda h, rb=rb, hd=hd, q0=q0: h.tensor_tensor(attT[:, hd, q0:q0 + nq], self.psum[:, OB, 0:nq], rb[:, 0:nq], ALU.mult),
                     reads=[tO, rt], writes=[self.A("att", hd, half)])
            toks = {}
            for t_ in tK + tV:
                for (k_, v_) in t_.w + t_.r:
                    toks[k_] = max(toks.get(k_, 0), v_)
            slot_t.w = []
            slot_t.r = list(toks.items())
            self.tick()


_CACHE = {}


def _fm(v):
    v = np.asarray(v, np.float32)
    return np.ascontiguousarray(v.reshape(-1, 128).T)


def _rope_tables(pos_tok):
    pos_tok = np.asarray(pos_tok)
    rows = (pos_tok // 64).astype(np.float32)
    cols = (pos_tok % 64).astype(np.float32)
    inv = np.power(np.float32(10000.0), -np.arange(16, dtype=np.float32) * np.float32(2.0) / np.float32(32)).astype(np.float32)
    out = np.zeros((64, 2, len(pos_tok)), np.float32)
    for d in range(64):
        i, j, f = d // 32, (d % 32) // 16, d % 16
        ang = ((rows if i == 0 else cols) * inv[f]).astype(np.float32)
        out[d, 0] = np.cos(ang)
        out[d, 1] = np.sin(ang) * (-1.0 if j == 0 else 1.0)
    return out


_PARTNER = np.array([(d // 32) * 32 + (1 - (d % 32) // 16) * 16 + d % 16 for d in range(64)])


def kernel(x_prompt, x_sample, cache_ckv, cache_kpe, c, c_ctx, mod_w, mod_b, norm_mix_g, norm_ffn_g,
           ffn_w_gate, ffn_w_up, ffn_w_down, ev_w_in, ev_conv_w, ev_conv_b, ev_conv_ln_g, ev_conv_ln_b,
           ev_q_norm_g, ev_w_qb, ev_kv_norm_g, ev_w_kvb, ev_w_o, od_w_in, od_ln_g, od_ln_b, od_w_s, od_b_s,
           od_w_o, final_norm_g):
    f = lambda a: np.ascontiguousarray(np.asarray(a, dtype=np.float32))
    x_prompt, x_sample = f(x_prompt), f(x_sample)
    if "nc" not in _CACHE:
        _CACHE["nc"] = Builder().build()
    nc = _CACHE["nc"]
    par = np.zeros((128, NPAR), np.float32)

    def put(name, arr):
        arr = np.asarray(arr, np.float32)
        par[:, PCOL[name]:PCOL[name] + arr.shape[1]] = arr
    for l in range(2):
        put(f"nmg{l}", _fm(norm_mix_g[l])); put(f"nfg{l}", _fm(norm_ffn_g[l])); put(f"modb{l}", _fm(mod_b[l]))
    put("fng", _fm(final_norm_g)); put("convb", _fm(ev_conv_b[0])); put("clng", _fm(ev_conv_ln_g[0])); put("clnb", _fm(ev_conv_ln_b[0]))
    put("qng", _fm(ev_q_norm_g[0])); put("kvng", _fm(ev_kv_norm_g[0])); put("olng", _fm(od_ln_g[0])); put("olnb", _fm(od_ln_b[0]))
    cw = np.asarray(ev_conv_w, np.float32)[0, :, 0, :]
    put("convw", cw.T.reshape(8, 128, 31).transpose(1, 0, 2).reshape(128, 248))
    w_in = f(ev_w_in[0]); w_qb = f(ev_w_qb[0])
    w_in_sw = np.ascontiguousarray(w_in[:, 2816 + _PARTNER])
    w_qb_sw = np.ascontiguousarray(np.concatenate([w_qb[:, h * 192 + 128 + _PARTNER] for h in range(8)], axis=1))
    shared = {
        "params": par, "ident": np.eye(128, dtype=np.float32),
        "b_s": f(od_b_s[0]).reshape(1, 1024), "w_sT": np.ascontiguousarray(f(od_w_s[0]).transpose(2, 0, 1)),
        "ev_w_in": w_in, "ev_w_in_sw": w_in_sw, "ev_w_qb": w_qb, "ev_w_qb_sw": w_qb_sw,
        "ev_w_kvb": f(ev_w_kvb[0]), "ev_w_o": f(ev_w_o[0]), "od_w_in": f(od_w_in[0]), "od_w_o": f(od_w_o[0]),
        "ropek": _rope_tables(np.arange(4096)),
    }
    for l in range(2):
        shared[f"mod_w{l}"] = f(mod_w[l]); shared[f"wg{l}"] = f(ffn_w_gate[l]); shared[f"wu{l}"] = f(ffn_w_up[l]); shared[f"wd{l}"] = f(ffn_w_down[l])
    in_maps = []
    for i in range(8):
        b, q = i // 4, i % 4
        halo = np.zeros((32, D), np.float32)
        hm = np.zeros((128, 32), np.float32)
        if q > 0:
            halo[0:15] = x_sample[b, q * 1024 - 15:q * 1024]; hm[:, 0:15] = 1.0
        if q < 3:
            halo[15:30] = x_sample[b, (q + 1) * 1024:(q + 1) * 1024 + 15]; hm[:, 15:30] = 1.0
        condT = np.stack([_fm(c_ctx), _fm(c[b])], axis=-1)
        m = dict(shared)
        m.update({
            "xp": x_prompt[4 * i:4 * i + 4].reshape(T, D), "xs_own": np.ascontiguousarray(x_sample[b, q * 1024:(q + 1) * 1024]),
            "xs_all": x_sample[b], "xs_halo": halo, "hmask": hm, "cckv": f(cache_ckv[b, 0]), "ckpe": f(cache_kpe[b, 0]),
            "condT": np.ascontiguousarray(condT), "ropeq": _rope_tables(np.arange(q * 1024, (q + 1) * 1024)),
        })
        in_maps.append({"i_" + k_: v_ for k_, v_ in m.items()})
    res = run_bass_kernel_spmd(nc, in_maps, core_ids=list(range(8)))
    R = res.results
    y_prompt = np.stack([R[i]["o_yp"].reshape(4, SEQ_P, D) for i in range(8)], 0).reshape(32, SEQ_P, D)
    y_sample = np.stack([R[i]["o_ys"] for i in range(8)], 0).reshape(2, 4096, D)
    s_ckv = np.stack([R[i]["o_o_ckv"].reshape(4, 1, SEQ_P, 256) for i in range(8)], 0).reshape(32, 1, SEQ_P, 256)
    s_kpe = np.stack([R[i]["o_o_kpe"].reshape(4, 1, SEQ_P, 64) for i in range(8)], 0).reshape(32, 1, SEQ_P, 64)
    return (np.ascontiguousarray(y_prompt, np.float32), np.ascontiguousarray(y_sample, np.float32),
            np.ascontiguousarray(s_ckv, np.float32), np.ascontiguousarray(s_kpe, np.float32))
```

```python
import numpy as np
import concourse.bass as bass
import concourse.mybir as mybir
from concourse.bass_utils import run_bass_kernel_spmd

F32 = mybir.dt.float32
BF16 = mybir.dt.bfloat16
AF = mybir.ActivationFunctionType
ALU = mybir.AluOpType

D = 2048
NC16 = 16
T = 1024
DFF = 5632
EPS = 1e-6
SEQ_P = 256
NKS = 4352
SCALE = 192 ** -0.5
WSLOT = 8704
ARENA = 79104

PCOL = {}
_off = 0
for _n, _w in (("nmg0", 16), ("nmg1", 16), ("nfg0", 16), ("nfg1", 16), ("fng", 16),
               ("modb0", 96), ("modb1", 96), ("convb", 8), ("clng", 8), ("clnb", 8),
               ("qng", 4), ("kvng", 2), ("olng", 16), ("olnb", 16), ("convw", 248)):
    PCOL[_n] = _off
    _off += _w
NPAR = _off


class Tile:
    __slots__ = ("name", "w", "r", "excl")

    def __init__(self, name, w=None):
        self.name = name
        self.excl = False
        self.w = list(w) if w else []
        self.r = []


class Rec:
    def __init__(self):
        self.calls = []

    def __getattr__(self, name):
        def f(*a, **k):
            self.calls.append((name, a, k))
            return self
        return f


class Eng:
    def __init__(self, name):
        self.name = name
        self.items = []
        self.count = 0
        self.waited = {}


class Prog:
    ENG = ("pe", "act", "dve", "pool", "sp")

    def __init__(self, nc):
        self.nc = nc
        self.eng = {n: Eng(n) for n in self.ENG}
        self.sems = {}
        self.dma_val = {}
        self.out_tokens = []

    def sem(self, key):
        if key not in self.sems:
            self.sems[key] = self.nc.alloc_semaphore(name="s_" + key)
        return self.sems[key]

    def _deps(self, e, reads, writes):
        deps = {}
        own = "e_" + e.name
        for t in reads:
            for (k, v) in t.w:
                if deps.get(k, 0) < v:
                    deps[k] = v
            if t.excl:
                for (k, v) in t.r:
                    if k != own and deps.get(k, 0) < v:
                        deps[k] = v
        for t in writes:
            for (k, v) in t.w:
                if deps.get(k, 0) < v:
                    deps[k] = v
            for (k, v) in t.r:
                if deps.get(k, 0) < v:
                    deps[k] = v
        waits = []
        for k, v in deps.items():
            if k == "e_pe" and e.name == "pe":
                continue
            if e.waited.get(k, 0) >= v:
                continue
            e.waited[k] = v
            waits.append((k, v))
        return waits

    def _commit(self, tok, reads, writes):
        for t in reads:
            t.r = [x for x in t.r if x[0] != tok[0]] + [tok]
        for t in writes:
            t.w = [tok]
            t.r = []

    def op(self, en, fn, reads=(), writes=()):
        e = self.eng[en]
        waits = self._deps(e, reads, writes)
        e.count += 1
        tok = ("e_" + en, e.count)
        rec = Rec()
        fn(rec)
        assert rec.calls
        e.items.append(("op", waits, rec.calls))
        self._commit(tok, reads, writes)
        return tok

    def dma(self, qn, out_ap, in_ap, skey, reads=(), writes=(), is_output=False):
        e = self.eng[qn]
        waits = self._deps(e, reads, writes)
        k = "d_" + skey
        self.dma_val[k] = self.dma_val.get(k, 0) + 16
        tok = (k, self.dma_val[k])
        e.items.append(("dma", waits, (out_ap, in_ap, k)))
        self._commit(tok, reads, writes)
        if is_output:
            self.out_tokens.append(tok)
        return tok

    def dma_multi(self, qn, pairs, skey, reads=(), writes=()):
        e = self.eng[qn]
        waits = self._deps(e, reads, writes)
        k = "d_" + skey
        for i, (out_ap, in_ap) in enumerate(pairs):
            self.dma_val[k] = self.dma_val.get(k, 0) + 16
            e.items.append(("dma", waits if i == 0 else [], (out_ap, in_ap, k)))
        tok = (k, self.dma_val[k])
        self._commit(tok, reads, writes)
        return tok

    def fence_tokens(self):
        toks = [("e_" + n, self.eng[n].count) for n in ("pe", "act", "dve", "pool") if self.eng[n].count]
        for k, v in self.dma_val.items():
            if not k.startswith("d_w"):
                toks.append((k, v))
        return toks

    def emit(self):
        nc = self.nc
        fin = {}
        for (k, v) in self.out_tokens:
            fin[k] = max(fin.get(k, 0), v)
        sp = self.eng["sp"]
        sp.items.append(("wait", [(k, v) for k, v in fin.items()], None))
        for en in self.ENG:
            self.sem("e_" + en)
        for k in self.dma_val:
            self.sem(k)

        def run(en, h):
            e = self.eng[en]
            own = self.sems["e_" + en]
            for kind, waits, payload in e.items:
                for (k, v) in waits:
                    h.wait_ge(self.sems[k], v)
                if kind == "op":
                    for (mname, a, kw) in payload:
                        ins = getattr(h, mname)(*a, **kw)
                    ins.then_inc(own, 1)
                elif kind == "dma":
                    out_ap, in_ap, k = payload
                    h.dma_start(out=out_ap, in_=in_ap).then_inc(self.sems[k], 16)

        with nc.Block() as block:
            @block.tensor
            def _(h):
                run("pe", h)

            @block.scalar
            def _(h):
                run("act", h)

            @block.vector
            def _(h):
                run("dve", h)

            @block.gpsimd
            def _(h):
                run("pool", h)

            @block.sync
            def _(h):
                run("sp", h)


class Builder:
    def __init__(self):
        nc = self.nc = bass.Bass("TRN2", target_bir_lowering=False)
        self.P = Prog(nc)
        self.dram = {}
        self.tiles = {}
        self.arena_tiles = set()
        self.fence_toks = []
        self.xT = nc.alloc_sbuf_tensor("xT", [128, 16, T], F32)
        self.wslot = [nc.alloc_sbuf_tensor(f"wslot{i}", [128, WSLOT], BF16) for i in range(3)]
        self.ring = [nc.alloc_sbuf_tensor(f"ring{i}", [128, 512], F32) for i in range(3)]
        self.rstd = [nc.alloc_sbuf_tensor(f"rstd{i}", [128, 512], F32) for i in range(2)]
        self.par = nc.alloc_sbuf_tensor("par", [128, NPAR], F32)
        self.mods = [nc.alloc_sbuf_tensor(f"mods{l}", [128, 2, 96], F32) for l in range(2)]
        self.acoef = nc.alloc_sbuf_tensor("acoef", [128, 8, 16], F32)
        self.identf = nc.alloc_sbuf_tensor("identf", [128, 128], F32)
        self.identb = nc.alloc_sbuf_tensor("identb", [128, 128], BF16)
        self.ones = nc.alloc_sbuf_tensor("ones", [128, 128], BF16)
        self.condT = nc.alloc_sbuf_tensor("condT", [128, 16, 2], F32)
        self.scT = nc.alloc_sbuf_tensor("scT", [128, 16, 2], BF16)
        self.hmask = nc.alloc_sbuf_tensor("hmask", [128, 32], F32)
        self.arena = nc.alloc_sbuf_tensor("arena", [128, ARENA // 2], BF16)
        self.psum = nc.alloc_psum_tensor("psum", [128, 8, 512], F32)
        self.wn = 0
        self.rn = 0
        self.bn = 0
        self.nbanks = 4

    def din(self, name, shape):
        t = self.nc.dram_tensor("i_" + name, list(shape), F32, kind="ExternalInput")
        self.dram[name] = t.ap()
        return self.dram[name]

    def dout(self, name, shape):
        t = self.nc.dram_tensor("o_" + name, list(shape), F32, kind="ExternalOutput")
        self.dram[name] = t.ap()
        return self.dram[name]

    def Tl(self, *key, arena=False):
        t = self.tiles.get(key)
        if t is None:
            t = Tile(key, self.fence_toks if arena else None)
            t.excl = key[0] == "ps"
            self.tiles[key] = t
            if arena:
                self.arena_tiles.add(key)
        return t

    def A(self, *key):
        return self.Tl(*key, arena=True)

    def fence(self):
        self.fence_toks = self.P.fence_tokens()
        for k in self.arena_tiles:
            del self.tiles[k]
        self.arena_tiles = set()

    def av(self, off, dt, n):
        assert off % 32 == 0 and off + n * (4 if dt == F32 else 2) <= ARENA, (off, n)
        a = self.arena[:, off // 2: off // 2 + n * (2 if dt == F32 else 1)]
        return a.bitcast(F32) if dt == F32 else a

    def pc(self, name, c=0, n=1):
        o = PCOL[name] + c
        return self.par[:, o:o + n]

    def bank(self):
        b = self.bn
        self.bn = (self.bn + 1) % self.nbanks
        return b, self.Tl("ps", b)

    def ringbuf(self):
        r = self.rn
        self.rn = (self.rn + 1) % 3
        return self.ring[r], self.Tl("ring", r)

    def wload(self, parts, tot):
        s = self.wn
        self.wn = (self.wn + 1) % 3
        nk = parts[0][1]
        view = self.wslot[s][:, 0:nk * tot].rearrange("p (k f) -> p k f", k=nk)
        tw = self.Tl("w", s)
        self.P.dma_multi("pool", [(view[:, :, co:co + ncol], src) for (co, nk_, ncol, src) in parts], f"w{s}", writes=[tw])
        return view, tw

    def mm(self, out_ap, pairs, reads, wt, first=True, last=True):
        def fn(h):
            n = len(pairs)
            for i, (l, r) in enumerate(pairs):
                ins = h.matmul(out_ap, l, r, start=(first and i == 0), stop=(last and i == n - 1))
            return ins
        self.P.op("pe", fn, reads=reads, writes=[wt])

    def mm1(self, out_ap, l, r, start, stop, reads, wt):
        self.P.op("pe", lambda h: h.matmul(out_ap, l, r, start=start, stop=stop), reads=list(reads) + [self.Tl("ones")], writes=[wt])

    def wsrc(self, name, r0, nk, c0, ncol):
        w = self.dram[name]
        return w[r0:r0 + nk * 128, c0:c0 + ncol].rearrange("(k p) f -> p k f", p=128)

    def make_rstd(self, dst_ap, dst_t, bank_ap, bank_t, inv_n):
        P = self.P
        P.op("act", lambda h: h.activation(dst_ap, bank_ap, AF.Sqrt, scale=inv_n, bias=EPS), reads=[bank_t], writes=[dst_t])
        P.op("dve", lambda h: h.reciprocal(dst_ap, dst_ap), reads=[dst_t], writes=[dst_t])

    def norm_mod(self, xsrc, xt, hdst, ht, ntiles, aidx, sh_ap, rd_extra=(), rs_idx=None):
        P = self.P
        SB = 5
        for ti, (t0, n) in enumerate(ntiles):
            bt = self.Tl("ps", SB)
            bap = self.psum[:, SB, 0:n]
            for c in range(16):
                rb, rt = self.ringbuf()
                sq = rb.bitcast(BF16)[:, 0:n]
                P.op("act", lambda h, sq=sq, c=c: h.activation(sq, xsrc(c, t0, n), AF.Square), reads=[xt(c, ti)], writes=[rt])
                self.mm1(bap, self.ones[:], sq, c == 0, c == 15, [rt], bt)
            ri = (ti % 2) if rs_idx is None else rs_idx
            rs = self.rstd[ri][:, 0:n]
            rst = self.Tl("rstd", ri)
            self.make_rstd(rs, rst, bap, bt, 1.0 / D)
            for c in range(16):
                rb, rt = self.ringbuf()
                tmp = rb[:, 0:n]
                P.op("dve", lambda h, tmp=tmp, c=c: h.tensor_tensor(tmp, xsrc(c, t0, n), rs, ALU.mult), reads=[xt(c, ti), rst], writes=[rt])
                P.op("act", lambda h, tmp=tmp, c=c: h.activation(hdst(c, t0, n), tmp, AF.Identity,
                                                                  scale=self.acoef[:, aidx, c:c + 1], bias=sh_ap(c)),
                     reads=[rt, self.Tl("acoef")] + list(rd_extra), writes=[ht(c, ti)])

    def load_T(self, src_rows, n, stg_ap, stg_t, dst, dst_t, skey, ncols=2048, dst_grp=None):
        P = self.P
        P.dma("sp", stg_ap[0:n, :], src_rows, skey, writes=[stg_t])
        nch = (ncols + 127) // 128
        for g in range(0, nch, 4):
            b, bt = self.bank()
            cs = list(range(g, min(g + 4, nch)))

            def tr(h, cs=cs, b=b):
                for j, c in enumerate(cs):
                    w = min(128, ncols - c * 128)
                    ins = h.transpose(self.psum[0:w, b, j * 128:j * 128 + n], stg_ap[0:n, c * 128:c * 128 + w], self.identf[0:n, 0:n])
                return ins
            P.op("pe", tr, reads=[stg_t, self.Tl("ident")], writes=[bt])
            if dst_grp is not None:
                ncs = len(cs)
                src = self.psum[:, b, 0:ncs * 128].rearrange("p (j t) -> p j t", j=ncs)[:, :, 0:n]
                wts = [dst_t(c) for c in cs]
                if (g // 4) % 2 == 0:
                    P.op("act", lambda h, src=src, g=g: h.activation(dst_grp(g), src, AF.Copy), reads=[bt], writes=wts)
                else:
                    P.op("dve", lambda h, src=src, g=g: h.tensor_copy(dst_grp(g), src), reads=[bt], writes=wts)
                continue
            for j, c in enumerate(cs):
                w = min(128, ncols - c * 128)
                eng = "act" if ((g // 4) % 2 == 0) else "dve"
                src = self.psum[0:w, b, j * 128:j * 128 + n]
                if eng == "act":
                    P.op("act", lambda h, c=c, src=src: h.activation(dst(c), src, AF.Copy), reads=[bt], writes=[dst_t(c)])
                else:
                    P.op("dve", lambda h, c=c, src=src: h.tensor_copy(dst(c), src), reads=[bt], writes=[dst_t(c)])

    def build(self):
        nc, P = self.nc, self.P
        din = self.din
        xp = din("xp", [T, D]); xs_own = din("xs_own", [T, D]); xs_all = din("xs_all", [4096, D])
        xs_halo = din("xs_halo", [32, D]); hmask_d = din("hmask", [128, 32])
        cckv = din("cckv", [256, 256]); ckpe = din("ckpe", [256, 64])
        condT_d = din("condT", [128, 16, 2]); par_d = din("params", [128, NPAR])
        ropeq = din("ropeq", [64, 2, T]); ropek = din("ropek", [64, 2, 4096])
        ident_d = din("ident", [128, 128])
        bs_d = din("b_s", [1, 1024]); wsT_d = din("w_sT", [128, 8, 128])
        for l in range(2):
            din(f"mod_w{l}", [D, 6 * D]); din(f"wg{l}", [D, DFF]); din(f"wu{l}", [D, DFF]); din(f"wd{l}", [DFF, D])
        din("ev_w_in", [D, 2880]); din("ev_w_in_sw", [D, 64]); din("ev_w_qb", [512, 1536]); din("ev_w_qb_sw", [512, 512])
        din("ev_w_kvb", [256, 2048]); din("ev_w_o", [D, D]); din("od_w_in", [D, 4096]); din("od_w_o", [D, D])
        yp = self.dout("yp", [T, D]); ys = self.dout("ys", [T, D])
        o_ckv = self.dout("o_ckv", [T, 256]); o_kpe = self.dout("o_kpe", [T, 64])

        t_par, t_id, t_cond = self.Tl("par"), self.Tl("ident"), self.Tl("cond")
        P.dma_multi("sp", [(self.par[:], par_d), (self.identf[:], ident_d), (self.condT[:], condT_d), (self.hmask[:], hmask_d)],
                    "setup", writes=[t_par, t_id, t_cond, self.Tl("hmask")])
        P.op("dve", lambda h: h.tensor_copy(self.identb[:], self.identf[:]), reads=[t_id], writes=[self.Tl("identb")])
        P.op("dve", lambda h: h.memset(self.ones[:], 1.0), writes=[self.Tl("ones")])
        P.op("act", lambda h: h.activation(self.scT[:], self.condT[:], AF.Silu), reads=[t_cond], writes=[self.Tl("scT")])
        self.tiles[("ones",)] = self.Tl("ones")

        self.mods_done = {}
        self.mod_pending = [(0, b_) for b_ in range(8, 24)] + [(1, b_) for b_ in range(24)]
        self.tick_n = 0
        self.tick_every = 1
        import os
        kskip = os.environ.get("KSKIP", "")
        import os
        kpass = os.environ.get("KPASS", "PS")
        kstage = int(os.environ.get("KSTAGE", "9"))
        for ps_ in ("P", "S"):
            if ps_ not in kpass:
                continue
            self.pass_ = ps_
            cond = 0 if ps_ == "P" else 1
            self.fence()
            if "l" not in kskip and not (ps_ == "S" and getattr(self, "preloaded", False)):
                self.load_x(xp if ps_ == "P" else xs_own)
            if "m" not in kskip:
                self.compute_mods(0, 0, 8)
            for l in range(2):
                if kstage >= 1 + 2 * l:
                    if l == 0:
                        self.even_mixer(cond)
                    else:
                        self.odd_mixer(cond)
                if kstage >= 2 + 2 * l:
                    self.compute_mods(l, 12, 24)
                    self.ffn(l, cond)
            if "f" not in kskip:
                nxt = xs_own if (ps_ == "P" and "S" in kpass and "l" not in kskip) else None
                self.final(yp if ps_ == "P" else ys, next_src=nxt)
                self.preloaded = nxt is not None
        P.emit()
        return nc

    def compute_mods(self, l, b0, b1):
        P = self.P
        for blk in range(b0, b1):
            if (l, blk) in self.mods_done:
                continue
            self.mods_done[(l, blk)] = True
            view, tw = self.wload([(0, 16, 512, self.wsrc(f"mod_w{l}", 0, 16, blk * 512, 512))], 512)
            b, bt = self.bank()

            def fn(h, view=view, b=b):
                for f in range(4):
                    for k in range(16):
                        ins = h.matmul(self.psum[:, b, f * 2:f * 2 + 2], view[:, k, f * 128:(f + 1) * 128], self.scT[:, k, :],
                                       start=(k == 0), stop=(k == 15))
                return ins
            P.op("pe", fn, reads=[tw, self.Tl("scT")], writes=[bt])
            tm = self.Tl("mods", l, blk)
            for j in range(2):
                src = self.psum[:, b, 0:8].rearrange("p (f j) -> p f j", j=2)[:, :, j]
                P.op("dve", lambda h, src=src, j=j, blk=blk: h.tensor_tensor(self.mods[l][:, j, blk * 4:blk * 4 + 4], src,
                                                                              self.pc(f"modb{l}", blk * 4, 4), ALU.add),
                     reads=[bt, self.Tl("par")], writes=[tm])
            which, r = divmod(blk, 4)
            if r == 3 and which in (1, 4):
                sub = 0 if which == 1 else 1
                gname = ("nmg" if sub == 0 else "nfg") + str(l)
                rd = [self.Tl("mods", l, which * 4 + i) for i in range(4)] + [self.Tl("par")]
                for j in range(2):
                    P.op("dve", lambda h, j=j, which=which, sub=sub, gname=gname:
                         h.scalar_tensor_tensor(self.acoef[:, l * 4 + sub * 2 + j, :], self.mods[l][:, j, which * 16:(which + 1) * 16], 1.0,
                                                self.pc(gname, 0, 16), ALU.add, ALU.mult),
                         reads=rd, writes=[self.Tl("acoef")])

    def tick(self):
        self.tick_n += 1
        if self.tick_n % self.tick_every:
            return
        while self.mod_pending:
            l, blk = self.mod_pending.pop(0)
            if (l, blk) not in self.mods_done:
                self.compute_mods(l, blk, blk + 1)
                return

    def modcol(self, l, cond, which, c):
        col = which * 16 + c
        return self.mods[l][:, cond, col:col + 1]

    def modtile(self, l, which):
        return [self.Tl("mods", l, which * 4 + i) for i in range(4)]

    def load_x(self, src, chunks=range(8)):
        for n in chunks:
            stg = self.av(32768 + (n % 2) * 8192, F32, 2048)
            st = self.A("stg", n % 2)
            self.load_T(src[n * 128:(n + 1) * 128, :], 128, stg, st,
                        lambda c, n=n: self.xT[:, c, n * 128:(n + 1) * 128],
                        lambda c, n=n: self.Tl("x", c, n // 4), f"stg{n % 2}",
                        dst_grp=lambda g, n=n: self.xT[:, g:g + 4, n * 128:(n + 1) * 128])

    def x_ap(self, c, t0, n):
        return self.xT[:, c, t0:t0 + n]

    def hT_view(self):
        return self.av(0, BF16, 16 * T).rearrange("p (c t) -> p c t", c=16)

    def resid(self, bank_ap, bt, c, half, gate_ap, gate_tiles):
        xs = self.xT[:, c, half * 512:(half + 1) * 512]
        xt = self.Tl("x", c, half)
        self.P.op("dve", lambda h: h.scalar_tensor_tensor(xs, bank_ap, gate_ap, xs, ALU.mult, ALU.add),
                  reads=[bt, xt] + gate_tiles, writes=[xt])

    def ffn(self, l, cond):
        P = self.P
        self.fence()
        hT = self.hT_view()
        self.norm_mod(self.x_ap, lambda c, ti: self.Tl("x", c, ti),
                      lambda c, t0, n: hT[:, c, t0:t0 + n], lambda c, ti: self.A("h", c, ti),
                      [(0, 512), (512, 512)], l * 4 + 2 + cond, lambda c: self.modcol(l, cond, 3, c), rd_extra=self.modtile(l, 3))
        self.tick_every = 2
        act = self.av(32768, BF16, 11 * T).rearrange("p (j t) -> p j t", j=11)
        gate_tiles = self.modtile(l, 5)
        self.nbanks = 8
        for G in range(4):
            j0 = G * 11
            jj = 0
            while jj < 11:
                nj = min(2, 11 - jj)
                ncol = nj * 128
                c0 = (j0 + jj) * 128
                view, tw = self.wload([(0, 16, ncol, self.wsrc(f"wg{l}", 0, 16, c0, ncol)),
                                       (ncol, 16, ncol, self.wsrc(f"wu{l}", 0, 16, c0, ncol))], 2 * ncol)
                for half in range(2):
                    for q in range(nj):
                        hs = [hT[:, k, half * 512:(half + 1) * 512] for k in range(16)]
                        hts = [self.A("h", k, half) for k in range(16)]
                        bg, btg = self.bank()
                        self.mm(self.psum[:, bg, :], [(view[:, k, q * 128:(q + 1) * 128], hs[k]) for k in range(16)], [tw] + hts, btg)
                        bu, btu = self.bank()
                        self.mm(self.psum[:, bu, :], [(view[:, k, ncol + q * 128:ncol + (q + 1) * 128], hs[k]) for k in range(16)], [tw] + hts, btu)
                        rb, rt = self.ringbuf()
                        sg = rb[:, :]
                        P.op("act", lambda h, sg=sg, bg=bg: h.activation(sg, self.psum[:, bg, :], AF.Silu), reads=[btg], writes=[rt])
                        ta = self.A("act", jj + q, half)
                        P.op("dve", lambda h, sg=sg, bu=bu, jq=jj + q, half=half:
                             h.tensor_tensor(act[:, jq, half * 512:(half + 1) * 512], sg, self.psum[:, bu, :], ALU.mult),
                             reads=[rt, btu], writes=[ta])
                jj += nj
                self.tick()
            for cb in range(4):
                view, tw = self.wload([(0, 11, 512, self.wsrc(f"wd{l}", j0 * 128, 11, cb * 512, 512))], 512)
                for cc in range(4):
                    c = cb * 4 + cc
                    for half in range(2):
                        b, bt = self.bank()
                        self.mm(self.psum[:, b, :], [(view[:, j, cc * 128:(cc + 1) * 128], act[:, j, half * 512:(half + 1) * 512]) for j in range(11)],
                                [tw] + [self.A("act", j, half) for j in range(11)], bt)
                        self.resid(self.psum[:, b, :], bt, c, half, self.modcol(l, cond, 5, c), gate_tiles)
                self.tick()
        self.nbanks, self.bn = 4, 0

    def final(self, out, next_src=None):
        P = self.P
        self.fence()
        SB = 5
        for half in range(2):
            bt = self.Tl("ps", SB)
            bap = self.psum[:, SB, :]
            for c in range(16):
                rb, rt = self.ringbuf()
                sq = rb.bitcast(BF16)[:, 0:512]
                P.op("act", lambda h, sq=sq, c=c: h.activation(sq, self.xT[:, c, half * 512:(half + 1) * 512], AF.Square),
                     reads=[self.Tl("x", c, half)], writes=[rt])
                self.mm1(bap, self.ones[:], sq, c == 0, c == 15, [rt], bt)
            rs = self.rstd[half][:, :]
            rst = self.Tl("rstd", half)
            self.make_rstd(rs, rst, bap, bt, 1.0 / D)
            for c in range(16):
                xs = self.xT[:, c, half * 512:(half + 1) * 512]
                P.op("dve", lambda h, xs=xs, c=c: h.scalar_tensor_tensor(xs, xs, self.pc("fng", c), rs, ALU.mult, ALU.mult),
                     reads=[rst, self.Tl("par"), self.Tl("x", c, half)], writes=[self.Tl("x", c, half)])
            for n4 in range(4):
                n = half * 4 + n4
                stg = self.av((n % 2) * 8192, F32, 2048)
                st = self.A("ostg", n % 2)
                for g in range(4):
                    b, bt2 = self.bank()

                    def tr(h, g=g, b=b, n=n):
                        for j in range(4):
                            c = g * 4 + j
                            ins = h.transpose(self.psum[:, b, j * 128:(j + 1) * 128], self.xT[:, c, n * 128:(n + 1) * 128], self.identf[:])
                        return ins
                    P.op("pe", tr, reads=[self.Tl("x", g * 4 + j, half) for j in range(4)] + [self.Tl("ident")], writes=[bt2])
                    dst = stg[:, g * 512:(g + 1) * 512]
                    if g % 2 == 0:
                        P.op("act", lambda h, dst=dst, b=b: h.activation(dst, self.psum[:, b, :], AF.Copy), reads=[bt2], writes=[st])
                    else:
                        P.op("dve", lambda h, dst=dst, b=b: h.tensor_copy(dst, self.psum[:, b, :]), reads=[bt2], writes=[st])
                P.dma("sp", out[n * 128:(n + 1) * 128, :], stg, f"ostg{n % 2}", reads=[st], is_output=True)
            if next_src is not None:
                self.load_x(next_src, range(half * 4, half * 4 + 4))

    def odd_mixer(self, cond):
        P = self.P
        l = 1
        self.compute_mods(1, 0, 12)
        self.fence()
        hT = self.hT_view()
        self.norm_mod(self.x_ap, lambda c, ti: self.Tl("x", c, ti),
                      lambda c, t0, n: hT[:, c, t0:t0 + n], lambda c, ti: self.A("h", c, ti),
                      [(0, 512), (512, 512)], l * 4 + 0 + cond, lambda c: self.modcol(l, cond, 0, c), rd_extra=self.modtile(l, 0))
        self.tick_every = 1
        uT = self.av(32768, BF16, 16 * 512).rearrange("p (c t) -> p c t", c=16)
        vg = self.av(49152, BF16, 4 * 2048).rearrange("p (n f) -> p n f", n=4)
        bsr = self.av(49152, F32, 1024)
        T2 = self.av(65536, F32, 16 * 128).rearrange("p (c t) -> p c t", c=16)
        wsT = self.av(73728, BF16, 1024).rearrange("p (g t) -> p g t", g=8)
        wsTf = self.av(32768, F32, 1024).rearrange("p (g t) -> p g t", g=8)
        st = self.av(75776, F32, 64)
        t_ws, t_wsf, t_bsr, t_T2 = self.A("wsT"), self.A("wsTf"), self.A("bsr"), self.A("T2")
        P.dma("sp", wsTf, self.dram["w_sT"], "osetup1", writes=[t_wsf])
        P.dma("sp", bsr, self.dram["b_s"].partition_broadcast(128), "osetup2", writes=[t_bsr])
        P.op("dve", lambda h: h.tensor_copy(wsT, wsTf), reads=[t_wsf], writes=[t_ws])
        for g in range(8):
            b, bt = self.bank()
            self.mm(self.psum[:, b, 0:128], [(self.ones[:], wsT[:, g, :])], [t_ws], bt)
            for cc in range(2):
                c = g * 2 + cc
                P.op("dve", lambda h, c=c, b=b, g=g: h.scalar_tensor_tensor(T2[:, c, :], self.psum[:, b, 0:128], self.pc("olnb", c),
                                                                         bsr[:, g * 128:(g + 1) * 128], ALU.mult, ALU.add),
                     reads=[bt, t_bsr, self.Tl("par")], writes=[t_T2])
        gate_tiles = self.modtile(l, 2)
        self.fence()
        self.nbanks = 8
        t_ws, t_T2 = self.A("wsT"), self.A("T2")
        def vpart(half):
            hs = [hT[:, k, half * 512:(half + 1) * 512] for k in range(16)]
            hts = [self.A("h", k, half) for k in range(16)]
            t_st = self.A("ost")
            P.op("dve", lambda h: h.memset(st, 0.0), writes=[t_st])
            for cb in range(4):
                self.tick()
                view, tw = self.wload([(0, 16, 512, self.wsrc("od_w_in", 0, 16, 2048 + cb * 512, 512))], 512)
                for n in range(4):
                    tk = half * 512 + n * 128
                    b, bt = self.bank()
                    self.mm(self.psum[:, b, :], [(hT[:, k, tk:tk + 128], view[:, k, :]) for k in range(16)], [tw] + hts, bt)
                    tv = self.A("vg", n, cb)
                    dstv = vg[:, n, cb * 512:(cb + 1) * 512]
                    P.op("act", lambda h, dstv=dstv, b=b, n=n, cb=cb: h.activation(dstv, self.psum[:, b, :], AF.Gelu_apprx_tanh,
                                                                                accum_out=st[:, n * 4 + cb:n * 4 + cb + 1]),
                         reads=[bt], writes=[tv, t_st])
                    rb, rt = self.ringbuf()
                    P.op("act", lambda h, dstv=dstv, rb=rb, n=n, cb=cb: h.activation(rb.bitcast(BF16)[:, 0:512], dstv, AF.Square,
                                                                                  accum_out=st[:, 16 + n * 4 + cb:16 + n * 4 + cb + 1]),
                         reads=[tv], writes=[rt, t_st])
                    yield
            s1 = st[:, 0:16].rearrange("p (n c) -> p n c", c=4)
            s2 = st[:, 16:32].rearrange("p (n c) -> p n c", c=4)
            mean, ex2, var, rsd, nmr = (st[:, 32 + 4 * i:36 + 4 * i] for i in range(5))

            P.op("dve", lambda h: h.tensor_reduce(mean, s1, mybir.AxisListType.X, ALU.add), reads=[t_st], writes=[t_st])
            P.op("dve", lambda h: h.tensor_reduce(ex2, s2, mybir.AxisListType.X, ALU.add), reads=[t_st], writes=[t_st])
            P.op("dve", lambda h: h.tensor_scalar(mean, mean, 1.0 / 2048, None, ALU.mult), reads=[t_st], writes=[t_st])
            P.op("dve", lambda h: h.tensor_tensor(var, mean, mean, ALU.mult), reads=[t_st], writes=[t_st])
            P.op("dve", lambda h: h.scalar_tensor_tensor(var, ex2, 1.0 / 2048, var, ALU.mult, ALU.subtract), reads=[t_st], writes=[t_st])
            P.op("act", lambda h: h.activation(rsd, var, AF.Sqrt, scale=1.0, bias=EPS), reads=[t_st], writes=[t_st])
            P.op("dve", lambda h: h.reciprocal(rsd, rsd), reads=[t_st], writes=[t_st])
            P.op("dve", lambda h: h.scalar_tensor_tensor(nmr, mean, -1.0, rsd, ALU.mult, ALU.mult), reads=[t_st], writes=[t_st])
            for n in range(4):
                tvs = [self.A("vg", n, cb) for cb in range(4)]
                P.op("act", lambda h, n=n: h.activation(vg[:, n, :], vg[:, n, :], AF.Identity, scale=rsd[:, n:n + 1], bias=nmr[:, n:n + 1]),
                     reads=tvs + [t_st], writes=tvs)
        def upart(half):
            hs = [hT[:, k, half * 512:(half + 1) * 512] for k in range(16)]
            hts = [self.A("h", k, half) for k in range(16)]
            for cb in range(4):
                self.tick()
                view, tw = self.wload([(0, 16, 512, self.wsrc("od_w_in", 0, 16, cb * 512, 512))], 512)
                for cc in range(4):
                    c = cb * 4 + cc
                    b, bt = self.bank()
                    self.mm(self.psum[:, b, :], [(view[:, k, cc * 128:(cc + 1) * 128], hs[k]) for k in range(16)], [tw] + hts, bt)
                    P.op("act", lambda h, c=c, b=b: h.activation(uT[:, c, :], self.psum[:, b, :], AF.Gelu_apprx_tanh),
                         reads=[bt], writes=[self.A("u", c)])
        def sppart(half):
            for c4 in range(4):
                for n in range(4):
                    tvs = [self.A("vg", n, c4)]
                    b, bt = self.bank()

                    def sp(h, n=n, c4=c4, b=b):
                        for j in range(4):
                            c = c4 * 4 + j
                            ins = h.matmul(self.psum[:, b, j * 128:(j + 1) * 128], vg[:, n, c * 128:(c + 1) * 128], wsT[:, c // 2, :],
                                           start=True, stop=True)
                        return ins
                    P.op("pe", sp, reads=tvs + [t_ws], writes=[bt])
                    for j in range(4):
                        c = c4 * 4 + j
                        rb, rt = self.ringbuf()
                        tmp = rb[:, 0:128]
                        P.op("dve", lambda h, tmp=tmp, b=b, j=j, c=c: h.scalar_tensor_tensor(tmp, self.psum[:, b, j * 128:(j + 1) * 128], self.pc("olng", c),
                                                                                          T2[:, c, :], ALU.mult, ALU.add),
                             reads=[bt, t_T2, self.Tl("par")], writes=[rt])
                        us = uT[:, c, n * 128:(n + 1) * 128]
                        P.op("dve", lambda h, tmp=tmp, us=us: h.tensor_tensor(us, us, tmp, ALU.mult),
                             reads=[rt, self.A("u", c)], writes=[self.A("u", c)])
                    yield
        def wopart(half):
            for cb in range(4):
                self.tick()
                view, tw = self.wload([(0, 16, 512, self.wsrc("od_w_o", 0, 16, cb * 512, 512))], 512)
                for cc in range(4):
                    c = cb * 4 + cc
                    b, bt = self.bank()
                    self.mm(self.psum[:, b, :], [(view[:, k, cc * 128:(cc + 1) * 128], uT[:, k, :]) for k in range(16)],
                            [tw] + [self.A("u", k) for k in range(16)], bt)
                    self.resid(self.psum[:, b, :], bt, c, half, self.modcol(l, cond, 2, c), gate_tiles)
        def drain(*gens):
            gens = list(gens)
            while gens:
                for g_ in list(gens):
                    try:
                        next(g_)
                    except StopIteration:
                        gens.remove(g_)
        drain(vpart(0))
        upart(0)
        drain(sppart(0), vpart(1))
        wopart(0)
        upart(1)
        drain(sppart(1))
        wopart(1)
        self.nbanks, self.bn = 4, 0

    def even_mixer(self, cond):
        P = self.P
        l = 0
        isS = self.pass_ == "S"
        self.fence()
        if isS:
            xh = self.av(71168, F32, 16 * 32).rearrange("p (c t) -> p c t", c=16)
            hh = self.av(73216, BF16, 16 * 32).rearrange("p (c t) -> p c t", c=16)
            stg = self.av(0, F32, 2048)
            self.load_T(self.dram["xs_halo"], 32, stg, self.A("hstg"), lambda c: xh[:, c, :], lambda c: self.A("xh", c), "hstg",
                        dst_grp=lambda g: xh[:, g:g + 4, :])
            self.norm_mod(lambda c, t0, n: xh[:, c, :], lambda c, ti: self.A("xh", c),
                          lambda c, t0, n: hh[:, c, :], lambda c, ti: self.A("hh", c), [(0, 32)], cond,
                          lambda c: self.modcol(l, cond, 0, c), rd_extra=self.modtile(l, 0))
            self.fence()
        hT = self.hT_view()
        self.norm_mod(self.x_ap, lambda c, ti: self.Tl("x", c, ti),
                      lambda c, t0, n: hT[:, c, t0:t0 + n], lambda c, ti: self.A("h", c, ti),
                      [(0, 512), (512, 512)], l * 4 + 0 + cond, lambda c: self.modcol(l, cond, 0, c), rd_extra=self.modtile(l, 0))
        self.tick_every = 1
        NK = NKS if isS else T
        if isS:
            ckv_all = self.av(32768, BF16, 2 * NKS).rearrange("p (c t) -> p c t", c=2)
            kpe_all = self.av(50176, BF16, NKS)
        else:
            ckv_all = self.av(32768, BF16, 2 * T).rearrange("p (c t) -> p c t", c=2)
            kpe_all = self.av(36864, BF16, T)
        qg = self.av(58880, BF16, 4 * T).rearrange("p (c t) -> p c t", c=4)
        rstd_q = self.av(67072, F32, T)
        if not isS:
            self.kside(lambda k, t0, n: hT[:, k, t0:t0 + n], lambda k, ti: self.A("h", k, ti), [(0, 512), (512, 512)],
                       ckv_all, kpe_all, 0, cond, state_out=True)
        if isS:
            PADW = 1054
            a_pad = self.av(38912, BF16, 8 * PADW).rearrange("p (c t) -> p c t", c=8)
        else:
            PADW = 4 * 286
            a_pad = self.av(38912, BF16, 8 * PADW).rearrange("p (c t) -> p c t", c=8)
            ap4 = self.av(38912, BF16, 8 * PADW).rearrange("p (c s t) -> p c s t", c=8, s=4)
            P.op("dve", lambda h: h.memset(a_pad[:, :, :], 0.0), writes=[self.A("apad", c) for c in range(8)])
        for cp in range(4):
            view, tw = self.wload([(0, 16, 256, self.wsrc("ev_w_in", 0, 16, cp * 256, 256)),
                                   (256, 16, 256, self.wsrc("ev_w_in", 0, 16, 1024 + cp * 256, 256))], 512)
            for q in range(2):
                c = cp * 2 + q
                tiles_ = [(0, 512, 0), (512, 512, 1)] + ([(0, 32, 2)] if isS else [])
                for (t0, n, kind) in tiles_:
                    if kind == 2:
                        hs = [hh[:, k, :] for k in range(16)]
                        hts = [self.A("hh", k) for k in range(16)]
                    else:
                        hs = [hT[:, k, t0:t0 + n] for k in range(16)]
                        hts = [self.A("h", k, kind) for k in range(16)]
                    bv, btv = self.bank()
                    self.mm(self.psum[:, bv, 0:n], [(view[:, k, q * 128:(q + 1) * 128], hs[k]) for k in range(16)], [tw] + hts, btv)
                    bg, btg = self.bank()
                    self.mm(self.psum[:, bg, 0:n], [(view[:, k, 256 + q * 128:256 + (q + 1) * 128], hs[k]) for k in range(16)], [tw] + hts, btg)
                    rb, rt = self.ringbuf()
                    sg = rb[:, 0:n]
                    P.op("act", lambda h, sg=sg, bg=bg, n=n: h.activation(sg, self.psum[:, bg, 0:n], AF.Sigmoid), reads=[btg], writes=[rt])
                    ta = self.A("apad", c)
                    if kind == 2:
                        P.op("dve", lambda h, sg=sg: h.tensor_tensor(sg, sg, self.hmask[:, :], ALU.mult), reads=[rt, self.Tl("hmask")], writes=[rt])
                        P.op("dve", lambda h, sg=sg, bv=bv, c=c: h.tensor_tensor(a_pad[:, c, 0:15], sg[:, 0:15], self.psum[:, bv, 0:15], ALU.mult),
                             reads=[rt, btv], writes=[ta])
                        P.op("dve", lambda h, sg=sg, bv=bv, c=c: h.tensor_tensor(a_pad[:, c, 1039:1054], sg[:, 15:30], self.psum[:, bv, 15:30], ALU.mult),
                             reads=[rt, btv], writes=[ta])
                    elif isS:
                        P.op("dve", lambda h, sg=sg, bv=bv, c=c, t0=t0: h.tensor_tensor(a_pad[:, c, 15 + t0:15 + t0 + 512], sg, self.psum[:, bv, :], ALU.mult),
                             reads=[rt, btv], writes=[ta])
                    else:
                        s0 = t0 // 256
                        P.op("dve", lambda h, sg=sg, bv=bv, c=c, s0=s0: h.tensor_tensor(ap4[:, c, s0:s0 + 2, 15:271],
                                                                                     sg.rearrange("p (s t) -> p s t", s=2),
                                                                                     self.psum[:, bv, :].rearrange("p (s t) -> p s t", s=2), ALU.mult),
                             reads=[rt, btv], writes=[ta])
            self.tick()
        view, tw = self.wload([(0, 16, 512, self.wsrc("ev_w_in", 0, 16, 2048, 512))], 512)
        SB = 5
        for half in range(2):
            hs = [hT[:, k, half * 512:(half + 1) * 512] for k in range(16)]
            hts = [self.A("h", k, half) for k in range(16)]
            sbt = self.Tl("ps", SB)
            for c in range(4):
                b, bt = self.bank()
                self.mm(self.psum[:, b, :], [(view[:, k, c * 128:(c + 1) * 128], hs[k]) for k in range(16)], [tw] + hts, bt)
                rb, rt = self.ringbuf()
                sq = rb.bitcast(BF16)[:, 0:512]
                P.op("act", lambda h, sq=sq, b=b: h.activation(sq, self.psum[:, b, :], AF.Square), reads=[bt], writes=[rt])
                self.mm1(self.psum[:, SB, :], self.ones[:], sq, c == 0, c == 3, [rt], sbt)
                P.op("dve", lambda h, b=b, c=c, half=half: h.tensor_scalar(qg[:, c, half * 512:(half + 1) * 512], self.psum[:, b, :],
                                                                          self.pc("qng", c), None, ALU.mult),
                     reads=[bt, self.Tl("par")], writes=[self.A("qg", c, half)])
            self.make_rstd(rstd_q[:, half * 512:(half + 1) * 512], self.A("rstdq", half), self.psum[:, SB, :], sbt, 1.0 / 512)
        self.fence()
        y = self.av(0, F32, 8 * T).rearrange("p (c t) -> p c t", c=8)
        diag = self.av(71168, BF16, 31 * 128).rearrange("p (k t) -> p k t", k=31)
        cw = self.pc("convw", 0, 248).rearrange("p (c k) -> p c k", c=8)
        S1, S2 = 4, 5
        for c in range(8):
            t_dgs = [self.A("diag", 0), self.A("diag", 1)]
            for k in range(31):
                P.op("dve", lambda h, c=c, k=k: h.tensor_scalar(diag[:, k, :], self.identb[:], cw[:, c, k:k + 1], None, ALU.mult),
                     reads=[self.Tl("identb"), self.Tl("par")], writes=[t_dgs[k // 16]])
            if isS:
                units = [(half * 512, 512, a_pad[:, c, half * 512:half * 512 + 542]) for half in range(2)]
            else:
                units = [(s * 256, 256, a_pad[:, c, s * 286:(s + 1) * 286]) for s in range(4)]
            if c % 2 == 1:
                self.tick()
            ubanks = [self.bank() for _ in units]
            for (t0, n, win), (b, bt) in zip(units, ubanks):
                self.mm(self.psum[:, b, 0:n], [(diag[:, k, :], win[:, k:k + n]) for k in range(16)], [t_dgs[0], self.A("apad", c)], bt, last=False)
            for (t0, n, win), (b, bt) in zip(units, ubanks):
                self.mm(self.psum[:, b, 0:n], [(diag[:, k, :], win[:, k:k + n]) for k in range(16, 31)], [t_dgs[1], self.A("apad", c)], bt, first=False)
                P.op("act", lambda h, b=b, c=c, t0=t0, n=n: h.activation(y[:, c, t0:t0 + n], self.psum[:, b, 0:n], AF.Identity,
                                                                      bias=self.pc("convb", c)),
                     reads=[bt, self.Tl("par")], writes=[self.A("y", c, t0 // 512)])
        self.fence()
        aT = self.av(38912, BF16, 8 * T).rearrange("p (c t) -> p c t", c=8)
        for half in range(2):
            t1, t2 = self.Tl("ps", S1), self.Tl("ps", S2)
            for c in range(8):
                ys_ = y[:, c, half * 512:(half + 1) * 512]
                rb, rt = self.ringbuf()
                yb = rb.bitcast(BF16)[:, 0:512]
                P.op("dve", lambda h, yb=yb, ys_=ys_: h.tensor_copy(yb, ys_), reads=[self.A("y", c, half)], writes=[rt])
                self.mm1(self.psum[:, S1, :], self.ones[:], yb, c == 0, c == 7, [rt], t1)
                rb2, rt2 = self.ringbuf()
                sq = rb2.bitcast(BF16)[:, 0:512]
                P.op("act", lambda h, sq=sq, ys_=ys_: h.activation(sq, ys_, AF.Square), reads=[self.A("y", c, half)], writes=[rt2])
                self.mm1(self.psum[:, S2, :], self.ones[:], sq, c == 0, c == 7, [rt2], t2)
            mean = self.rstd[0][:, :]
            rs = self.rstd[1][:, :]
            tm, tr_ = self.Tl("rstd", 0), self.Tl("rstd", 1)
            P.op("act", lambda h: h.activation(mean, self.psum[:, S1, :], AF.Copy, scale=1.0 / 1024), reads=[t1], writes=[tm])
            rb, rt = self.ringbuf()
            msq = rb[:, :]
            P.op("dve", lambda h, msq=msq: h.tensor_tensor(msq, mean, mean, ALU.mult), reads=[tm], writes=[rt])
            P.op("dve", lambda h, msq=msq: h.scalar_tensor_tensor(rs, self.psum[:, S2, :], 1.0 / 1024, msq, ALU.mult, ALU.subtract),
                 reads=[t2, rt], writes=[tr_])
            P.op("act", lambda h: h.activation(rs, rs, AF.Sqrt, scale=1.0, bias=EPS), reads=[tr_], writes=[tr_])
            P.op("dve", lambda h: h.reciprocal(rs, rs), reads=[tr_], writes=[tr_])
            for c in range(8):
                ys_ = y[:, c, half * 512:(half + 1) * 512]
                rb, rt = self.ringbuf()
                tmp = rb[:, :]
                P.op("dve", lambda h, tmp=tmp, ys_=ys_: h.tensor_tensor(tmp, ys_, mean, ALU.subtract), reads=[self.A("y", c, half), tm], writes=[rt])
                P.op("dve", lambda h, tmp=tmp: h.tensor_tensor(tmp, tmp, rs, ALU.mult), reads=[rt, tr_], writes=[rt])
                P.op("act", lambda h, tmp=tmp, c=c, half=half: h.activation(aT[:, c, half * 512:(half + 1) * 512], tmp, AF.Silu,
                                                                         scale=self.pc("clng", c), bias=self.pc("clnb", c)),
                     reads=[rt, self.Tl("par")], writes=[self.A("aT", c, half)])
        self.compute_mods(0, 8, 12)
        gate_tiles = self.modtile(l, 2)
        for half in range(2):
          for cb in range(4):
            view, tw = self.wload([(0, 8, 512, self.wsrc("ev_w_o", 0, 8, cb * 512, 512))], 512)
            for cc in range(4):
                c = cb * 4 + cc
                if True:
                    b, bt = self.bank()
                    self.mm(self.psum[:, b, :], [(view[:, k, cc * 128:(cc + 1) * 128], aT[:, k, half * 512:(half + 1) * 512]) for k in range(8)],
                            [tw] + [self.A("aT", k, half) for k in range(8)], bt)
                    self.resid(self.psum[:, b, :], bt, c, half, self.modcol(l, cond, 2, c), gate_tiles)
        if isS:
            self.fence()
            self.kside_sample(ckv_all, kpe_all, cond)
        self.fence()
        self.attention(ckv_all, kpe_all, qg, rstd_q, NK)
        attT = self.av(0, BF16, 8 * T).rearrange("p (c t) -> p c t", c=8)
        for cb in range(4):
            view, tw = self.wload([(0, 8, 512, self.wsrc("ev_w_o", 1024, 8, cb * 512, 512))], 512)
            for cc in range(4):
                c = cb * 4 + cc
                for half in range(2):
                    b, bt = self.bank()
                    self.mm(self.psum[:, b, :], [(view[:, k, cc * 128:(cc + 1) * 128], attT[:, k, half * 512:(half + 1) * 512]) for k in range(8)],
                            [tw] + [self.A("att", k, half) for k in range(8)], bt)
                    self.resid(self.psum[:, b, :], bt, c, half, self.modcol(l, cond, 2, c), gate_tiles)

    def kside(self, hsrc, htile, ntiles, ckv_all, kpe_all, koff, cond, state_out=False, rope_tok0=None, rs_idx=None):
        P = self.P
        SB = 4
        for ti, (t0, n) in enumerate(ntiles):
            view, tw = self.wload([(0, 16, 320, self.wsrc("ev_w_in", 0, 16, 2560, 320)),
                                   (320, 16, 64, self.wsrc("ev_w_in_sw", 0, 16, 0, 64))], 384)
            hs = [hsrc(k, t0, n) for k in range(16)]
            hts = [htile(k, ti) for k in range(16)]
            kb = []
            sbt = self.Tl("ps", SB)
            for c in range(2):
                b, bt = self.bank()
                self.mm(self.psum[:, b, 0:n], [(view[:, k, c * 128:(c + 1) * 128], hs[k]) for k in range(16)], [tw] + hts, bt)
                kb.append((b, bt))
                rb, rt = self.ringbuf()
                sq = rb.bitcast(BF16)[:, 0:n]
                P.op("act", lambda h, sq=sq, b=b, n=n: h.activation(sq, self.psum[:, b, 0:n], AF.Square), reads=[bt], writes=[rt])
                self.mm1(self.psum[:, SB, 0:n], self.ones[:], sq, c == 0, c == 1, [rt], sbt)
            if rs_idx is None:
                rs = self.rstd[ti % 2][:, 0:n]
                rst = self.Tl("rstd", ti % 2)
            else:
                rs = self.rstd[rs_idx][:, 256:256 + n]
                rst = self.Tl("rstdk", rs_idx)
            self.make_rstd(rs, rst, self.psum[:, SB, 0:n], sbt, 1.0 / 256)
            kt = self.A("kall", (koff + t0) // 512)
            if state_out:
                ckf = self.av(71168, F32, 2 * 512).rearrange("p (c t) -> p c t", c=2)
                kpf = self.av(75264, F32, 512)
            for c in range(2):
                b, bt = kb[c]
                if state_out:
                    P.op("dve", lambda h, b=b, c=c, n=n: h.scalar_tensor_tensor(ckf[:, c, 0:n], self.psum[:, b, 0:n], self.pc("kvng", c), rs, ALU.mult, ALU.mult),
                         reads=[bt, rst, self.Tl("par")], writes=[self.A("ckf", c)])
                    P.op("act", lambda h, c=c, n=n, t0=t0: h.activation(ckv_all[:, c, koff + t0:koff + t0 + n], ckf[:, c, 0:n], AF.Copy),
                         reads=[self.A("ckf", c)], writes=[kt])
                else:
                    P.op("dve", lambda h, b=b, c=c, n=n, t0=t0: h.scalar_tensor_tensor(ckv_all[:, c, koff + t0:koff + t0 + n], self.psum[:, b, 0:n],
                                                                                    self.pc("kvng", c), rs, ALU.mult, ALU.mult),
                         reads=[bt, rst, self.Tl("par")], writes=[kt])
            b, bt = self.bank()
            self.mm(self.psum[0:64, b, 0:n], [(view[:, k, 256:320], hs[k]) for k in range(16)], [tw] + hts, bt)
            if rope_tok0 is None:
                if state_out:
                    P.op("act", lambda h, b=b, n=n: h.activation(kpf[0:64, 0:n], self.psum[0:64, b, 0:n], AF.Copy), reads=[bt], writes=[self.A("kpf")])
                P.op("dve", lambda h, b=b, n=n, t0=t0: h.tensor_copy(kpe_all[0:64, koff + t0:koff + t0 + n], self.psum[0:64, b, 0:n]), reads=[bt], writes=[kt])
            else:
                b2, bt2 = self.bank()
                self.mm(self.psum[0:64, b2, 0:n], [(view[:, k, 320:384], hs[k]) for k in range(16)], [tw] + hts, bt2)
                rk = self.av(71168, F32, 2 * 512).rearrange("p (j t) -> p j t", j=2)
                trk = self.A("ropek")
                P.dma("sp", rk[0:64, :, 0:n], self.dram["ropek"][:, :, rope_tok0 + t0:rope_tok0 + t0 + n], "ropek", writes=[trk])
                rb, rt = self.ringbuf()
                rb2, rt2 = self.ringbuf()
                P.op("dve", lambda h, rb=rb, b=b, n=n: h.tensor_tensor(rb[0:64, 0:n], self.psum[0:64, b, 0:n], rk[0:64, 0, 0:n], ALU.mult), reads=[bt, trk], writes=[rt])
                P.op("dve", lambda h, rb2=rb2, b2=b2, n=n: h.tensor_tensor(rb2[0:64, 0:n], self.psum[0:64, b2, 0:n], rk[0:64, 1, 0:n], ALU.mult), reads=[bt2, trk], writes=[rt2])
                P.op("dve", lambda h, rb=rb, rb2=rb2, n=n, t0=t0: h.tensor_tensor(kpe_all[0:64, koff + t0:koff + t0 + n], rb[0:64, 0:n], rb2[0:64, 0:n], ALU.add),
                     reads=[rt, rt2], writes=[kt])
            if state_out:
                so = self.av(77312, F32, 320)
                for n4 in range(n // 128):
                    b, bt = self.bank()
                    tso = self.A("so")

                    def tr(h, b=b, n4=n4):
                        for c in range(2):
                            h.transpose(self.psum[:, b, c * 128:(c + 1) * 128], ckf[:, c, n4 * 128:(n4 + 1) * 128], self.identf[:])
                        return h.transpose(self.psum[:, b, 256:320], kpf[0:64, n4 * 128:(n4 + 1) * 128], self.identf[0:64, 0:64])
                    P.op("pe", tr, reads=[self.A("ckf", 0), self.A("ckf", 1), self.A("kpf"), self.Tl("ident")], writes=[bt])
                    P.op("dve", lambda h, b=b: h.tensor_copy(so[:, 0:320], self.psum[:, b, 0:320]), reads=[bt], writes=[tso])
                    r0 = t0 + n4 * 128
                    P.dma("sp", self.dram["o_ckv"][r0:r0 + 128, :], so[:, 0:256], "so", reads=[tso], is_output=True)
                    P.dma("sp", self.dram["o_kpe"][r0:r0 + 128, :], so[:, 256:320], "so", reads=[tso], is_output=True)

    def kside_sample(self, ckv_all, kpe_all, cond):
        P = self.P
        l = 0
        stg = self.av(0, F32, 2048)
        for n in range(2):
            tk = self.A("kall", 0)
            self.load_T(self.dram["cckv"][n * 128:(n + 1) * 128, :], 128, stg[:, 0:256], self.A("kstg", 0),
                        lambda c, n=n: ckv_all[:, c, n * 128:(n + 1) * 128], lambda c: tk, "kstg0", ncols=256)
            self.load_T(self.dram["ckpe"][n * 128:(n + 1) * 128, :], 128, stg[:, 256:320], self.A("kstg", 0),
                        lambda c, n=n: kpe_all[0:64, n * 128:(n + 1) * 128], lambda c: tk, "kstg0", ncols=64)
        stgs = [self.av(i * 8192, F32, 2048) for i in range(2)]
        xns = [self.av(16384 + i * 4096, BF16, 2048) for i in range(2)]
        hts_ = [self.av(24576 + i * 4096, BF16, 16 * 128).rearrange("p (c t) -> p c t", c=16) for i in range(2)]
        ssb = self.av(75264, F32, 8)
        psb = self.psum[:, :, :].bitcast(BF16)

        def s1(tt):
            pi = tt % 2
            st_, xn = stgs[pi], xns[pi]
            tst, txn, tss = self.A("kstg", pi), self.A("kxn", pi), self.A("kss", pi)
            P.dma("sp", st_, self.dram["xs_all"][tt * 128:(tt + 1) * 128, :], f"kstg{pi}", writes=[tst])
            P.op("dve", lambda h: h.memset(ssb[:, pi:pi + 1], 0.0), writes=[tss])
            P.op("act", lambda h: h.activation(xn, st_, AF.Square, accum_out=ssb[:, pi:pi + 1]), reads=[tst], writes=[txn, tss])
            P.op("act", lambda h: h.activation(ssb[:, 2 + pi:3 + pi], ssb[:, pi:pi + 1], AF.Sqrt, scale=1.0 / D, bias=EPS), reads=[tss], writes=[tss])
            P.op("dve", lambda h: h.reciprocal(ssb[:, 2 + pi:3 + pi], ssb[:, 2 + pi:3 + pi]), reads=[tss], writes=[tss])
            P.op("dve", lambda h: h.tensor_scalar(xn, st_, ssb[:, 2 + pi:3 + pi], None, ALU.mult), reads=[tst, tss], writes=[txn])

        def s2(tt):
            pi = tt % 2
            xn, ht_ = xns[pi], hts_[pi]
            txn = self.A("kxn", pi)
            rd = [self.Tl("acoef")] + self.modtile(l, 0)
            for g in range(4):
                b, bt = self.bank()

                def tr(h, g=g, b=b):
                    for j in range(4):
                        c = g * 4 + j
                        ins = h.transpose(psb[:, b, j * 128:(j + 1) * 128], xn[:, c * 128:(c + 1) * 128], self.identb[:])
                    return ins
                P.op("pe", tr, reads=[txn, self.Tl("identb")], writes=[bt])
                for j in range(4):
                    c = g * 4 + j
                    src = psb[:, b, j * 128:(j + 1) * 128]
                    if g % 2 == 0:
                        P.op("act", lambda h, c=c, src=src: h.activation(ht_[:, c, :], src, AF.Identity, scale=self.acoef[:, cond, c:c + 1],
                                                                     bias=self.modcol(l, cond, 0, c)),
                             reads=[bt] + rd, writes=[self.A("kh", pi, c)])
                    else:
                        P.op("dve", lambda h, c=c, src=src: h.tensor_scalar(ht_[:, c, :], src, self.acoef[:, cond, c:c + 1], self.modcol(l, cond, 0, c),
                                                                        ALU.mult, ALU.add),
                             reads=[bt] + rd, writes=[self.A("kh", pi, c)])

        def s3(tt):
            pi = tt % 2
            ht_ = hts_[pi]
            self.kside(lambda k, t0, n, ht_=ht_: ht_[:, k, :], lambda k, ti, pi=pi: self.A("kh", pi, k), [(0, 128)], ckv_all, kpe_all,
                       256 + tt * 128, cond, rope_tok0=tt * 128, rs_idx=pi)
        for step in range(32 + 2):
            if step < 32:
                s1(step)
            if 0 <= step - 1 < 32:
                s2(step - 1)
            if 0 <= step - 2 < 32:
                s3(step - 2)

    def attention(self, ckv_all, kpe_all, qg, rstd_q, NK):
        P = self.P
        isS = self.pass_ == "S"
        attT = self.av(0, BF16, 8 * T).rearrange("p (c t) -> p c t", c=8)
        PT = [self.av(16384 + i * 1024, BF16, 512) for i in range(3)]
        qn_h = self.av(19456, BF16, T)
        qpe_h = self.av(21504, BF16, T)
        wkvs = [self.av(o_, BF16, 512).rearrange("p (k f) -> p k f", k=2) for o_ in (23552, 75776)]
        wqs = [self.av(o_, BF16, 1024).rearrange("p (k f) -> p k f", k=4) for o_ in (24576, 76800)]

        def load_head_w(hd):
            si = hd % 2
            pairs = [(wkvs[si], self.wsrc("ev_w_kvb", 0, 2, hd * 256, 256)),
                     (wqs[si][:, :, 0:192], self.wsrc("ev_w_qb", 0, 4, hd * 192, 192))]
            if isS:
                pairs.append((wqs[si][:, :, 192:256], self.wsrc("ev_w_qb_sw", 0, 4, hd * 64, 64)))
            P.dma_multi("pool", pairs, f"whd{si}", writes=[self.A("whd", si)])
        nkc = NK // 128
        if isS:
            crs = [self.av(71168, F32, T), self.av(26624, F32, T)]
            tcr = self.A("crs")
            P.dma_multi("sp", [(crs[j][0:64, :], self.dram["ropeq"][:, j, :]) for j in range(2)], "ropeq", writes=[tcr])
            for j in range(2):
                for half in range(2):
                    sl = crs[j][0:64, half * 512:(half + 1) * 512]
                    P.op("dve", lambda h, sl=sl, half=half: h.tensor_tensor(sl, sl, rstd_q[0:64, half * 512:(half + 1) * 512], ALU.mult),
                         reads=[tcr, self.A("rstdq", half)], writes=[tcr])
        unit_i = 0
        kts = [self.A("kall", i) for i in range((NK + 511) // 512)]
        load_head_w(0)
        for hd in range(8):
            wkv, wq = wkvs[hd % 2], wqs[hd % 2]
            twk = twq = self.A("whd", hd % 2)
            if hd + 1 < 8:
                load_head_w(hd + 1)
            s = self.wn
            self.wn = (self.wn + 1) % 3
            slot_t = self.Tl("w", s)
            n512 = (NK + 511) // 512

            def sub(nm, slot_t=slot_t):
                t = Tile(nm)
                t.w = list(slot_t.w)
                t.r = list(slot_t.r)
                return t
            tK = [sub(("K", i)) for i in range(n512)]
            tV = [sub(("V", i)) for i in range(n512)]
            KT = self.wslot[s][:, 0:NK]
            V = self.wslot[s][:, NK:2 * NK].rearrange("p (n d) -> p n d", d=128)
            for i, k0 in enumerate(range(0, NK, 512)):
                n = min(512, NK - k0)
                b, bt = self.bank()
                self.mm(self.psum[:, b, 0:n], [(wkv[:, k, 0:128], ckv_all[:, k, k0:k0 + n]) for k in range(2)], [twk] + kts, bt)
                if i % 2 == 0:
                    P.op("act", lambda h, b=b, k0=k0, n=n: h.activation(KT[:, k0:k0 + n], self.psum[:, b, 0:n], AF.Copy), reads=[bt], writes=[tK[i]])
                else:
                    P.op("dve", lambda h, b=b, k0=k0, n=n: h.tensor_copy(KT[:, k0:k0 + n], self.psum[:, b, 0:n]), reads=[bt], writes=[tK[i]])
                b, bt = self.bank()
                nch = n // 128

                def vf(h, b=b, k0=k0, nch=nch):
                    for j in range(nch):
                        for k in range(2):
                            ins = h.matmul(self.psum[:, b, j * 128:(j + 1) * 128], ckv_all[:, k, k0 + j * 128:k0 + (j + 1) * 128], wkv[:, k, 128:256],
                                           start=(k == 0), stop=(k == 1))
                    return ins
                P.op("pe", vf, reads=[twk] + kts, writes=[bt])
                vdst = V[:, k0 // 128:k0 // 128 + nch, :]
                vsrc = self.psum[:, b, 0:n].rearrange("p (n d) -> p n d", d=128)
                if i % 2 == 1:
                    P.op("act", lambda h, vdst=vdst, vsrc=vsrc: h.activation(vdst, vsrc, AF.Copy), reads=[bt], writes=[tV[i]])
                else:
                    P.op("dve", lambda h, vdst=vdst, vsrc=vsrc: h.tensor_copy(vdst, vsrc), reads=[bt], writes=[tV[i]])
            for half in range(2):
                qs = [qg[:, k, half * 512:(half + 1) * 512] for k in range(4)]
                qts = [self.A("qg", k, half) for k in range(4)]
                b, bt = self.bank()
                self.mm(self.psum[:, b, :], [(wq[:, k, 0:128], qs[k]) for k in range(4)], [twq] + qts, bt)
                P.op("dve", lambda h, b=b, half=half: h.tensor_tensor(qn_h[:, half * 512:(half + 1) * 512], self.psum[:, b, :],
                                                                     rstd_q[:, half * 512:(half + 1) * 512], ALU.mult),
                     reads=[bt, self.A("rstdq", half)], writes=[self.A("qn", half)])
                b, bt = self.bank()
                self.mm(self.psum[0:64, b, :], [(wq[:, k, 128:192], qs[k]) for k in range(4)], [twq] + qts, bt)
                if not isS:
                    P.op("dve", lambda h, b=b, half=half: h.tensor_tensor(qpe_h[0:64, half * 512:(half + 1) * 512], self.psum[0:64, b, :],
                                                                         rstd_q[0:64, half * 512:(half + 1) * 512], ALU.mult),
                         reads=[bt, self.A("rstdq", half)], writes=[self.A("qpe", half)])
                else:
                    b2, bt2 = self.bank()
                    self.mm(self.psum[0:64, b2, :], [(wq[:, k, 192:256], qs[k]) for k in range(4)], [twq] + qts, bt2)
                    rb, rt = self.ringbuf()
                    rb2, rt2 = self.ringbuf()
                    P.op("dve", lambda h, rb=rb, b=b, half=half: h.tensor_tensor(rb[0:64, :], self.psum[0:64, b, :], crs[0][0:64, half * 512:(half + 1) * 512], ALU.mult),
                         reads=[bt, tcr], writes=[rt])
                    P.op("dve", lambda h, rb2=rb2, b2=b2, half=half: h.tensor_tensor(rb2[0:64, :], self.psum[0:64, b2, :], crs[1][0:64, half * 512:(half + 1) * 512], ALU.mult),
                         reads=[bt2, tcr], writes=[rt2])
                    P.op("dve", lambda h, rb=rb, rb2=rb2, half=half: h.tensor_tensor(qpe_h[0:64, half * 512:(half + 1) * 512], rb[0:64, :], rb2[0:64, :], ALU.add),
                         reads=[rt, rt2], writes=[self.A("qpe", half)])
            if isS:
                units = [(half * 512, 512, list(range(nkc)), half) for half in range(2)]
            else:
                units = [(s_ * 256, 256, [2 * s_, 2 * s_ + 1], s_ // 2) for s_ in range(4)]
            for (q0, nq, kcs, half) in units:
                OB, SBK = (6, 7) if unit_i % 2 == 0 else (4, 5)
                unit_i += 1
                qrd = [self.A("qn", half), self.A("qpe", half)]
                tO, tS = self.Tl("ps", OB), self.Tl("ps", SBK)
                pend = []

                def emit_qk(j):
                    kc = kcs[j]
                    b, bt = self.bank()
                    self.mm(self.psum[:, b, 0:nq], [(KT[:, kc * 128:(kc + 1) * 128], qn_h[:, q0:q0 + nq]),
                                                    (kpe_all[0:64, kc * 128:(kc + 1) * 128], qpe_h[0:64, q0:q0 + nq])],
                            qrd + [kts[kc // 4], tK[kc // 4]], bt)
                    pend.append((b, bt))
                nj = len(kcs)
                for j in range(min(2, nj)):
                    emit_qk(j)
                for j in range(nj):
                    b, bt = pend[j]
                    pi = j % 3
                    tp = self.A("PT", pi)
                    P.op("act", lambda h, b=b, pi=pi: h.activation(PT[pi][:, 0:nq], self.psum[:, b, 0:nq], AF.Exp, scale=SCALE), reads=[bt], writes=[tp])
                    if j + 2 < nj:
                        emit_qk(j + 2)
                    kc = kcs[j]
                    self.mm1(self.psum[:, OB, 0:nq], V[:, kc, :], PT[pi][:, 0:nq], j == 0, j == nj - 1, [tp, tV[kc // 4]], tO)
                    self.mm1(self.psum[:, SBK, 0:nq], self.ones[:], PT[pi][:, 0:nq], j == 0, j == nj - 1, [tp], tS)
                rb, rt = self.ringbuf()
                P.op("dve", lambda h, rb=rb: h.reciprocal(rb[:, 0:nq], self.psum[:, SBK, 0:nq]), reads=[tS], writes=[rt])
                P.op("dve", lambda h, rb=rb, hd=hd, q0=q0: h.tensor_tensor(attT[:, hd, q0:q0 + nq], self.psum[:, OB, 0:nq], rb[:, 0:nq], ALU.mult),
                     reads=[tO, rt], writes=[self.A("att", hd, half)])
            toks = {}
            for t_ in tK + tV:
                for (k_, v_) in t_.w + t_.r:
                    toks[k_] = max(toks.get(k_, 0), v_)
            slot_t.w = []
            slot_t.r = list(toks.items())
            self.tick()


_CACHE = {}


def _fm(v):
    v = np.asarray(v, np.float32)
    return np.ascontiguousarray(v.reshape(-1, 128).T)


def _rope_tables(pos_tok):
    pos_tok = np.asarray(pos_tok)
    rows = (pos_tok // 64).astype(np.float32)
    cols = (pos_tok % 64).astype(np.float32)
    inv = np.power(np.float32(10000.0), -np.arange(16, dtype=np.float32) * np.float32(2.0) / np.float32(32)).astype(np.float32)
    out = np.zeros((64, 2, len(pos_tok)), np.float32)
    for d in range(64):
        i, j, f = d // 32, (d % 32) // 16, d % 16
        ang = ((rows if i == 0 else cols) * inv[f]).astype(np.float32)
        out[d, 0] = np.cos(ang)
        out[d, 1] = np.sin(ang) * (-1.0 if j == 0 else 1.0)
    return out


_PARTNER = np.array([(d // 32) * 32 + (1 - (d % 32) // 16) * 16 + d % 16 for d in range(64)])


def kernel(x_prompt, x_sample, cache_ckv, cache_kpe, c, c_ctx, mod_w, mod_b, norm_mix_g, norm_ffn_g,
           ffn_w_gate, ffn_w_up, ffn_w_down, ev_w_in, ev_conv_w, ev_conv_b, ev_conv_ln_g, ev_conv_ln_b,
           ev_q_norm_g, ev_w_qb, ev_kv_norm_g, ev_w_kvb, ev_w_o, od_w_in, od_ln_g, od_ln_b, od_w_s, od_b_s,
           od_w_o, final_norm_g):
    f = lambda a: np.ascontiguousarray(np.asarray(a, dtype=np.float32))
    x_prompt, x_sample = f(x_prompt), f(x_sample)
    if "nc" not in _CACHE:
        _CACHE["nc"] = Builder().build()
    nc = _CACHE["nc"]
    par = np.zeros((128, NPAR), np.float32)

    def put(name, arr):
        arr = np.asarray(arr, np.float32)
        par[:, PCOL[name]:PCOL[name] + arr.shape[1]] = arr
    for l in range(2):
        put(f"nmg{l}", _fm(norm_mix_g[l])); put(f"nfg{l}", _fm(norm_ffn_g[l])); put(f"modb{l}", _fm(mod_b[l]))
    put("fng", _fm(final_norm_g)); put("convb", _fm(ev_conv_b[0])); put("clng", _fm(ev_conv_ln_g[0])); put("clnb", _fm(ev_conv_ln_b[0]))
    put("qng", _fm(ev_q_norm_g[0])); put("kvng", _fm(ev_kv_norm_g[0])); put("olng", _fm(od_ln_g[0])); put("olnb", _fm(od_ln_b[0]))
    cw = np.asarray(ev_conv_w, np.float32)[0, :, 0, :]
    put("convw", cw.T.reshape(8, 128, 31).transpose(1, 0, 2).reshape(128, 248))
    w_in = f(ev_w_in[0]); w_qb = f(ev_w_qb[0])
    w_in_sw = np.ascontiguousarray(w_in[:, 2816 + _PARTNER])
    w_qb_sw = np.ascontiguousarray(np.concatenate([w_qb[:, h * 192 + 128 + _PARTNER] for h in range(8)], axis=1))
    shared = {
        "params": par, "ident": np.eye(128, dtype=np.float32),
        "b_s": f(od_b_s[0]).reshape(1, 1024), "w_sT": np.ascontiguousarray(f(od_w_s[0]).transpose(2, 0, 1)),
        "ev_w_in": w_in, "ev_w_in_sw": w_in_sw, "ev_w_qb": w_qb, "ev_w_qb_sw": w_qb_sw,
        "ev_w_kvb": f(ev_w_kvb[0]), "ev_w_o": f(ev_w_o[0]), "od_w_in": f(od_w_in[0]), "od_w_o": f(od_w_o[0]),
        "ropek": _rope_tables(np.arange(4096)),
    }
    for l in range(2):
        shared[f"mod_w{l}"] = f(mod_w[l]); shared[f"wg{l}"] = f(ffn_w_gate[l]); shared[f"wu{l}"] = f(ffn_w_up[l]); shared[f"wd{l}"] = f(ffn_w_down[l])
    in_maps = []
    for i in range(8):
        b, q = i // 4, i % 4
        halo = np.zeros((32, D), np.float32)
        hm = np.zeros((128, 32), np.float32)
        if q > 0:
            halo[0:15] = x_sample[b, q * 1024 - 15:q * 1024]; hm[:, 0:15] = 1.0
        if q < 3:
            halo[15:30] = x_sample[b, (q + 1) * 1024:(q + 1) * 1024 + 15]; hm[:, 15:30] = 1.0
        condT = np.stack([_fm(c_ctx), _fm(c[b])], axis=-1)
        m = dict(shared)
        m.update({
            "xp": x_prompt[4 * i:4 * i + 4].reshape(T, D), "xs_own": np.ascontiguousarray(x_sample[b, q * 1024:(q + 1) * 1024]),
            "xs_all": x_sample[b], "xs_halo": halo, "hmask": hm, "cckv": f(cache_ckv[b, 0]), "ckpe": f(cache_kpe[b, 0]),
            "condT": np.ascontiguousarray(condT), "ropeq": _rope_tables(np.arange(q * 1024, (q + 1) * 1024)),
        })
        in_maps.append({"i_" + k_: v_ for k_, v_ in m.items()})
    res = run_bass_kernel_spmd(nc, in_maps, core_ids=list(range(8)))
    R = res.results
    y_prompt = np.stack([R[i]["o_yp"].reshape(4, SEQ_P, D) for i in range(8)], 0).reshape(32, SEQ_P, D)
    y_sample = np.stack([R[i]["o_ys"] for i in range(8)], 0).reshape(2, 4096, D)
    s_ckv = np.stack([R[i]["o_o_ckv"].reshape(4, 1, SEQ_P, 256) for i in range(8)], 0).reshape(32, 1, SEQ_P, 256)
    s_kpe = np.stack([R[i]["o_o_kpe"].reshape(4, 1, SEQ_P, 64) for i in range(8)], 0).reshape(32, 1, SEQ_P, 64)
    return (np.ascontiguousarray(y_prompt, np.float32), np.ascontiguousarray(y_sample, np.float32),
            np.ascontiguousarray(s_ckv, np.float32), np.ascontiguousarray(s_kpe, np.float32))
```

```python
import numpy as np
import concourse.bass as bass
import concourse.mybir as mybir
from concourse.bass_utils import run_bass_kernel_spmd

F32 = mybir.dt.float32
BF16 = mybir.dt.bfloat16
AF = mybir.ActivationFunctionType
ALU = mybir.AluOpType

D = 2048
NC16 = 16
T = 1024
DFF = 5632
EPS = 1e-6
SEQ_P = 256
NKS = 4352
SCALE = 192 ** -0.5
WSLOT = 8704
ARENA = 79104

PCOL = {}
_off = 0
for _n, _w in (("nmg0", 16), ("nmg1", 16), ("nfg0", 16), ("nfg1", 16), ("fng", 16),
               ("modb0", 96), ("modb1", 96), ("convb", 8), ("clng", 8), ("clnb", 8),
               ("qng", 4), ("kvng", 2), ("olng", 16), ("olnb", 16), ("convw", 248)):
    PCOL[_n] = _off
    _off += _w
NPAR = _off


class Tile:
    __slots__ = ("name", "w", "r", "excl")

    def __init__(self, name, w=None):
        self.name = name
        self.excl = False
        self.w = list(w) if w else []
        self.r = []


class Rec:
    def __init__(self):
        self.calls = []

    def __getattr__(self, name):
        def f(*a, **k):
            self.calls.append((name, a, k))
            return self
        return f


class Eng:
    def __init__(self, name):
        self.name = name
        self.items = []
        self.count = 0
        self.waited = {}


class Prog:
    ENG = ("pe", "act", "dve", "pool", "sp")

    def __init__(self, nc):
        self.nc = nc
        self.eng = {n: Eng(n) for n in self.ENG}
        self.sems = {}
        self.dma_val = {}
        self.out_tokens = []

    def sem(self, key):
        if key not in self.sems:
            self.sems[key] = self.nc.alloc_semaphore(name="s_" + key)
        return self.sems[key]

    def _deps(self, e, reads, writes):
        deps = {}
        own = "e_" + e.name
        for t in reads:
            for (k, v) in t.w:
                if deps.get(k, 0) < v:
                    deps[k] = v
            if t.excl:
                for (k, v) in t.r:
                    if k != own and deps.get(k, 0) < v:
                        deps[k] = v
        for t in writes:
            for (k, v) in t.w:
                if deps.get(k, 0) < v:
                    deps[k] = v
            for (k, v) in t.r:
                if deps.get(k, 0) < v:
                    deps[k] = v
        waits = []
        for k, v in deps.items():
            if k == "e_pe" and e.name == "pe":
                continue
            if e.waited.get(k, 0) >= v:
                continue
            e.waited[k] = v
            waits.append((k, v))
        return waits

    def _commit(self, tok, reads, writes):
        for t in reads:
            t.r = [x for x in t.r if x[0] != tok[0]] + [tok]
        for t in writes:
            t.w = [tok]
            t.r = []

    def op(self, en, fn, reads=(), writes=()):
        e = self.eng[en]
        waits = self._deps(e, reads, writes)
        e.count += 1
        tok = ("e_" + en, e.count)
        rec = Rec()
        fn(rec)
        assert rec.calls
        e.items.append(("op", waits, rec.calls))
        self._commit(tok, reads, writes)
        return tok

    def dma(self, qn, out_ap, in_ap, skey, reads=(), writes=(), is_output=False):
        e = self.eng[qn]
        waits = self._deps(e, reads, writes)
        k = "d_" + skey
        self.dma_val[k] = self.dma_val.get(k, 0) + 16
        tok = (k, self.dma_val[k])
        e.items.append(("dma", waits, (out_ap, in_ap, k)))
        self._commit(tok, reads, writes)
        if is_output:
            self.out_tokens.append(tok)
        return tok

    def dma_multi(self, qn, pairs, skey, reads=(), writes=()):
        e = self.eng[qn]
        waits = self._deps(e, reads, writes)
        k = "d_" + skey
        for i, (out_ap, in_ap) in enumerate(pairs):
            self.dma_val[k] = self.dma_val.get(k, 0) + 16
            e.items.append(("dma", waits if i == 0 else [], (out_ap, in_ap, k)))
        tok = (k, self.dma_val[k])
        self._commit(tok, reads, writes)
        return tok

    def fence_tokens(self):
        toks = [("e_" + n, self.eng[n].count) for n in ("pe", "act", "dve", "pool") if self.eng[n].count]
        for k, v in self.dma_val.items():
            if not k.startswith("d_w"):
                toks.append((k, v))
        return toks

    def emit(self):
        nc = self.nc
        fin = {}
        for (k, v) in self.out_tokens:
            fin[k] = max(fin.get(k, 0), v)
        sp = self.eng["sp"]
        sp.items.append(("wait", [(k, v) for k, v in fin.items()], None))
        for en in self.ENG:
            self.sem("e_" + en)
        for k in self.dma_val:
            self.sem(k)

        def run(en, h):
            e = self.eng[en]
            own = self.sems["e_" + en]
            for kind, waits, payload in e.items:
                for (k, v) in waits:
                    h.wait_ge(self.sems[k], v)
                if kind == "op":
                    for (mname, a, kw) in payload:
                        ins = getattr(h, mname)(*a, **kw)
                    ins.then_inc(own, 1)
                elif kind == "dma":
                    out_ap, in_ap, k = payload
                    h.dma_start(out=out_ap, in_=in_ap).then_inc(self.sems[k], 16)

        with nc.Block() as block:
            @block.tensor
            def _(h):
                run("pe", h)

            @block.scalar
            def _(h):
                run("act", h)

            @block.vector
            def _(h):
                run("dve", h)

            @block.gpsimd
            def _(h):
                run("pool", h)

            @block.sync
            def _(h):
                run("sp", h)


class Builder:
    def __init__(self):
        nc = self.nc = bass.Bass("TRN2", target_bir_lowering=False)
        self.P = Prog(nc)
        self.dram = {}
        self.tiles = {}
        self.arena_tiles = set()
        self.fence_toks = []
        self.xT = nc.alloc_sbuf_tensor("xT", [128, 16, T], F32)
        self.wslot = [nc.alloc_sbuf_tensor(f"wslot{i}", [128, WSLOT], BF16) for i in range(3)]
        self.ring = [nc.alloc_sbuf_tensor(f"ring{i}", [128, 512], F32) for i in range(3)]
        self.rstd = [nc.alloc_sbuf_tensor(f"rstd{i}", [128, 512], F32) for i in range(2)]
        self.par = nc.alloc_sbuf_tensor("par", [128, NPAR], F32)
        self.mods = [nc.alloc_sbuf_tensor(f"mods{l}", [128, 2, 96], F32) for l in range(2)]
        self.acoef = nc.alloc_sbuf_tensor("acoef", [128, 8, 16], F32)
        self.identf = nc.alloc_sbuf_tensor("identf", [128, 128], F32)
        self.identb = nc.alloc_sbuf_tensor("identb", [128, 128], BF16)
        self.ones = nc.alloc_sbuf_tensor("ones", [128, 128], BF16)
        self.condT = nc.alloc_sbuf_tensor("condT", [128, 16, 2], F32)
        self.scT = nc.alloc_sbuf_tensor("scT", [128, 16, 2], BF16)
        self.hmask = nc.alloc_sbuf_tensor("hmask", [128, 32], F32)
        self.arena = nc.alloc_sbuf_tensor("arena", [128, ARENA // 2], BF16)
        self.psum = nc.alloc_psum_tensor("psum", [128, 8, 512], F32)
        self.wn = 0
        self.rn = 0
        self.bn = 0
        self.nbanks = 4

    def din(self, name, shape):
        t = self.nc.dram_tensor("i_" + name, list(shape), F32, kind="ExternalInput")
        self.dram[name] = t.ap()
        return self.dram[name]

    def dout(self, name, shape):
        t = self.nc.dram_tensor("o_" + name, list(shape), F32, kind="ExternalOutput")
        self.dram[name] = t.ap()
        return self.dram[name]

    def Tl(self, *key, arena=False):
        t = self.tiles.get(key)
        if t is None:
            t = Tile(key, self.fence_toks if arena else None)
            t.excl = key[0] == "ps"
            self.tiles[key] = t
            if arena:
                self.arena_tiles.add(key)
        return t

    def A(self, *key):
        return self.Tl(*key, arena=True)

    def fence(self):
        self.fence_toks = self.P.fence_tokens()
        for k in self.arena_tiles:
            del self.tiles[k]
        self.arena_tiles = set()

    def av(self, off, dt, n):
        assert off % 32 == 0 and off + n * (4 if dt == F32 else 2) <= ARENA, (off, n)
        a = self.arena[:, off // 2: off // 2 + n * (2 if dt == F32 else 1)]
        return a.bitcast(F32) if dt == F32 else a

    def pc(self, name, c=0, n=1):
        o = PCOL[name] + c
        return self.par[:, o:o + n]

    def bank(self):
        b = self.bn
        self.bn = (self.bn + 1) % self.nbanks
        return b, self.Tl("ps", b)

    def ringbuf(self):
        r = self.rn
        self.rn = (self.rn + 1) % 3
        return self.ring[r], self.Tl("ring", r)

    def wload(self, parts, tot):
        s = self.wn
        self.wn = (self.wn + 1) % 3
        nk = parts[0][1]
        view = self.wslot[s][:, 0:nk * tot].rearrange("p (k f) -> p k f", k=nk)
        tw = self.Tl("w", s)
        self.P.dma_multi("pool", [(view[:, :, co:co + ncol], src) for (co, nk_, ncol, src) in parts], f"w{s}", writes=[tw])
        return view, tw

    def mm(self, out_ap, pairs, reads, wt, first=True, last=True):
        def fn(h):
            n = len(pairs)
            for i, (l, r) in enumerate(pairs):
                ins = h.matmul(out_ap, l, r, start=(first and i == 0), stop=(last and i == n - 1))
            return ins
        self.P.op("pe", fn, reads=reads, writes=[wt])

    def mm1(self, out_ap, l, r, start, stop, reads, wt):
        self.P.op("pe", lambda h: h.matmul(out_ap, l, r, start=start, stop=stop), reads=list(reads) + [self.Tl("ones")], writes=[wt])

    def wsrc(self, name, r0, nk, c0, ncol):
        w = self.dram[name]
        return w[r0:r0 + nk * 128, c0:c0 + ncol].rearrange("(k p) f -> p k f", p=128)

    def make_rstd(self, dst_ap, dst_t, bank_ap, bank_t, inv_n):
        P = self.P
        P.op("act", lambda h: h.activation(dst_ap, bank_ap, AF.Sqrt, scale=inv_n, bias=EPS), reads=[bank_t], writes=[dst_t])
        P.op("dve", lambda h: h.reciprocal(dst_ap, dst_ap), reads=[dst_t], writes=[dst_t])

    def norm_mod(self, xsrc, xt, hdst, ht, ntiles, aidx, sh_ap, rd_extra=(), rs_idx=None):
        P = self.P
        SB = 5
        for ti, (t0, n) in enumerate(ntiles):
            bt = self.Tl("ps", SB)
            bap = self.psum[:, SB, 0:n]
            for c in range(16):
                rb, rt = self.ringbuf()
                sq = rb.bitcast(BF16)[:, 0:n]
                P.op("act", lambda h, sq=sq, c=c: h.activation(sq, xsrc(c, t0, n), AF.Square), reads=[xt(c, ti)], writes=[rt])
                self.mm1(bap, self.ones[:], sq, c == 0, c == 15, [rt], bt)
            ri = (ti % 2) if rs_idx is None else rs_idx
            rs = self.rstd[ri][:, 0:n]
            rst = self.Tl("rstd", ri)
            self.make_rstd(rs, rst, bap, bt, 1.0 / D)
            for c in range(16):
                rb, rt = self.ringbuf()
                tmp = rb[:, 0:n]
                P.op("dve", lambda h, tmp=tmp, c=c: h.tensor_tensor(tmp, xsrc(c, t0, n), rs, ALU.mult), reads=[xt(c, ti), rst], writes=[rt])
                P.op("act", lambda h, tmp=tmp, c=c: h.activation(hdst(c, t0, n), tmp, AF.Identity,
                                                                  scale=self.acoef[:, aidx, c:c + 1], bias=sh_ap(c)),
                     reads=[rt, self.Tl("acoef")] + list(rd_extra), writes=[ht(c, ti)])

    def load_T(self, src_rows, n, stg_ap, stg_t, dst, dst_t, skey, ncols=2048, dst_grp=None):
        P = self.P
        P.dma("sp", stg_ap[0:n, :], src_rows, skey, writes=[stg_t])
        nch = (ncols + 127) // 128
        for g in range(0, nch, 4):
            b, bt = self.bank()
            cs = list(range(g, min(g + 4, nch)))

            def tr(h, cs=cs, b=b):
                for j, c in enumerate(cs):
                    w = min(128, ncols - c * 128)
                    ins = h.transpose(self.psum[0:w, b, j * 128:j * 128 + n], stg_ap[0:n, c * 128:c * 128 + w], self.identf[0:n, 0:n])
                return ins
            P.op("pe", tr, reads=[stg_t, self.Tl("ident")], writes=[bt])
            if dst_grp is not None:
                ncs = len(cs)
                src = self.psum[:, b, 0:ncs * 128].rearrange("p (j t) -> p j t", j=ncs)[:, :, 0:n]
                wts = [dst_t(c) for c in cs]
                if (g // 4) % 2 == 0:
                    P.op("act", lambda h, src=src, g=g: h.activation(dst_grp(g), src, AF.Copy), reads=[bt], writes=wts)
                else:
                    P.op("dve", lambda h, src=src, g=g: h.tensor_copy(dst_grp(g), src), reads=[bt], writes=wts)
                continue
            for j, c in enumerate(cs):
                w = min(128, ncols - c * 128)
                eng = "act" if ((g // 4) % 2 == 0) else "dve"
                src = self.psum[0:w, b, j * 128:j * 128 + n]
                if eng == "act":
                    P.op("act", lambda h, c=c, src=src: h.activation(dst(c), src, AF.Copy), reads=[bt], writes=[dst_t(c)])
                else:
                    P.op("dve", lambda h, c=c, src=src: h.tensor_copy(dst(c), src), reads=[bt], writes=[dst_t(c)])

    def build(self):
        nc, P = self.nc, self.P
        din = self.din
        xp = din("xp", [T, D]); xs_own = din("xs_own", [T, D]); xs_all = din("xs_all", [4096, D])
        xs_halo = din("xs_halo", [32, D]); hmask_d = din("hmask", [128, 32])
        cckv = din("cckv", [256, 256]); ckpe = din("ckpe", [256, 64])
        condT_d = din("condT", [128, 16, 2]); par_d = din("params", [128, NPAR])
        ropeq = din("ropeq", [64, 2, T]); ropek = din("ropek", [64, 2, 4096])
        ident_d = din("ident", [128, 128])
        bs_d = din("b_s", [1, 1024]); wsT_d = din("w_sT", [128, 8, 128])
        for l in range(2):
            din(f"mod_w{l}", [D, 6 * D]); din(f"wg{l}", [D, DFF]); din(f"wu{l}", [D, DFF]); din(f"wd{l}", [DFF, D])
        din("ev_w_in", [D, 2880]); din("ev_w_in_sw", [D, 64]); din("ev_w_qb", [512, 1536]); din("ev_w_qb_sw", [512, 512])
        din("ev_w_kvb", [256, 2048]); din("ev_w_o", [D, D]); din("od_w_in", [D, 4096]); din("od_w_o", [D, D])
        yp = self.dout("yp", [T, D]); ys = self.dout("ys", [T, D])
        o_ckv = self.dout("o_ckv", [T, 256]); o_kpe = self.dout("o_kpe", [T, 64])

        t_par, t_id, t_cond = self.Tl("par"), self.Tl("ident"), self.Tl("cond")
        P.dma_multi("sp", [(self.par[:], par_d), (self.identf[:], ident_d), (self.condT[:], condT_d), (self.hmask[:], hmask_d)],
                    "setup", writes=[t_par, t_id, t_cond, self.Tl("hmask")])
        P.op("dve", lambda h: h.tensor_copy(self.identb[:], self.identf[:]), reads=[t_id], writes=[self.Tl("identb")])
        P.op("dve", lambda h: h.memset(self.ones[:], 1.0), writes=[self.Tl("ones")])
        P.op("act", lambda h: h.activation(self.scT[:], self.condT[:], AF.Silu), reads=[t_cond], writes=[self.Tl("scT")])
        self.tiles[("ones",)] = self.Tl("ones")

        self.mods_done = {}
        self.mod_pending = [(0, b_) for b_ in range(8, 24)] + [(1, b_) for b_ in range(24)]
        self.tick_n = 0
        self.tick_every = 1
        import os
        kskip = os.environ.get("KSKIP", "")
        import os
        kpass = os.environ.get("KPASS", "PS")
        kstage = int(os.environ.get("KSTAGE", "9"))
        for ps_ in ("P", "S"):
            if ps_ not in kpass:
                continue
            self.pass_ = ps_
            cond = 0 if ps_ == "P" else 1
            self.fence()
            if "l" not in kskip and not (ps_ == "S" and getattr(self, "preloaded", False)):
                self.load_x(xp if ps_ == "P" else xs_own)
            if "m" not in kskip:
                self.compute_mods(0, 0, 8)
            for l in range(2):
                if kstage >= 1 + 2 * l:
                    if l == 0:
                        self.even_mixer(cond)
                    else:
                        self.odd_mixer(cond)
                if kstage >= 2 + 2 * l:
                    self.compute_mods(l, 12, 24)
                    self.ffn(l, cond)
            if "f" not in kskip:
                nxt = xs_own if (ps_ == "P" and "S" in kpass and "l" not in kskip) else None
                self.final(yp if ps_ == "P" else ys, next_src=nxt)
                self.preloaded = nxt is not None
        P.emit()
        return nc

    def compute_mods(self, l, b0, b1):
        P = self.P
        for blk in range(b0, b1):
            if (l, blk) in self.mods_done:
                continue
            self.mods_done[(l, blk)] = True
            view, tw = self.wload([(0, 16, 512, self.wsrc(f"mod_w{l}", 0, 16, blk * 512, 512))], 512)
            b, bt = self.bank()

            def fn(h, view=view, b=b):
                for f in range(4):
                    for k in range(16):
                        ins = h.matmul(self.psum[:, b, f * 2:f * 2 + 2], view[:, k, f * 128:(f + 1) * 128], self.scT[:, k, :],
                                       start=(k == 0), stop=(k == 15))
                return ins
            P.op("pe", fn, reads=[tw, self.Tl("scT")], writes=[bt])
            tm = self.Tl("mods", l, blk)
            for j in range(2):
                src = self.psum[:, b, 0:8].rearrange("p (f j) -> p f j", j=2)[:, :, j]
                P.op("dve", lambda h, src=src, j=j, blk=blk: h.tensor_tensor(self.mods[l][:, j, blk * 4:blk * 4 + 4], src,
                                                                              self.pc(f"modb{l}", blk * 4, 4), ALU.add),
                     reads=[bt, self.Tl("par")], writes=[tm])
            which, r = divmod(blk, 4)
            if r == 3 and which in (1, 4):
                sub = 0 if which == 1 else 1
                gname = ("nmg" if sub == 0 else "nfg") + str(l)
                rd = [self.Tl("mods", l, which * 4 + i) for i in range(4)] + [self.Tl("par")]
                for j in range(2):
                    P.op("dve", lambda h, j=j, which=which, sub=sub, gname=gname:
                         h.scalar_tensor_tensor(self.acoef[:, l * 4 + sub * 2 + j, :], self.mods[l][:, j, which * 16:(which + 1) * 16], 1.0,
                                                self.pc(gname, 0, 16), ALU.add, ALU.mult),
                         reads=rd, writes=[self.Tl("acoef")])

    def tick(self):
        self.tick_n += 1
        if self.tick_n % self.tick_every:
            return
        while self.mod_pending:
            l, blk = self.mod_pending.pop(0)
            if (l, blk) not in self.mods_done:
                self.compute_mods(l, blk, blk + 1)
                return

    def modcol(self, l, cond, which, c):
        col = which * 16 + c
        return self.mods[l][:, cond, col:col + 1]

    def modtile(self, l, which):
        return [self.Tl("mods", l, which * 4 + i) for i in range(4)]

    def load_x(self, src, chunks=range(8)):
        for n in chunks:
            stg = self.av(32768 + (n % 2) * 8192, F32, 2048)
            st = self.A("stg", n % 2)
            self.load_T(src[n * 128:(n + 1) * 128, :], 128, stg, st,
                        lambda c, n=n: self.xT[:, c, n * 128:(n + 1) * 128],
                        lambda c, n=n: self.Tl("x", c, n // 4), f"stg{n % 2}",
                        dst_grp=lambda g, n=n: self.xT[:, g:g + 4, n * 128:(n + 1) * 128])

    def x_ap(self, c, t0, n):
        return self.xT[:, c, t0:t0 + n]

    def hT_view(self):
        return self.av(0, BF16, 16 * T).rearrange("p (c t) -> p c t", c=16)

    def resid(self, bank_ap, bt, c, half, gate_ap, gate_tiles):
        xs = self.xT[:, c, half * 512:(half + 1) * 512]
        xt = self.Tl("x", c, half)
        self.P.op("dve", lambda h: h.scalar_tensor_tensor(xs, bank_ap, gate_ap, xs, ALU.mult, ALU.add),
                  reads=[bt, xt] + gate_tiles, writes=[xt])

    def ffn(self, l, cond):
        P = self.P
        self.fence()
        hT = self.hT_view()
        self.norm_mod(self.x_ap, lambda c, ti: self.Tl("x", c, ti),
                      lambda c, t0, n: hT[:, c, t0:t0 + n], lambda c, ti: self.A("h", c, ti),
                      [(0, 512), (512, 512)], l * 4 + 2 + cond, lambda c: self.modcol(l, cond, 3, c), rd_extra=self.modtile(l, 3))
        self.tick_every = 2
        act = self.av(32768, BF16, 11 * T).rearrange("p (j t) -> p j t", j=11)
        gate_tiles = self.modtile(l, 5)
        self.nbanks = 8
        for G in range(4):
            j0 = G * 11
            jj = 0
            while jj < 11:
                nj = min(2, 11 - jj)
                ncol = nj * 128
                c0 = (j0 + jj) * 128
                view, tw = self.wload([(0, 16, ncol, self.wsrc(f"wg{l}", 0, 16, c0, ncol)),
                                       (ncol, 16, ncol, self.wsrc(f"wu{l}", 0, 16, c0, ncol))], 2 * ncol)
                for half in range(2):
                    for q in range(nj):
                        hs = [hT[:, k, half * 512:(half + 1) * 512] for k in range(16)]
                        hts = [self.A("h", k, half) for k in range(16)]
                        bg, btg = self.bank()
                        self.mm(self.psum[:, bg, :], [(view[:, k, q * 128:(q + 1) * 128], hs[k]) for k in range(16)], [tw] + hts, btg)
                        bu, btu = self.bank()
                        self.mm(self.psum[:, bu, :], [(view[:, k, ncol + q * 128:ncol + (q + 1) * 128], hs[k]) for k in range(16)], [tw] + hts, btu)
                        rb, rt = self.ringbuf()
                        sg = rb[:, :]
                        P.op("act", lambda h, sg=sg, bg=bg: h.activation(sg, self.psum[:, bg, :], AF.Silu), reads=[btg], writes=[rt])
                        ta = self.A("act", jj + q, half)
                        P.op("dve", lambda h, sg=sg, bu=bu, jq=jj + q, half=half:
                             h.tensor_tensor(act[:, jq, half * 512:(half + 1) * 512], sg, self.psum[:, bu, :], ALU.mult),
                             reads=[rt, btu], writes=[ta])
                jj += nj
                self.tick()
            for cb in range(4):
                view, tw = self.wload([(0, 11, 512, self.wsrc(f"wd{l}", j0 * 128, 11, cb * 512, 512))], 512)
                for cc in range(4):
                    c = cb * 4 + cc
                    for half in range(2):
                        b, bt = self.bank()
                        self.mm(self.psum[:, b, :], [(view[:, j, cc * 128:(cc + 1) * 128], act[:, j, half * 512:(half + 1) * 512]) for j in range(11)],
                                [tw] + [self.A("act", j, half) for j in range(11)], bt)
                        self.resid(self.psum[:, b, :], bt, c, half, self.modcol(l, cond, 5, c), gate_tiles)
                self.tick()
        self.nbanks, self.bn = 4, 0

    def final(self, out, next_src=None):
        P = self.P
        self.fence()
        SB = 5
        for half in range(2):
            bt = self.Tl("ps", SB)
            bap = self.psum[:, SB, :]
            for c in range(16):
                rb, rt = self.ringbuf()
                sq = rb.bitcast(BF16)[:, 0:512]
                P.op("act", lambda h, sq=sq, c=c: h.activation(sq, self.xT[:, c, half * 512:(half + 1) * 512], AF.Square),
                     reads=[self.Tl("x", c, half)], writes=[rt])
                self.mm1(bap, self.ones[:], sq, c == 0, c == 15, [rt], bt)
            rs = self.rstd[half][:, :]
            rst = self.Tl("rstd", half)
            self.make_rstd(rs, rst, bap, bt, 1.0 / D)
            for c in range(16):
                xs = self.xT[:, c, half * 512:(half + 1) * 512]
                P.op("dve", lambda h, xs=xs, c=c: h.scalar_tensor_tensor(xs, xs, self.pc("fng", c), rs, ALU.mult, ALU.mult),
                     reads=[rst, self.Tl("par"), self.Tl("x", c, half)], writes=[self.Tl("x", c, half)])
            for n4 in range(4):
                n = half * 4 + n4
                stg = self.av((n % 2) * 8192, F32, 2048)
                st = self.A("ostg", n % 2)
                for g in range(4):
                    b, bt2 = self.bank()

                    def tr(h, g=g, b=b, n=n):
                        for j in range(4):
                            c = g * 4 + j
                            ins = h.transpose(self.psum[:, b, j * 128:(j + 1) * 128], self.xT[:, c, n * 128:(n + 1) * 128], self.identf[:])
                        return ins
                    P.op("pe", tr, reads=[self.Tl("x", g * 4 + j, half) for j in range(4)] + [self.Tl("ident")], writes=[bt2])
                    dst = stg[:, g * 512:(g + 1) * 512]
                    if g % 2 == 0:
                        P.op("act", lambda h, dst=dst, b=b: h.activation(dst, self.psum[:, b, :], AF.Copy), reads=[bt2], writes=[st])
                    else:
                        P.op("dve", lambda h, dst=dst, b=b: h.tensor_copy(dst, self.psum[:, b, :]), reads=[bt2], writes=[st])
                P.dma("sp", out[n * 128:(n + 1) * 128, :], stg, f"ostg{n % 2}", reads=[st], is_output=True)
            if next_src is not None:
                self.load_x(next_src, range(half * 4, half * 4 + 4))

    def odd_mixer(self, cond):
        P = self.P
        l = 1
        self.compute_mods(1, 0, 12)
        self.fence()
        hT = self.hT_view()
        self.norm_mod(self.x_ap, lambda c, ti: self.Tl("x", c, ti),
                      lambda c, t0, n: hT[:, c, t0:t0 + n], lambda c, ti: self.A("h", c, ti),
                      [(0, 512), (512, 512)], l * 4 + 0 + cond, lambda c: self.modcol(l, cond, 0, c), rd_extra=self.modtile(l, 0))
        self.tick_every = 1
        uT = self.av(32768, BF16, 16 * 512).rearrange("p (c t) -> p c t", c=16)
        vg = self.av(49152, BF16, 4 * 2048).rearrange("p (n f) -> p n f", n=4)
        bsr = self.av(49152, F32, 1024)
        T2 = self.av(65536, F32, 16 * 128).rearrange("p (c t) -> p c t", c=16)
        wsT = self.av(73728, BF16, 1024).rearrange("p (g t) -> p g t", g=8)
        wsTf = self.av(32768, F32, 1024).rearrange("p (g t) -> p g t", g=8)
        st = self.av(75776, F32, 64)
        t_ws, t_wsf, t_bsr, t_T2 = self.A("wsT"), self.A("wsTf"), self.A("bsr"), self.A("T2")
        P.dma("sp", wsTf, self.dram["w_sT"], "osetup1", writes=[t_wsf])
        P.dma("sp", bsr, self.dram["b_s"].partition_broadcast(128), "osetup2", writes=[t_bsr])
        P.op("dve", lambda h: h.tensor_copy(wsT, wsTf), reads=[t_wsf], writes=[t_ws])
        for g in range(8):
            b, bt = self.bank()
            self.mm(self.psum[:, b, 0:128], [(self.ones[:], wsT[:, g, :])], [t_ws], bt)
            for cc in range(2):
                c = g * 2 + cc
                P.op("dve", lambda h, c=c, b=b, g=g: h.scalar_tensor_tensor(T2[:, c, :], self.psum[:, b, 0:128], self.pc("olnb", c),
                                                                         bsr[:, g * 128:(g + 1) * 128], ALU.mult, ALU.add),
                     reads=[bt, t_bsr, self.Tl("par")], writes=[t_T2])
        gate_tiles = self.modtile(l, 2)
        self.fence()
        self.nbanks = 8
        t_ws, t_T2 = self.A("wsT"), self.A("T2")
        def vpart(half):
            hs = [hT[:, k, half * 512:(half + 1) * 512] for k in range(16)]
            hts = [self.A("h", k, half) for k in range(16)]
            t_st = self.A("ost")
            P.op("dve", lambda h: h.memset(st, 0.0), writes=[t_st])
            for cb in range(4):
                self.tick()
                view, tw = self.wload([(0, 16, 512, self.wsrc("od_w_in", 0, 16, 2048 + cb * 512, 512))], 512)
                for n in range(4):
                    tk = half * 512 + n * 128
                    b, bt = self.bank()
                    self.mm(self.psum[:, b, :], [(hT[:, k, tk:tk + 128], view[:, k, :]) for k in range(16)], [tw] + hts, bt)
                    tv = self.A("vg", n, cb)
                    dstv = vg[:, n, cb * 512:(cb + 1) * 512]
                    P.op("act", lambda h, dstv=dstv, b=b, n=n, cb=cb: h.activation(dstv, self.psum[:, b, :], AF.Gelu_apprx_tanh,
                                                                                accum_out=st[:, n * 4 + cb:n * 4 + cb + 1]),
                         reads=[bt], writes=[tv, t_st])
                    rb, rt = self.ringbuf()
                    P.op("act", lambda h, dstv=dstv, rb=rb, n=n, cb=cb: h.activation(rb.bitcast(BF16)[:, 0:512], dstv, AF.Square,
                                                                                  accum_out=st[:, 16 + n * 4 + cb:16 + n * 4 + cb + 1]),
                         reads=[tv], writes=[rt, t_st])
                    yield
            s1 = st[:, 0:16].rearrange("p (n c) -> p n c", c=4)
            s2 = st[:, 16:32].rearrange("p (n c) -> p n c", c=4)
            mean, ex2, var, rsd, nmr = (st[:, 32 + 4 * i:36 + 4 * i] for i in range(5))

            P.op("dve", lambda h: h.tensor_reduce(mean, s1, mybir.AxisListType.X, ALU.add), reads=[t_st], writes=[t_st])
            P.op("dve", lambda h: h.tensor_reduce(ex2, s2, mybir.AxisListType.X, ALU.add), reads=[t_st], writes=[t_st])
            P.op("dve", lambda h: h.tensor_scalar(mean, mean, 1.0 / 2048, None, ALU.mult), reads=[t_st], writes=[t_st])
            P.op("dve", lambda h: h.tensor_tensor(var, mean, mean, ALU.mult), reads=[t_st], writes=[t_st])
            P.op("dve", lambda h: h.scalar_tensor_tensor(var, ex2, 1.0 / 2048, var, ALU.mult, ALU.subtract), reads=[t_st], writes=[t_st])
            P.op("act", lambda h: h.activation(rsd, var, AF.Sqrt, scale=1.0, bias=EPS), reads=[t_st], writes=[t_st])
            P.op("dve", lambda h: h.reciprocal(rsd, rsd), reads=[t_st], writes=[t_st])
            P.op("dve", lambda h: h.scalar_tensor_tensor(nmr, mean, -1.0, rsd, ALU.mult, ALU.mult), reads=[t_st], writes=[t_st])
            for n in range(4):
                tvs = [self.A("vg", n, cb) for cb in range(4)]
                P.op("act", lambda h, n=n: h.activation(vg[:, n, :], vg[:, n, :], AF.Identity, scale=rsd[:, n:n + 1], bias=nmr[:, n:n + 1]),
                     reads=tvs + [t_st], writes=tvs)
        def upart(half):
            hs = [hT[:, k, half * 512:(half + 1) * 512] for k in range(16)]
            hts = [self.A("h", k, half) for k in range(16)]
            for cb in range(4):
                self.tick()
                view, tw = self.wload([(0, 16, 512, self.wsrc("od_w_in", 0, 16, cb * 512, 512))], 512)
                for cc in range(4):
                    c = cb * 4 + cc
                    b, bt = self.bank()
                    self.mm(self.psum[:, b, :], [(view[:, k, cc * 128:(cc + 1) * 128], hs[k]) for k in range(16)], [tw] + hts, bt)
                    P.op("act", lambda h, c=c, b=b: h.activation(uT[:, c, :], self.psum[:, b, :], AF.Gelu_apprx_tanh),
                         reads=[bt], writes=[self.A("u", c)])
        def sppart(half):
            for c4 in range(4):
                for n in range(4):
                    tvs = [self.A("vg", n, c4)]
                    b, bt = self.bank()

                    def sp(h, n=n, c4=c4, b=b):
                        for j in range(4):
                            c = c4 * 4 + j
                            ins = h.matmul(self.psum[:, b, j * 128:(j + 1) * 128], vg[:, n, c * 128:(c + 1) * 128], wsT[:, c // 2, :],
                                           start=True, stop=True)
                        return ins
                    P.op("pe", sp, reads=tvs + [t_ws], writes=[bt])
                    for j in range(4):
                        c = c4 * 4 + j
                        rb, rt = self.ringbuf()
                        tmp = rb[:, 0:128]
                        P.op("dve", lambda h, tmp=tmp, b=b, j=j, c=c: h.scalar_tensor_tensor(tmp, self.psum[:, b, j * 128:(j + 1) * 128], self.pc("olng", c),
                                                                                          T2[:, c, :], ALU.mult, ALU.add),
                             reads=[bt, t_T2, self.Tl("par")], writes=[rt])
                        us = uT[:, c, n * 128:(n + 1) * 128]
                        P.op("dve", lambda h, tmp=tmp, us=us: h.tensor_tensor(us, us, tmp, ALU.mult),
                             reads=[rt, self.A("u", c)], writes=[self.A("u", c)])
                    yield
        def wopart(half):
            for cb in range(4):
                self.tick()
                view, tw = self.wload([(0, 16, 512, self.wsrc("od_w_o", 0, 16, cb * 512, 512))], 512)
                for cc in range(4):
                    c = cb * 4 + cc
                    b, bt = self.bank()
                    self.mm(self.psum[:, b, :], [(view[:, k, cc * 128:(cc + 1) * 128], uT[:, k, :]) for k in range(16)],
                            [tw] + [self.A("u", k) for k in range(16)], bt)
                    self.resid(self.psum[:, b, :], bt, c, half, self.modcol(l, cond, 2, c), gate_tiles)
        def drain(*gens):
            gens = list(gens)
            while gens:
                for g_ in list(gens):
                    try:
                        next(g_)
                    except StopIteration:
                        gens.remove(g_)
        drain(vpart(0))
        upart(0)
        drain(sppart(0), vpart(1))
        wopart(0)
        upart(1)
        drain(sppart(1))
        wopart(1)
        self.nbanks, self.bn = 4, 0

    def even_mixer(self, cond):
        P = self.P
        l = 0
        isS = self.pass_ == "S"
        self.fence()
        if isS:
            xh = self.av(71168, F32, 16 * 32).rearrange("p (c t) -> p c t", c=16)
            hh = self.av(73216, BF16, 16 * 32).rearrange("p (c t) -> p c t", c=16)
            stg = self.av(0, F32, 2048)
            self.load_T(self.dram["xs_halo"], 32, stg, self.A("hstg"), lambda c: xh[:, c, :], lambda c: self.A("xh", c), "hstg",
                        dst_grp=lambda g: xh[:, g:g + 4, :])
            self.norm_mod(lambda c, t0, n: xh[:, c, :], lambda c, ti: self.A("xh", c),
                          lambda c, t0, n: hh[:, c, :], lambda c, ti: self.A("hh", c), [(0, 32)], cond,
                          lambda c: self.modcol(l, cond, 0, c), rd_extra=self.modtile(l, 0))
            self.fence()
        hT = self.hT_view()
        self.norm_mod(self.x_ap, lambda c, ti: self.Tl("x", c, ti),
                      lambda c, t0, n: hT[:, c, t0:t0 + n], lambda c, ti: self.A("h", c, ti),
                      [(0, 512), (512, 512)], l * 4 + 0 + cond, lambda c: self.modcol(l, cond, 0, c), rd_extra=self.modtile(l, 0))
        self.tick_every = 1
        NK = NKS if isS else T
        if isS:
            ckv_all = self.av(32768, BF16, 2 * NKS).rearrange("p (c t) -> p c t", c=2)
            kpe_all = self.av(50176, BF16, NKS)
        else:
            ckv_all = self.av(32768, BF16, 2 * T).rearrange("p (c t) -> p c t", c=2)
            kpe_all = self.av(36864, BF16, T)
        qg = self.av(58880, BF16, 4 * T).rearrange("p (c t) -> p c t", c=4)
        rstd_q = self.av(67072, F32, T)
        if not isS:
            self.kside(lambda k, t0, n: hT[:, k, t0:t0 + n], lambda k, ti: self.A("h", k, ti), [(0, 512), (512, 512)],
                       ckv_all, kpe_all, 0, cond, state_out=True)
        if isS:
            PADW = 1054
            a_pad = self.av(38912, BF16, 8 * PADW).rearrange("p (c t) -> p c t", c=8)
        else:
            PADW = 4 * 286
            a_pad = self.av(38912, BF16, 8 * PADW).rearrange("p (c t) -> p c t", c=8)
            ap4 = self.av(38912, BF16, 8 * PADW).rearrange("p (c s t) -> p c s t", c=8, s=4)
            P.op("dve", lambda h: h.memset(a_pad[:, :, :], 0.0), writes=[self.A("apad", c) for c in range(8)])
        for cp in range(4):
            view, tw = self.wload([(0, 16, 256, self.wsrc("ev_w_in", 0, 16, cp * 256, 256)),
                                   (256, 16, 256, self.wsrc("ev_w_in", 0, 16, 1024 + cp * 256, 256))], 512)
            for q in range(2):
                c = cp * 2 + q
                tiles_ = [(0, 512, 0), (512, 512, 1)] + ([(0, 32, 2)] if isS else [])
                for (t0, n, kind) in tiles_:
                    if kind == 2:
                        hs = [hh[:, k, :] for k in range(16)]
                        hts = [self.A("hh", k) for k in range(16)]
                    else:
                        hs = [hT[:, k, t0:t0 + n] for k in range(16)]
                        hts = [self.A("h", k, kind) for k in range(16)]
                    bv, btv = self.bank()
                    self.mm(self.psum[:, bv, 0:n], [(view[:, k, q * 128:(q + 1) * 128], hs[k]) for k in range(16)], [tw] + hts, btv)
                    bg, btg = self.bank()
                    self.mm(self.psum[:, bg, 0:n], [(view[:, k, 256 + q * 128:256 + (q + 1) * 128], hs[k]) for k in range(16)], [tw] + hts, btg)
                    rb, rt = self.ringbuf()
                    sg = rb[:, 0:n]
                    P.op("act", lambda h, sg=sg, bg=bg, n=n: h.activation(sg, self.psum[:, bg, 0:n], AF.Sigmoid), reads=[btg], writes=[rt])
                    ta = self.A("apad", c)
                    if kind == 2:
                        P.op("dve", lambda h, sg=sg: h.tensor_tensor(sg, sg, self.hmask[:, :], ALU.mult), reads=[rt, self.Tl("hmask")], writes=[rt])
                        P.op("dve", lambda h, sg=sg, bv=bv, c=c: h.tensor_tensor(a_pad[:, c, 0:15], sg[:, 0:15], self.psum[:, bv, 0:15], ALU.mult),
                             reads=[rt, btv], writes=[ta])
                        P.op("dve", lambda h, sg=sg, bv=bv, c=c: h.tensor_tensor(a_pad[:, c, 1039:1054], sg[:, 15:30], self.psum[:, bv, 15:30], ALU.mult),
                             reads=[rt, btv], writes=[ta])
                    elif isS:
                        P.op("dve", lambda h, sg=sg, bv=bv, c=c, t0=t0: h.tensor_tensor(a_pad[:, c, 15 + t0:15 + t0 + 512], sg, self.psum[:, bv, :], ALU.mult),
                             reads=[rt, btv], writes=[ta])
                    else:
                        s0 = t0 // 256
                        P.op("dve", lambda h, sg=sg, bv=bv, c=c, s0=s0: h.tensor_tensor(ap4[:, c, s0:s0 + 2, 15:271],
                                                                                     sg.rearrange("p (s t) -> p s t", s=2),
                                                                                     self.psum[:, bv, :].rearrange("p (s t) -> p s t", s=2), ALU.mult),
                             reads=[rt, btv], writes=[ta])
            self.tick()
        view, tw = self.wload([(0, 16, 512, self.wsrc("ev_w_in", 0, 16, 2048, 512))], 512)
        SB = 5
        for half in range(2):
            hs = [hT[:, k, half * 512:(half + 1) * 512] for k in range(16)]
            hts = [self.A("h", k, half) for k in range(16)]
            sbt = self.Tl("ps", SB)
            for c in range(4):
                b, bt = self.bank()
                self.mm(self.psum[:, b, :], [(view[:, k, c * 128:(c + 1) * 128], hs[k]) for k in range(16)], [tw] + hts, bt)
                rb, rt = self.ringbuf()
                sq = rb.bitcast(BF16)[:, 0:512]
                P.op("act", lambda h, sq=sq, b=b: h.activation(sq, self.psum[:, b, :], AF.Square), reads=[bt], writes=[rt])
                self.mm1(self.psum[:, SB, :], self.ones[:], sq, c == 0, c == 3, [rt], sbt)
                P.op("dve", lambda h, b=b, c=c, half=half: h.tensor_scalar(qg[:, c, half * 512:(half + 1) * 512], self.psum[:, b, :],
                                                                          self.pc("qng", c), None, ALU.mult),
                     reads=[bt, self.Tl("par")], writes=[self.A("qg", c, half)])
            self.make_rstd(rstd_q[:, half * 512:(half + 1) * 512], self.A("rstdq", half), self.psum[:, SB, :], sbt, 1.0 / 512)
        self.fence()
        y = self.av(0, F32, 8 * T).rearrange("p (c t) -> p c t", c=8)
        diag = self.av(71168, BF16, 31 * 128).rearrange("p (k t) -> p k t", k=31)
        cw = self.pc("convw", 0, 248).rearrange("p (c k) -> p c k", c=8)
        S1, S2 = 4, 5
        for c in range(8):
            t_dgs = [self.A("diag", 0), self.A("diag", 1)]
            for k in range(31):
                P.op("dve", lambda h, c=c, k=k: h.tensor_scalar(diag[:, k, :], self.identb[:], cw[:, c, k:k + 1], None, ALU.mult),
                     reads=[self.Tl("identb"), self.Tl("par")], writes=[t_dgs[k // 16]])
            if isS:
                units = [(half * 512, 512, a_pad[:, c, half * 512:half * 512 + 542]) for half in range(2)]
            else:
                units = [(s * 256, 256, a_pad[:, c, s * 286:(s + 1) * 286]) for s in range(4)]
            if c % 2 == 1:
                self.tick()
            ubanks = [self.bank() for _ in units]
            for (t0, n, win), (b, bt) in zip(units, ubanks):
                self.mm(self.psum[:, b, 0:n], [(diag[:, k, :], win[:, k:k + n]) for k in range(16)], [t_dgs[0], self.A("apad", c)], bt, last=False)
            for (t0, n, win), (b, bt) in zip(units, ubanks):
                self.mm(self.psum[:, b, 0:n], [(diag[:, k, :], win[:, k:k + n]) for k in range(16, 31)], [t_dgs[1], self.A("apad", c)], bt, first=False)
                P.op("act", lambda h, b=b, c=c, t0=t0, n=n: h.activation(y[:, c, t0:t0 + n], self.psum[:, b, 0:n], AF.Identity,
                                                                      bias=self.pc("convb", c)),
                     reads=[bt, self.Tl("par")], writes=[self.A("y", c, t0 // 512)])
        self.fence()
        aT = self.av(38912, BF16, 8 * T).rearrange("p (c t) -> p c t", c=8)
        for half in range(2):
            t1, t2 = self.Tl("ps", S1), self.Tl("ps", S2)
            for c in range(8):
                ys_ = y[:, c, half * 512:(half + 1) * 512]
                rb, rt = self.ringbuf()
                yb = rb.bitcast(BF16)[:, 0:512]
                P.op("dve", lambda h, yb=yb, ys_=ys_: h.tensor_copy(yb, ys_), reads=[self.A("y", c, half)], writes=[rt])
                self.mm1(self.psum[:, S1, :], self.ones[:], yb, c == 0, c == 7, [rt], t1)
                rb2, rt2 = self.ringbuf()
                sq = rb2.bitcast(BF16)[:, 0:512]
                P.op("act", lambda h, sq=sq, ys_=ys_: h.activation(sq, ys_, AF.Square), reads=[self.A("y", c, half)], writes=[rt2])
                self.mm1(self.psum[:, S2, :], self.ones[:], sq, c == 0, c == 7, [rt2], t2)
            mean = self.rstd[0][:, :]
            rs = self.rstd[1][:, :]
            tm, tr_ = self.Tl("rstd", 0), self.Tl("rstd", 1)
            P.op("act", lambda h: h.activation(mean, self.psum[:, S1, :], AF.Copy, scale=1.0 / 1024), reads=[t1], writes=[tm])
            rb, rt = self.ringbuf()
            msq = rb[:, :]
            P.op("dve", lambda h, msq=msq: h.tensor_tensor(msq, mean, mean, ALU.mult), reads=[tm], writes=[rt])
            P.op("dve", lambda h, msq=msq: h.scalar_tensor_tensor(rs, self.psum[:, S2, :], 1.0 / 1024, msq, ALU.mult, ALU.subtract),
                 reads=[t2, rt], writes=[tr_])
            P.op("act", lambda h: h.activation(rs, rs, AF.Sqrt, scale=1.0, bias=EPS), reads=[tr_], writes=[tr_])
            P.op("dve", lambda h: h.reciprocal(rs, rs), reads=[tr_], writes=[tr_])
            for c in range(8):
                ys_ = y[:, c, half * 512:(half + 1) * 512]
                rb, rt = self.ringbuf()
                tmp = rb[:, :]
                P.op("dve", lambda h, tmp=tmp, ys_=ys_: h.tensor_tensor(tmp, ys_, mean, ALU.subtract), reads=[self.A("y", c, half), tm], writes=[rt])
                P.op("dve", lambda h, tmp=tmp: h.tensor_tensor(tmp, tmp, rs, ALU.mult), reads=[rt, tr_], writes=[rt])
                P.op("act", lambda h, tmp=tmp, c=c, half=half: h.activation(aT[:, c, half * 512:(half + 1) * 512], tmp, AF.Silu,
                                                                         scale=self.pc("clng", c), bias=self.pc("clnb", c)),
                     reads=[rt, self.Tl("par")], writes=[self.A("aT", c, half)])
        self.compute_mods(0, 8, 12)
        gate_tiles = self.modtile(l, 2)
        for half in range(2):
          for cb in range(4):
            view, tw = self.wload([(0, 8, 512, self.wsrc("ev_w_o", 0, 8, cb * 512, 512))], 512)
            for cc in range(4):
                c = cb * 4 + cc
                if True:
                    b, bt = self.bank()
                    self.mm(self.psum[:, b, :], [(view[:, k, cc * 128:(cc + 1) * 128], aT[:, k, half * 512:(half + 1) * 512]) for k in range(8)],
                            [tw] + [self.A("aT", k, half) for k in range(8)], bt)
                    self.resid(self.psum[:, b, :], bt, c, half, self.modcol(l, cond, 2, c), gate_tiles)
        if isS:
            self.fence()
            self.kside_sample(ckv_all, kpe_all, cond)
        self.fence()
        self.attention(ckv_all, kpe_all, qg, rstd_q, NK)
        attT = self.av(0, BF16, 8 * T).rearrange("p (c t) -> p c t", c=8)
        for cb in range(4):
            view, tw = self.wload([(0, 8, 512, self.wsrc("ev_w_o", 1024, 8, cb * 512, 512))], 512)
            for cc in range(4):
                c = cb * 4 + cc
                for half in range(2):
                    b, bt = self.bank()
                    self.mm(self.psum[:, b, :], [(view[:, k, cc * 128:(cc + 1) * 128], attT[:, k, half * 512:(half + 1) * 512]) for k in range(8)],
                            [tw] + [self.A("att", k, half) for k in range(8)], bt)
                    self.resid(self.psum[:, b, :], bt, c, half, self.modcol(l, cond, 2, c), gate_tiles)

    def kside(self, hsrc, htile, ntiles, ckv_all, kpe_all, koff, cond, state_out=False, rope_tok0=None, rs_idx=None, wpre=None):
        P = self.P
        SB = 4
        for ti, (t0, n) in enumerate(ntiles):
            if wpre is not None:
                view, tw = wpre
            else:
                view, tw = self.wload([(0, 16, 320, self.wsrc("ev_w_in", 0, 16, 2560, 320)),
                                       (320, 16, 64, self.wsrc("ev_w_in_sw", 0, 16, 0, 64))], 384)
            hs = [hsrc(k, t0, n) for k in range(16)]
            hts = [htile(k, ti) for k in range(16)]
            kb = []
            sbt = self.Tl("ps", SB)
            for c in range(2):
                b, bt = self.bank()
                self.mm(self.psum[:, b, 0:n], [(view[:, k, c * 128:(c + 1) * 128], hs[k]) for k in range(16)], [tw] + hts, bt)
                kb.append((b, bt))
                rb, rt = self.ringbuf()
                sq = rb.bitcast(BF16)[:, 0:n]
                P.op("act", lambda h, sq=sq, b=b, n=n: h.activation(sq, self.psum[:, b, 0:n], AF.Square), reads=[bt], writes=[rt])
                self.mm1(self.psum[:, SB, 0:n], self.ones[:], sq, c == 0, c == 1, [rt], sbt)
            if rs_idx is None:
                rs = self.rstd[ti % 2][:, 0:n]
                rst = self.Tl("rstd", ti % 2)
            else:
                rs = self.rstd[rs_idx][:, 256:256 + n]
                rst = self.Tl("rstdk", rs_idx)
            self.make_rstd(rs, rst, self.psum[:, SB, 0:n], sbt, 1.0 / 256)
            kt = self.A("kall", (koff + t0) // 512)
            if state_out:
                ckf = self.av(71168, F32, 2 * 512).rearrange("p (c t) -> p c t", c=2)
                kpf = self.av(75264, F32, 512)
            for c in range(2):
                b, bt = kb[c]
                if state_out:
                    P.op("dve", lambda h, b=b, c=c, n=n: h.scalar_tensor_tensor(ckf[:, c, 0:n], self.psum[:, b, 0:n], self.pc("kvng", c), rs, ALU.mult, ALU.mult),
                         reads=[bt, rst, self.Tl("par")], writes=[self.A("ckf", c)])
                    P.op("act", lambda h, c=c, n=n, t0=t0: h.activation(ckv_all[:, c, koff + t0:koff + t0 + n], ckf[:, c, 0:n], AF.Copy),
                         reads=[self.A("ckf", c)], writes=[kt])
                else:
                    P.op("dve", lambda h, b=b, c=c, n=n, t0=t0: h.scalar_tensor_tensor(ckv_all[:, c, koff + t0:koff + t0 + n], self.psum[:, b, 0:n],
                                                                                    self.pc("kvng", c), rs, ALU.mult, ALU.mult),
                         reads=[bt, rst, self.Tl("par")], writes=[kt])
            b, bt = self.bank()
            self.mm(self.psum[0:64, b, 0:n], [(view[:, k, 256:320], hs[k]) for k in range(16)], [tw] + hts, bt)
            if rope_tok0 is None:
                if state_out:
                    P.op("act", lambda h, b=b, n=n: h.activation(kpf[0:64, 0:n], self.psum[0:64, b, 0:n], AF.Copy), reads=[bt], writes=[self.A("kpf")])
                P.op("dve", lambda h, b=b, n=n, t0=t0: h.tensor_copy(kpe_all[0:64, koff + t0:koff + t0 + n], self.psum[0:64, b, 0:n]), reads=[bt], writes=[kt])
            else:
                b2, bt2 = self.bank()
                self.mm(self.psum[0:64, b2, 0:n], [(view[:, k, 320:384], hs[k]) for k in range(16)], [tw] + hts, bt2)
                rk = self.av(71168, F32, 2 * 512).rearrange("p (j t) -> p j t", j=2)
                trk = self.A("ropek")
                P.dma("sp", rk[0:64, :, 0:n], self.dram["ropek"][:, :, rope_tok0 + t0:rope_tok0 + t0 + n], "ropek", writes=[trk])
                rb, rt = self.ringbuf()
                rb2, rt2 = self.ringbuf()
                P.op("dve", lambda h, rb=rb, b=b, n=n: h.tensor_tensor(rb[0:64, 0:n], self.psum[0:64, b, 0:n], rk[0:64, 0, 0:n], ALU.mult), reads=[bt, trk], writes=[rt])
                P.op("dve", lambda h, rb2=rb2, b2=b2, n=n: h.tensor_tensor(rb2[0:64, 0:n], self.psum[0:64, b2, 0:n], rk[0:64, 1, 0:n], ALU.mult), reads=[bt2, trk], writes=[rt2])
                P.op("dve", lambda h, rb=rb, rb2=rb2, n=n, t0=t0: h.tensor_tensor(kpe_all[0:64, koff + t0:koff + t0 + n], rb[0:64, 0:n], rb2[0:64, 0:n], ALU.add),
                     reads=[rt, rt2], writes=[kt])
            if state_out:
                so = self.av(77312, F32, 320)
                for n4 in range(n // 128):
                    b, bt = self.bank()
                    tso = self.A("so")

                    def tr(h, b=b, n4=n4):
                        for c in range(2):
                            h.transpose(self.psum[:, b, c * 128:(c + 1) * 128], ckf[:, c, n4 * 128:(n4 + 1) * 128], self.identf[:])
                        return h.transpose(self.psum[:, b, 256:320], kpf[0:64, n4 * 128:(n4 + 1) * 128], self.identf[0:64, 0:64])
                    P.op("pe", tr, reads=[self.A("ckf", 0), self.A("ckf", 1), self.A("kpf"), self.Tl("ident")], writes=[bt])
                    P.op("dve", lambda h, b=b: h.tensor_copy(so[:, 0:320], self.psum[:, b, 0:320]), reads=[bt], writes=[tso])
                    r0 = t0 + n4 * 128
                    P.dma("sp", self.dram["o_ckv"][r0:r0 + 128, :], so[:, 0:256], "so", reads=[tso], is_output=True)
                    P.dma("sp", self.dram["o_kpe"][r0:r0 + 128, :], so[:, 256:320], "so", reads=[tso], is_output=True)

    def kside_sample(self, ckv_all, kpe_all, cond):
        P = self.P
        l = 0
        stg = self.av(0, F32, 2048)
        for n in range(2):
            tk = self.A("kall", 0)
            self.load_T(self.dram["cckv"][n * 128:(n + 1) * 128, :], 128, stg[:, 0:256], self.A("kstg", 0),
                        lambda c, n=n: ckv_all[:, c, n * 128:(n + 1) * 128], lambda c: tk, "kstg0", ncols=256)
            self.load_T(self.dram["ckpe"][n * 128:(n + 1) * 128, :], 128, stg[:, 256:320], self.A("kstg", 0),
                        lambda c, n=n: kpe_all[0:64, n * 128:(n + 1) * 128], lambda c: tk, "kstg0", ncols=64)
        stgs = [self.av(i * 8192, F32, 2048) for i in range(2)]
        xns = [self.av(16384 + i * 4096, BF16, 2048) for i in range(2)]
        hts_ = [self.av(24576 + i * 4096, BF16, 16 * 128).rearrange("p (c t) -> p c t", c=16) for i in range(2)]
        ssb = self.av(75264, F32, 8)
        psb = self.psum[:, :, :].bitcast(BF16)

        def s1(tt):
            pi = tt % 2
            st_, xn = stgs[pi], xns[pi]
            tst, txn, tss = self.A("kstg", pi), self.A("kxn", pi), self.A("kss", pi)
            P.dma("sp", st_, self.dram["xs_all"][tt * 128:(tt + 1) * 128, :], f"kstg{pi}", writes=[tst])
            P.op("dve", lambda h: h.memset(ssb[:, pi:pi + 1], 0.0), writes=[tss])
            P.op("act", lambda h: h.activation(xn, st_, AF.Square, accum_out=ssb[:, pi:pi + 1]), reads=[tst], writes=[txn, tss])
            P.op("act", lambda h: h.activation(ssb[:, 2 + pi:3 + pi], ssb[:, pi:pi + 1], AF.Sqrt, scale=1.0 / D, bias=EPS), reads=[tss], writes=[tss])
            P.op("dve", lambda h: h.reciprocal(ssb[:, 2 + pi:3 + pi], ssb[:, 2 + pi:3 + pi]), reads=[tss], writes=[tss])
            P.op("dve", lambda h: h.tensor_scalar(xn, st_, ssb[:, 2 + pi:3 + pi], None, ALU.mult), reads=[tst, tss], writes=[txn])

        def s2(tt):
            pi = tt % 2
            xn, ht_ = xns[pi], hts_[pi]
            txn = self.A("kxn", pi)
            rd = [self.Tl("acoef")] + self.modtile(l, 0)
            for g in range(4):
                b, bt = self.bank()

                def tr(h, g=g, b=b):
                    for j in range(4):
                        c = g * 4 + j
                        ins = h.transpose(psb[:, b, j * 128:(j + 1) * 128], xn[:, c * 128:(c + 1) * 128], self.identb[:])
                    return ins
                P.op("pe", tr, reads=[txn, self.Tl("identb")], writes=[bt])
                for j in range(4):
                    c = g * 4 + j
                    src = psb[:, b, j * 128:(j + 1) * 128]
                    if g % 2 == 0:
                        P.op("act", lambda h, c=c, src=src: h.activation(ht_[:, c, :], src, AF.Identity, scale=self.acoef[:, cond, c:c + 1],
                                                                     bias=self.modcol(l, cond, 0, c)),
                             reads=[bt] + rd, writes=[self.A("kh", pi, c)])
                    else:
                        P.op("dve", lambda h, c=c, src=src: h.tensor_scalar(ht_[:, c, :], src, self.acoef[:, cond, c:c + 1], self.modcol(l, cond, 0, c),
                                                                        ALU.mult, ALU.add),
                             reads=[bt] + rd, writes=[self.A("kh", pi, c)])

        kw_res = self.wload([(0, 16, 320, self.wsrc("ev_w_in", 0, 16, 2560, 320)),
                             (320, 16, 64, self.wsrc("ev_w_in_sw", 0, 16, 0, 64))], 384)

        def s3(tt):
            pi = tt % 2
            ht_ = hts_[pi]
            self.kside(lambda k, t0, n, ht_=ht_: ht_[:, k, :], lambda k, ti, pi=pi: self.A("kh", pi, k), [(0, 128)], ckv_all, kpe_all,
                       256 + tt * 128, cond, rope_tok0=tt * 128, rs_idx=pi, wpre=kw_res)
        for step in range(32 + 2):
            if step < 32:
                s1(step)
            if 0 <= step - 1 < 32:
                s2(step - 1)
            if 0 <= step - 2 < 32:
                s3(step - 2)

    def attention(self, ckv_all, kpe_all, qg, rstd_q, NK):
        P = self.P
        isS = self.pass_ == "S"
        attT = self.av(0, BF16, 8 * T).rearrange("p (c t) -> p c t", c=8)
        PT = [self.av(16384 + i * 1024, BF16, 512) for i in range(3)]
        qn_h = self.av(19456, BF16, T)
        qpe_h = self.av(21504, BF16, T)
        wkvs = [self.av(o_, BF16, 512).rearrange("p (k f) -> p k f", k=2) for o_ in (23552, 75776)]
        wqs = [self.av(o_, BF16, 1024).rearrange("p (k f) -> p k f", k=4) for o_ in (24576, 76800)]

        def load_head_w(hd):
            si = hd % 2
            pairs = [(wkvs[si], self.wsrc("ev_w_kvb", 0, 2, hd * 256, 256)),
                     (wqs[si][:, :, 0:192], self.wsrc("ev_w_qb", 0, 4, hd * 192, 192))]
            if isS:
                pairs.append((wqs[si][:, :, 192:256], self.wsrc("ev_w_qb_sw", 0, 4, hd * 64, 64)))
            P.dma_multi("pool", pairs, f"whd{si}", writes=[self.A("whd", si)])
        nkc = NK // 128
        if isS:
            crs = [self.av(71168, F32, T), self.av(26624, F32, T)]
            tcr = self.A("crs")
            P.dma_multi("sp", [(crs[j][0:64, :], self.dram["ropeq"][:, j, :]) for j in range(2)], "ropeq", writes=[tcr])
            for j in range(2):
                for half in range(2):
                    sl = crs[j][0:64, half * 512:(half + 1) * 512]
                    P.op("dve", lambda h, sl=sl, half=half: h.tensor_tensor(sl, sl, rstd_q[0:64, half * 512:(half + 1) * 512], ALU.mult),
                         reads=[tcr, self.A("rstdq", half)], writes=[tcr])
        unit_i = 0
        kts = [self.A("kall", i) for i in range((NK + 511) // 512)]
        load_head_w(0)
        for hd in range(8):
            wkv, wq = wkvs[hd % 2], wqs[hd % 2]
            twk = twq = self.A("whd", hd % 2)
            if hd + 1 < 8:
                load_head_w(hd + 1)
            s = self.wn
            self.wn = (self.wn + 1) % 3
            slot_t = self.Tl("w", s)
            n512 = (NK + 511) // 512

            def sub(nm, slot_t=slot_t):
                t = Tile(nm)
                t.w = list(slot_t.w)
                t.r = list(slot_t.r)
                return t
            tK = [sub(("K", i)) for i in range(n512)]
            tV = [sub(("V", i)) for i in range(n512)]
            KT = self.wslot[s][:, 0:NK]
            V = self.wslot[s][:, NK:2 * NK].rearrange("p (n d) -> p n d", d=128)
            for i, k0 in enumerate(range(0, NK, 512)):
                n = min(512, NK - k0)
                b, bt = self.bank()
                self.mm(self.psum[:, b, 0:n], [(wkv[:, k, 0:128], ckv_all[:, k, k0:k0 + n]) for k in range(2)], [twk] + kts, bt)
                if i % 2 == 0:
                    P.op("act", lambda h, b=b, k0=k0, n=n: h.activation(KT[:, k0:k0 + n], self.psum[:, b, 0:n], AF.Copy), reads=[bt], writes=[tK[i]])
                else:
                    P.op("dve", lambda h, b=b, k0=k0, n=n: h.tensor_copy(KT[:, k0:k0 + n], self.psum[:, b, 0:n]), reads=[bt], writes=[tK[i]])
                b, bt = self.bank()
                nch = n // 128

                def vf(h, b=b, k0=k0, nch=nch):
                    for j in range(nch):
                        for k in range(2):
                            ins = h.matmul(self.psum[:, b, j * 128:(j + 1) * 128], ckv_all[:, k, k0 + j * 128:k0 + (j + 1) * 128], wkv[:, k, 128:256],
                                           start=(k == 0), stop=(k == 1))
                    return ins
                P.op("pe", vf, reads=[twk] + kts, writes=[bt])
                vdst = V[:, k0 // 128:k0 // 128 + nch, :]
                vsrc = self.psum[:, b, 0:n].rearrange("p (n d) -> p n d", d=128)
                if i % 2 == 1:
                    P.op("act", lambda h, vdst=vdst, vsrc=vsrc: h.activation(vdst, vsrc, AF.Copy), reads=[bt], writes=[tV[i]])
                else:
                    P.op("dve", lambda h, vdst=vdst, vsrc=vsrc: h.tensor_copy(vdst, vsrc), reads=[bt], writes=[tV[i]])
            for half in range(2):
                qs = [qg[:, k, half * 512:(half + 1) * 512] for k in range(4)]
                qts = [self.A("qg", k, half) for k in range(4)]
                b, bt = self.bank()
                self.mm(self.psum[:, b, :], [(wq[:, k, 0:128], qs[k]) for k in range(4)], [twq] + qts, bt)
                P.op("dve", lambda h, b=b, half=half: h.tensor_tensor(qn_h[:, half * 512:(half + 1) * 512], self.psum[:, b, :],
                                                                     rstd_q[:, half * 512:(half + 1) * 512], ALU.mult),
                     reads=[bt, self.A("rstdq", half)], writes=[self.A("qn", half)])
                b, bt = self.bank()
                self.mm(self.psum[0:64, b, :], [(wq[:, k, 128:192], qs[k]) for k in range(4)], [twq] + qts, bt)
                if not isS:
                    P.op("dve", lambda h, b=b, half=half: h.tensor_tensor(qpe_h[0:64, half * 512:(half + 1) * 512], self.psum[0:64, b, :],
                                                                         rstd_q[0:64, half * 512:(half + 1) * 512], ALU.mult),
                         reads=[bt, self.A("rstdq", half)], writes=[self.A("qpe", half)])
                else:
                    b2, bt2 = self.bank()
                    self.mm(self.psum[0:64, b2, :], [(wq[:, k, 192:256], qs[k]) for k in range(4)], [twq] + qts, bt2)
                    rb, rt = self.ringbuf()
                    rb2, rt2 = self.ringbuf()
                    P.op("dve", lambda h, rb=rb, b=b, half=half: h.tensor_tensor(rb[0:64, :], self.psum[0:64, b, :], crs[0][0:64, half * 512:(half + 1) * 512], ALU.mult),
                         reads=[bt, tcr], writes=[rt])
                    P.op("dve", lambda h, rb2=rb2, b2=b2, half=half: h.tensor_tensor(rb2[0:64, :], self.psum[0:64, b2, :], crs[1][0:64, half * 512:(half + 1) * 512], ALU.mult),
                         reads=[bt2, tcr], writes=[rt2])
                    P.op("dve", lambda h, rb=rb, rb2=rb2, half=half: h.tensor_tensor(qpe_h[0:64, half * 512:(half + 1) * 512], rb[0:64, :], rb2[0:64, :], ALU.add),
                         reads=[rt, rt2], writes=[self.A("qpe", half)])
            if isS:
                units = [(half * 512, 512, list(range(nkc)), half) for half in range(2)]
            else:
                units = [(s_ * 256, 256, [2 * s_, 2 * s_ + 1], s_ // 2) for s_ in range(4)]
            for (q0, nq, kcs, half) in units:
                OB, SBK = (6, 7) if unit_i % 2 == 0 else (4, 5)
                unit_i += 1
                qrd = [self.A("qn", half), self.A("qpe", half)]
                tO, tS = self.Tl("ps", OB), self.Tl("ps", SBK)
                pend = []

                def emit_qk(j):
                    kc = kcs[j]
                    b, bt = self.bank()
                    self.mm(self.psum[:, b, 0:nq], [(KT[:, kc * 128:(kc + 1) * 128], qn_h[:, q0:q0 + nq]),
                                                    (kpe_all[0:64, kc * 128:(kc + 1) * 128], qpe_h[0:64, q0:q0 + nq])],
                            qrd + [kts[kc // 4], tK[kc // 4]], bt)
                    pend.append((b, bt))
                nj = len(kcs)
                for j in range(min(2, nj)):
                    emit_qk(j)
                for j in range(nj):
                    b, bt = pend[j]
                    pi = j % 3
                    tp = self.A("PT", pi)
                    P.op("act", lambda h, b=b, pi=pi: h.activation(PT[pi][:, 0:nq], self.psum[:, b, 0:nq], AF.Exp, scale=SCALE), reads=[bt], writes=[tp])
                    if j + 2 < nj:
                        emit_qk(j + 2)
                    kc = kcs[j]
                    self.mm1(self.psum[:, OB, 0:nq], V[:, kc, :], PT[pi][:, 0:nq], j == 0, j == nj - 1, [tp, tV[kc // 4]], tO)
                    self.mm1(self.psum[:, SBK, 0:nq], self.ones[:], PT[pi][:, 0:nq], j == 0, j == nj - 1, [tp], tS)
                rb, rt = self.ringbuf()
                P.op("dve", lambda h, rb=rb: h.reciprocal(rb[:, 0:nq], self.psum[:, SBK, 0:nq]), reads=[tS], writes=[rt])
                P.op("dve", lambda h, rb=rb, hd=hd, q0=q0: h.tensor_tensor(attT[:, hd, q0:q0 + nq], self.psum[:, OB, 0:nq], rb[:, 0:nq], ALU.mult),
                     reads=[tO, rt], writes=[self.A("att", hd, half)])
            toks = {}
            for t_ in tK + tV:
                for (k_, v_) in t_.w + t_.r:
                    toks[k_] = max(toks.get(k_, 0), v_)
            slot_t.w = []
            slot_t.r = list(toks.items())
            self.tick()


_CACHE = {}


def _fm(v):
    v = np.asarray(v, np.float32)
    return np.ascontiguousarray(v.reshape(-1, 128).T)


def _rope_tables(pos_tok):
    pos_tok = np.asarray(pos_tok)
    rows = (pos_tok // 64).astype(np.float32)
    cols = (pos_tok % 64).astype(np.float32)
    inv = np.power(np.float32(10000.0), -np.arange(16, dtype=np.float32) * np.float32(2.0) / np.float32(32)).astype(np.float32)
    out = np.zeros((64, 2, len(pos_tok)), np.float32)
    for d in range(64):
        i, j, f = d // 32, (d % 32) // 16, d % 16
        ang = ((rows if i == 0 else cols) * inv[f]).astype(np.float32)
        out[d, 0] = np.cos(ang)
        out[d, 1] = np.sin(ang) * (-1.0 if j == 0 else 1.0)
    return out


_PARTNER = np.array([(d // 32) * 32 + (1 - (d % 32) // 16) * 16 + d % 16 for d in range(64)])


def kernel(x_prompt, x_sample, cache_ckv, cache_kpe, c, c_ctx, mod_w, mod_b, norm_mix_g, norm_ffn_g,
           ffn_w_gate, ffn_w_up, ffn_w_down, ev_w_in, ev_conv_w, ev_conv_b, ev_conv_ln_g, ev_conv_ln_b,
           ev_q_norm_g, ev_w_qb, ev_kv_norm_g, ev_w_kvb, ev_w_o, od_w_in, od_ln_g, od_ln_b, od_w_s, od_b_s,
           od_w_o, final_norm_g):
    f = lambda a: np.ascontiguousarray(np.asarray(a, dtype=np.float32))
    x_prompt, x_sample = f(x_prompt), f(x_sample)
    if "nc" not in _CACHE:
        _CACHE["nc"] = Builder().build()
    nc = _CACHE["nc"]
    par = np.zeros((128, NPAR), np.float32)

    def put(name, arr):
        arr = np.asarray(arr, np.float32)
        par[:, PCOL[name]:PCOL[name] + arr.shape[1]] = arr
    for l in range(2):
        put(f"nmg{l}", _fm(norm_mix_g[l])); put(f"nfg{l}", _fm(norm_ffn_g[l])); put(f"modb{l}", _fm(mod_b[l]))
    put("fng", _fm(final_norm_g)); put("convb", _fm(ev_conv_b[0])); put("clng", _fm(ev_conv_ln_g[0])); put("clnb", _fm(ev_conv_ln_b[0]))
    put("qng", _fm(ev_q_norm_g[0])); put("kvng", _fm(ev_kv_norm_g[0])); put("olng", _fm(od_ln_g[0])); put("olnb", _fm(od_ln_b[0]))
    cw = np.asarray(ev_conv_w, np.float32)[0, :, 0, :]
    put("convw", cw.T.reshape(8, 128, 31).transpose(1, 0, 2).reshape(128, 248))
    w_in = f(ev_w_in[0]); w_qb = f(ev_w_qb[0])
    w_in_sw = np.ascontiguousarray(w_in[:, 2816 + _PARTNER])
    w_qb_sw = np.ascontiguousarray(np.concatenate([w_qb[:, h * 192 + 128 + _PARTNER] for h in range(8)], axis=1))
    shared = {
        "params": par, "ident": np.eye(128, dtype=np.float32),
        "b_s": f(od_b_s[0]).reshape(1, 1024), "w_sT": np.ascontiguousarray(f(od_w_s[0]).transpose(2, 0, 1)),
        "ev_w_in": w_in, "ev_w_in_sw": w_in_sw, "ev_w_qb": w_qb, "ev_w_qb_sw": w_qb_sw,
        "ev_w_kvb": f(ev_w_kvb[0]), "ev_w_o": f(ev_w_o[0]), "od_w_in": f(od_w_in[0]), "od_w_o": f(od_w_o[0]),
        "ropek": _rope_tables(np.arange(4096)),
    }
    for l in range(2):
        shared[f"mod_w{l}"] = f(mod_w[l]); shared[f"wg{l}"] = f(ffn_w_gate[l]); shared[f"wu{l}"] = f(ffn_w_up[l]); shared[f"wd{l}"] = f(ffn_w_down[l])
    in_maps = []
    for i in range(8):
        b, q = i // 4, i % 4
        halo = np.zeros((32, D), np.float32)
        hm = np.zeros((128, 32), np.float32)
        if q > 0:
            halo[0:15] = x_sample[b, q * 1024 - 15:q * 1024]; hm[:, 0:15] = 1.0
        if q < 3:
            halo[15:30] = x_sample[b, (q + 1) * 1024:(q + 1) * 1024 + 15]; hm[:, 15:30] = 1.0
        condT = np.stack([_fm(c_ctx), _fm(c[b])], axis=-1)
        m = dict(shared)
        m.update({
            "xp": x_prompt[4 * i:4 * i + 4].reshape(T, D), "xs_own": np.ascontiguousarray(x_sample[b, q * 1024:(q + 1) * 1024]),
            "xs_all": x_sample[b], "xs_halo": halo, "hmask": hm, "cckv": f(cache_ckv[b, 0]), "ckpe": f(cache_kpe[b, 0]),
            "condT": np.ascontiguousarray(condT), "ropeq": _rope_tables(np.arange(q * 1024, (q + 1) * 1024)),
        })
        in_maps.append({"i_" + k_: v_ for k_, v_ in m.items()})
    res = run_bass_kernel_spmd(nc, in_maps, core_ids=list(range(8)))
    R = res.results
    y_prompt = np.stack([R[i]["o_yp"].reshape(4, SEQ_P, D) for i in range(8)], 0).reshape(32, SEQ_P, D)
    y_sample = np.stack([R[i]["o_ys"] for i in range(8)], 0).reshape(2, 4096, D)
    s_ckv = np.stack([R[i]["o_o_ckv"].reshape(4, 1, SEQ_P, 256) for i in range(8)], 0).reshape(32, 1, SEQ_P, 256)
    s_kpe = np.stack([R[i]["o_o_kpe"].reshape(4, 1, SEQ_P, 64) for i in range(8)], 0).reshape(32, 1, SEQ_P, 64)
    return (np.ascontiguousarray(y_prompt, np.float32), np.ascontiguousarray(y_sample, np.float32),
            np.ascontiguousarray(s_ckv, np.float32), np.ascontiguousarray(s_kpe, np.float32))
```
